# Optimizing a Trainium2 kernel written in Bass

```python
import math
import jax, jax.numpy as jnp
from jax import lax
import numpy as np


D_MODEL = 1024
BATCH = 2
SEQ = 16384
DEPTH = 4

HEAD_DIM = 64
N_RWKV_HEADS = 8
D_RWKV = N_RWKV_HEADS * HEAD_DIM
N_DIFF_HEADS = 4
D_DIFF = N_DIFF_HEADS * 2 * HEAD_DIM
DECAY_LORA = 64
AAA_LORA = 64
GATE_LORA = 128
RWKV_COLS = 3 * D_RWKV + DECAY_LORA + AAA_LORA + GATE_LORA
AB_COLS = RWKV_COLS + 3 * D_DIFF
RWKV_GN_EPS = 64e-5
ROPE_THETA = 500000.0
ROPE_DIM = HEAD_DIM // 4
Q_BLOCK = 128
CHUNK = 128
GMLP_GROUPS = 8
D_GMLP = D_MODEL
D_FF = 2816
CONV_W = 3
N_AB = (DEPTH + 1) // 2
N_C = DEPTH // 2
DEEPNORM_ALPHA = (2 * DEPTH) ** 0.25
DEEPNORM_BETA = (8 * DEPTH) ** -0.25

kernel_name = 'hybrid_rwkv7_diffattn_gmlp_convffn'


def layer_norm(x, g, b, eps=1e-5):
    xf = x.astype(jnp.float32)
    mu = jnp.mean(xf, axis=-1, keepdims=True)
    var = jnp.mean(jnp.square(xf - mu), axis=-1, keepdims=True)
    y = (xf - mu) * lax.rsqrt(var + eps)
    return (y * g.astype(jnp.float32) + b.astype(jnp.float32)).astype(x.dtype)


def rms_norm(x, g, eps=1e-5):
    xf = x.astype(jnp.float32)
    y = xf * lax.rsqrt(jnp.mean(jnp.square(xf), axis=-1, keepdims=True) + eps)
    return (y * g.astype(jnp.float32)).astype(x.dtype)


def shift_prev(t):
    return jnp.pad(t[:, :-1], ((0, 0), (1, 0), (0, 0)))


def rope_tables(seq):
    inv_freq = ROPE_THETA ** (-jnp.arange(0, ROPE_DIM, 2, dtype=jnp.float32) / ROPE_DIM)
    ang = jnp.arange(seq, dtype=jnp.float32)[:, None] * inv_freq[None, :]
    return jnp.cos(ang), jnp.sin(ang)


def apply_partial_rope(t, cos, sin):
    half = ROPE_DIM // 2
    c = cos[None, :, None, None, :].astype(t.dtype)
    s = sin[None, :, None, None, :].astype(t.dtype)
    t1, t2, rest = t[..., :half], t[..., half:ROPE_DIM], t[..., ROPE_DIM:]
    return jnp.concatenate([t1 * c - t2 * s, t1 * s + t2 * c, rest], axis=-1)


def wkv7_scan(r, decay, k, v, a_vec, b_vec):
    B, S, H, N = r.shape
    xs = tuple(jnp.moveaxis(t.astype(jnp.float32), 1, 0) for t in (r, decay, k, v, a_vec, b_vec))

    def step(state, inp):
        r_t, w_t, k_t, v_t, a_t, b_t = inp
        sa = jnp.einsum('bhvk,bhk->bhv', state, a_t)
        state = (state * w_t[:, :, None, :] + sa[..., None] * b_t[:, :, None, :]
                 + v_t[..., None] * k_t[:, :, None, :])
        y_t = jnp.einsum('bhvk,bhk->bhv', state, r_t)
        return state, y_t

    s0 = jnp.zeros((B, H, N, N), jnp.float32)
    _, y = lax.scan(step, s0, xs)
    return jnp.moveaxis(y, 0, 1)


def rwkv7_mix(pr, mu, w0, w2, a0, a2, g2, k_k, k_a, r_k, lnx_g, lnx_b):
    B, S, _ = pr.shape

    def heads(t):
        return t.reshape(B, S, N_RWKV_HEADS, HEAD_DIM)

    xs = pr + mu * (shift_prev(pr) - pr)
    r, k, v, xw, xa, xg = jnp.split(
        xs, [D_RWKV, 2 * D_RWKV, 3 * D_RWKV, 3 * D_RWKV + DECAY_LORA,
             3 * D_RWKV + DECAY_LORA + AAA_LORA], axis=-1)
    w = -jax.nn.softplus(-(w0 + jnp.tanh(xw) @ w2)) - 0.5
    decay = jnp.exp(-jnp.exp(w.astype(jnp.float32)))
    a = jax.nn.sigmoid(a0 + xa @ a2)
    g = jax.nn.sigmoid(xg) @ g2
    kk = heads(k * k_k).astype(jnp.float32)
    kk = kk / jnp.maximum(jnp.sqrt(jnp.sum(jnp.square(kk), axis=-1, keepdims=True)), 1e-12)
    k = k * (1.0 + (a - 1.0) * k_a)
    r_h, k_h, v_h, a_h = heads(r), heads(k), heads(v), heads(a)
    y = wkv7_scan(r_h, heads(decay), k_h, v_h, -kk, kk * a_h.astype(jnp.float32))
    mu_y = jnp.mean(y, axis=-1, keepdims=True)
    var_y = jnp.mean(jnp.square(y - mu_y), axis=-1, keepdims=True)
    yn = ((y - mu_y) * lax.rsqrt(var_y + RWKV_GN_EPS)).reshape(B, S, D_RWKV)
    yn = (yn * lnx_g.astype(jnp.float32) + lnx_b.astype(jnp.float32)).astype(pr.dtype)
    bonus = jnp.sum(r_h * k_h * r_k, axis=-1, keepdims=True) * v_h
    return (yn + bonus.reshape(B, S, D_RWKV)) * g


def diff_attention(pd, lam_q1, lam_k1, lam_q2, lam_k2, subln_g, lam_init, cos, sin):
    B, S, _ = pd.shape
    q, k, v = jnp.split(pd, 3, axis=-1)
    q = q.reshape(B, S, N_DIFF_HEADS, 2, HEAD_DIM)
    k = k.reshape(B, S, N_DIFF_HEADS, 2, HEAD_DIM)
    v = v.reshape(B, S, N_DIFF_HEADS, 2 * HEAD_DIM)
    q = apply_partial_rope(q, cos, sin) * (HEAD_DIM ** -0.5)
    k = apply_partial_rope(k, cos, sin)
    lam = (jnp.exp(jnp.sum(lam_q1.astype(jnp.float32) * lam_k1.astype(jnp.float32)))
           - jnp.exp(jnp.sum(lam_q2.astype(jnp.float32) * lam_k2.astype(jnp.float32)))
           + lam_init)
    n_blocks = S // Q_BLOCK
    qb = jnp.moveaxis(q.reshape(B, n_blocks, Q_BLOCK, N_DIFF_HEADS, 2, HEAD_DIM), 1, 0)
    k_pos = jnp.arange(S)

    def block(args):
        q_blk, i = args
        q_pos = i * Q_BLOCK + jnp.arange(Q_BLOCK)
        s = jnp.einsum('bqhcd,bkhcd->bhcqk', q_blk, k).astype(jnp.float32)
        mask = k_pos[None, :] <= q_pos[:, None]
        s = jnp.where(mask[None, None, None], s, -jnp.inf)
        p = jax.nn.softmax(s, axis=-1)
        attn = p[:, :, 0] - lam * p[:, :, 1]
        return jnp.einsum('bhqk,bkhd->bqhd', attn.astype(v.dtype), v)

    o = lax.map(block, (qb, jnp.arange(n_blocks)))
    o = jnp.moveaxis(o, 0, 1).reshape(B, S, N_DIFF_HEADS, 2 * HEAD_DIM)
    o = rms_norm(o, subln_g) * (1.0 - lam_init)
    return o.reshape(B, S, D_DIFF)


def chunked_spatial_gating(x, w_in, b_in, ln_g, ln_b, w_s, b_s):
    B, S, _ = x.shape
    h = jax.nn.gelu(x @ w_in + b_in, approximate=False)
    u, v = jnp.split(h, 2, axis=-1)
    v = layer_norm(v, ln_g, ln_b)
    vc = v.reshape(B, S // CHUNK, CHUNK, GMLP_GROUPS, D_GMLP // GMLP_GROUPS)
    ws = w_s * jnp.tril(jnp.ones((CHUNK, CHUNK), w_s.dtype))
    mixed = jnp.einsum('gts,bcsgd->bctgd', ws, vc) + b_s.T[None, None, :, :, None]
    return u * mixed.reshape(B, S, D_GMLP)


def conv_ffn(x, w_up, conv_w, conv_b, w_down):
    S = x.shape[1]
    gate, val = jnp.split(x @ w_up, 2, axis=-1)
    gp = jnp.pad(gate, ((0, 0), (CONV_W - 1, 0), (0, 0)))
    conv = conv_b
    for j in range(CONV_W):
        conv = conv + conv_w[j] * gp[:, j:j + S]
    return (jax.nn.silu(conv) * val) @ w_down


def setup_inputs(seed: int = 0) -> dict:
    key = jax.random.key(seed)
    keys = jax.random.split(key, 48)
    counter = [0]
    f32 = jnp.float32

    def nxt():
        kk = keys[counter[0]]
        counter[0] += 1
        return kk

    def nrm(shape, scale):
        return jax.random.normal(nxt(), shape, f32) * scale

    def unif(shape, lo, hi):
        return jax.random.uniform(nxt(), shape, f32, lo, hi)

    D = D_MODEL
    return {
        'x': nrm((BATCH, SEQ, D), 1.0),
        'ab_w_in': nrm((N_AB, D, AB_COLS), D ** -0.5),
        'ab_shift_mu': unif((N_AB, RWKV_COLS), 0.0, 1.0),
        'ab_w0': unif((N_AB, D_RWKV), -6.0, -1.0),
        'ab_w2': nrm((N_AB, DECAY_LORA, D_RWKV), 0.1),
        'ab_a0': nrm((N_AB, D_RWKV), 0.5),
        'ab_a2': nrm((N_AB, AAA_LORA, D_RWKV), AAA_LORA ** -0.5),
        'ab_g2': nrm((N_AB, GATE_LORA, D_RWKV), GATE_LORA ** -0.5),
        'ab_k_k': 0.85 + nrm((N_AB, D_RWKV), 0.05),
        'ab_k_a': 1.0 + nrm((N_AB, D_RWKV), 0.05),
        'ab_r_k': nrm((N_AB, N_RWKV_HEADS, HEAD_DIM), 0.1),
        'ab_lnx_g': 1.0 + nrm((N_AB, D_RWKV), 0.05),
        'ab_lnx_b': nrm((N_AB, D_RWKV), 0.01),
        'ab_lam_q1': nrm((N_AB, HEAD_DIM), 0.1),
        'ab_lam_k1': nrm((N_AB, HEAD_DIM), 0.1),
        'ab_lam_q2': nrm((N_AB, HEAD_DIM), 0.1),
        'ab_lam_k2': nrm((N_AB, HEAD_DIM), 0.1),
        'ab_subln_g': 1.0 + nrm((N_AB, 2 * HEAD_DIM), 0.05),
        'ab_w_out': nrm((N_AB, D_RWKV + D_DIFF, D), (D_RWKV + D_DIFF) ** -0.5 * DEEPNORM_BETA),
        'c_w_in': nrm((N_C, D, 2 * D_GMLP), D ** -0.5),
        'c_b_in': nrm((N_C, 2 * D_GMLP), 0.01),
        'c_ln_g': 1.0 + nrm((N_C, D_GMLP), 0.05),
        'c_ln_b': nrm((N_C, D_GMLP), 0.01),
        'c_w_s': nrm((N_C, GMLP_GROUPS, CHUNK, CHUNK), CHUNK ** -0.5),
        'c_b_s': 1.0 + nrm((N_C, GMLP_GROUPS, CHUNK), 0.05),
        'c_w_out': nrm((N_C, D_GMLP, D), D_GMLP ** -0.5 * DEEPNORM_BETA),
        'ln1_g': 1.0 + nrm((DEPTH, D), 0.05),
        'ln1_b': nrm((DEPTH, D), 0.01),
        'ffn_w_up': nrm((DEPTH, D, 2 * D_FF), D ** -0.5),
        'ffn_conv_w': nrm((DEPTH, CONV_W, D_FF), CONV_W ** -0.5),
        'ffn_conv_b': nrm((DEPTH, D_FF), 0.01),
        'ffn_w_down': nrm((DEPTH, D_FF, D), D_FF ** -0.5 * DEEPNORM_BETA),
        'ln2_g': 1.0 + nrm((DEPTH, D), 0.05),
        'ln2_b': nrm((DEPTH, D), 0.01),
    }


def reference(x, ab_w_in, ab_shift_mu, ab_w0, ab_w2, ab_a0, ab_a2, ab_g2, ab_k_k, ab_k_a,
              ab_r_k, ab_lnx_g, ab_lnx_b, ab_lam_q1, ab_lam_k1, ab_lam_q2, ab_lam_k2,
              ab_subln_g, ab_w_out, c_w_in, c_b_in, c_ln_g, c_ln_b, c_w_s, c_b_s, c_w_out,
              ln1_g, ln1_b, ffn_w_up, ffn_conv_w, ffn_conv_b, ffn_w_down, ln2_g, ln2_b):
    S = x.shape[1]
    cos, sin = rope_tables(S)
    for i in range(DEPTH):
        j = i // 2
        if i % 2 == 0:
            p = x @ ab_w_in[j]
            y_r = rwkv7_mix(p[..., :RWKV_COLS], ab_shift_mu[j], ab_w0[j], ab_w2[j], ab_a0[j],
                            ab_a2[j], ab_g2[j], ab_k_k[j], ab_k_a[j], ab_r_k[j],
                            ab_lnx_g[j], ab_lnx_b[j])
            lam_init = 0.8 - 0.6 * math.exp(-0.3 * i)
            y_d = diff_attention(p[..., RWKV_COLS:], ab_lam_q1[j], ab_lam_k1[j], ab_lam_q2[j],
                                 ab_lam_k2[j], ab_subln_g[j], lam_init, cos, sin)
            mix = jnp.concatenate([y_r, y_d], axis=-1) @ ab_w_out[j]
        else:
            mix = chunked_spatial_gating(x, c_w_in[j], c_b_in[j], c_ln_g[j], c_ln_b[j],
                                         c_w_s[j], c_b_s[j]) @ c_w_out[j]
        x = layer_norm(DEEPNORM_ALPHA * x + mix, ln1_g[i], ln1_b[i])
        ffn = conv_ffn(x, ffn_w_up[i], ffn_conv_w[i], ffn_conv_b[i], ffn_w_down[i])
        x = layer_norm(DEEPNORM_ALPHA * x + ffn, ln2_g[i], ln2_b[i])
    return x
```

```python
import math
import numpy as np
from contextlib import ExitStack
import concourse.bass as bass
import concourse.mybir as mybir
from concourse.bass_utils import run_bass_kernel_spmd

F32 = mybir.dt.float32
BF16 = mybir.dt.bfloat16
AF = mybir.ActivationFunctionType
ALU = mybir.AluOpType
AX = mybir.AxisListType

EPOCH = 12000
NDSLOT = 24


class Tk:
    def __init__(self, h, name):
        self.h = h
        self.name = name
        self.lw = None
        self.rd = {}

    def __getitem__(self, idx):
        return V(self, self.h[idx])

    def ap(self):
        return V(self, self.h[:])


class V:
    def __init__(self, tk, ap):
        self.tk = tk
        self.ap = ap

    def __getitem__(self, idx):
        return V(self.tk, self.ap[idx])

    def rearrange(self, *a, **k):
        return V(self.tk, self.ap.rearrange(*a, **k))

    def bitcast(self, dt):
        return V(self.tk, self.ap.bitcast(dt))

    def to_broadcast(self, shape):
        return V(self.tk, self.ap.to_broadcast(shape))


def _ap(x):
    return x.ap if isinstance(x, V) else x


class Prog:
    def __init__(self, name="k"):
        self.nc = bass.Bass("TRN2", target_bir_lowering=False)
        self.es = ExitStack()
        nc = self.nc
        self.engs = {"pe": nc.tensor, "act": nc.scalar, "dve": nc.vector,
                     "pool": nc.gpsimd, "sp": nc.sync}
        self.cnt = {k: 0 for k in self.engs}
        self.sems = {k: [] for k in self.engs}
        self.waited = {k: {} for k in self.engs}
        self.dsem = [self.es.enter_context(nc.semaphore(f"d{i}")) for i in range(NDSLOT)]
        self.dval = [0] * NDSLOT
        self.dslot = 0
        self.nt = 0
        self.out_tokens = []

    def sb(self, shape, dt=F32, name=None):
        self.nt += 1
        name = name or f"t{self.nt}"
        h = self.es.enter_context(self.nc.sbuf_tensor(name, list(shape), dt))
        return Tk(h, name)

    def ps(self, shape, dt=F32, name=None):
        self.nt += 1
        name = name or f"p{self.nt}"
        h = self.es.enter_context(self.nc.psum_tensor(name, list(shape), dt))
        return Tk(h, name)

    def dram(self, name, shape, dt=F32, kind="Internal"):
        h = self.nc.dram_tensor(name, list(shape), dt, kind=kind)
        return Tk(h, name)

    def _sem(self, key, val):
        if isinstance(key, tuple):
            return self.dsem[key[1]], val
        ep = (val - 1) // EPOCH
        lst = self.sems[key]
        while len(lst) <= ep:
            lst.append(self.es.enter_context(self.nc.semaphore(f"s_{key}_{len(lst)}")))
        return lst[ep], (val - 1) % EPOCH + 1

    def _deps(self, e, reads, writes, pe_acc=False):
        deps = {}

        def add(tok):
            if tok is None:
                return
            k, v = tok
            if deps.get(k, 0) < v:
                deps[k] = v

        for x in reads:
            if isinstance(x, V):
                add(x.tk.lw)
        for x in writes:
            if isinstance(x, V):
                lw = x.tk.lw
                if not (pe_acc and lw is not None and lw[0] == "pe"):
                    add(lw)
                for k, v in x.tk.rd.items():
                    add((k, v))
        w = self.waited[e]
        eng = self.engs[e]
        for k, v in deps.items():
            if w.get(k, 0) >= v:
                continue
            w[k] = v
            sem, sv = self._sem(k, v)
            eng.wait_ge(sem, sv)

    def _mark(self, tok, reads, writes):
        k, v = tok
        for x in reads:
            if isinstance(x, V):
                if x.tk.rd.get(k, 0) < v:
                    x.tk.rd[k] = v
        for x in writes:
            if isinstance(x, V):
                x.tk.lw = tok
                x.tk.rd = {}

    def emit(self, e, fn, reads, writes, pe_acc=False):
        self._deps(e, reads, writes, pe_acc)
        ins = fn(self.engs[e])
        self.cnt[e] += 1
        tok = (e, self.cnt[e])
        sem, sv = self._sem(e, self.cnt[e])
        ins.then_inc(sem, 1)
        self._mark(tok, reads, writes)
        return tok

    def dma(self, q, out, in_, **kw):
        reads, writes = [in_], [out]
        self._deps(q, reads, writes)
        slot = self.dslot
        self.dslot = (slot + 1) % NDSLOT
        key = ("d", slot)
        prev = self.dval[slot]
        w = self.waited[q]
        if prev > 0 and w.get(key, 0) < prev:
            w[key] = prev
            self.engs[q].wait_ge(self.dsem[slot], prev)
        ins = self.engs[q].dma_start(out=_ap(out), in_=_ap(in_), **kw)
        ins.then_inc(self.dsem[slot], 16)
        self.dval[slot] += 16
        tok = (key, self.dval[slot])
        self._mark(tok, reads, writes)
        return tok

    def wait_tok(self, e, tok):
        k, v = tok
        w = self.waited[e]
        if w.get(k, 0) >= v:
            return
        w[k] = v
        sem, sv = self._sem(k, v)
        self.engs[e].wait_ge(sem, sv)

    def mm(self, out, lhsT, rhs, start=True, stop=True):
        return self.emit("pe", lambda E: E.matmul(_ap(out), _ap(lhsT), _ap(rhs), start=start, stop=stop),
                         [lhsT, rhs], [out], pe_acc=True)

    def transpose(self, out, in_, ident):
        return self.emit("pe", lambda E: E.transpose(_ap(out), _ap(in_), _ap(ident)),
                         [in_, ident], [out], pe_acc=True)

    def act(self, out, in_, func, bias=None, scale=1.0, accum_out=None, e="act"):
        reads = [in_]
        kw = {}
        if bias is not None:
            kw["bias"] = _ap(bias)
            reads.append(bias)
        if not isinstance(scale, (int, float)):
            reads.append(scale)
        kw["scale"] = _ap(scale)
        writes = [out]
        if accum_out is not None:
            kw["accum_out"] = _ap(accum_out)
            writes.append(accum_out)
        return self.emit(e, lambda E: E.activation(_ap(out), _ap(in_), func, **kw), reads, writes)

    def tt(self, out, in0, in1, op, e="dve"):
        return self.emit(e, lambda E: E.tensor_tensor(_ap(out), _ap(in0), _ap(in1), op), [in0, in1], [out])

    def ts(self, out, in0, s1, op0, s2=None, op1=None, e="dve", accum_out=None):
        reads = [in0, s1, s2]
        kw = {}
        writes = [out]
        if op1 is not None:
            kw["op1"] = op1
        if accum_out is not None:
            kw["accum_out"] = _ap(accum_out)
            writes.append(accum_out)
        return self.emit(e, lambda E: E.tensor_scalar(_ap(out), _ap(in0), _ap(s1), _ap(s2), op0, **kw),
                         reads, writes)

    def stt(self, out, in0, scalar, in1, op0, op1, e="dve", accum_out=None):
        kw = {}
        writes = [out]
        if accum_out is not None:
            kw["accum_out"] = _ap(accum_out)
            writes.append(accum_out)
        return self.emit(e, lambda E: E.scalar_tensor_tensor(_ap(out), _ap(in0), _ap(scalar), _ap(in1), op0, op1, **kw),
                         [in0, scalar, in1], writes)

    def copy(self, out, in_, e="dve"):
        if e == "act":
            return self.emit(e, lambda E: E.copy(_ap(out), _ap(in_)), [in_], [out])
        return self.emit(e, lambda E: E.tensor_copy(_ap(out), _ap(in_)), [in_], [out])

    def memset(self, out, val, e="dve"):
        return self.emit(e, lambda E: E.memset(_ap(out), val), [], [out])

    def reduce(self, out, in_, op, axis=AX.X, e="dve"):
        return self.emit(e, lambda E: E.tensor_reduce(_ap(out), _ap(in_), axis, op), [in_], [out])

    def const(self, val):
        if not hasattr(self, "_consts"):
            self._consts = {}
        if val not in self._consts:
            t = self.sb([128, 1], F32)
            self.memset(t[:], float(val), e="pool")
            self._consts[val] = t
        return self._consts[val]

    def recip(self, out, in_):
        return self.emit("dve", lambda E: E.reciprocal(_ap(out), _ap(in_)), [in_], [out])

    def bn_stats(self, out, in_):
        return self.emit("dve", lambda E: E.bn_stats(_ap(out), _ap(in_)), [in_], [out])

    def bn_aggr(self, out, in_):
        return self.emit("dve", lambda E: E.bn_aggr(_ap(out), _ap(in_)), [in_], [out])

    def finish(self, toks):
        for t in toks:
            self.wait_tok("pool", t)
        self.es.close()
        return self.nc


ALPHA = 8.0 ** 0.25
D = 1024
DFF = 2816
NFC = 22


def load_cast(P, dst_views, src_views, stg, i0=0):
    for i, (d, s) in enumerate(zip(dst_views, src_views)):
        st = stg[(i0 + i) % len(stg)]
        n = s.ap.shape[-1]
        P.dma("sp", st[:, 0:n], s)
        P.copy(d, st[:, 0:n], e=("dve" if (i0 + i) % 2 else "pool"))
    return i0 + len(dst_views)


def ln_inplace(P, z, TT, onesm, g, b, pl, sq, stat, eps=1e-5, out_bf=None):
    for c in range(8):
        P.mm(pl[0][:, 0:TT], onesm[:], z[:, c, :], start=(c == 0), stop=(c == 7))
        s = sq[c % 2]
        P.act(s[:, 0:TT], z[:, c, :], AF.Square)
        P.mm(pl[1][:, 0:TT], onesm[:], s[:, 0:TT], start=(c == 0), stop=(c == 7))
    mean, msq, rstd = stat
    P.copy(mean[:, 0:TT], pl[0][:, 0:TT], e="act")
    P.tt(msq[:, 0:TT], mean[:, 0:TT], mean[:, 0:TT], ALU.mult, e="pool")
    P.tt(rstd[:, 0:TT], pl[1][:, 0:TT], msq[:, 0:TT], ALU.subtract)
    P.act(rstd[:, 0:TT], rstd[:, 0:TT], AF.Sqrt, bias=P.const(eps)[:, 0:1])
    P.recip(rstd[:, 0:TT], rstd[:, 0:TT])
    for c in range(8):
        P.tt(z[:, c, :], z[:, c, :], mean[:, 0:TT], ALU.subtract)
        P.tt(z[:, c, :], z[:, c, :], rstd[:, 0:TT], ALU.mult, e="pool")
        P.act(z[:, c, :], z[:, c, :], AF.Identity, bias=b[:, c:c + 1], scale=g[:, c:c + 1])


def build_ffn(NSEG=2, SEGLEN=2048, TT=256):
    P = Prog()
    x_in = P.dram("x1", [128, 8, NSEG, SEGLEN + 2], F32, "ExternalInput")
    hmask_d = P.dram("hmask", [128, NSEG], F32, "ExternalInput")
    wup_d = P.dram("w_up", [128, 8, 2 * DFF], F32, "ExternalInput")
    wdn_d = P.dram("w_down", [128, NFC, D], F32, "ExternalInput")
    cw_d = P.dram("conv_w", [128, 3, NFC], F32, "ExternalInput")
    cb_d = P.dram("conv_b", [128, NFC], F32, "ExternalInput")
    g_d = P.dram("ln_g", [128, 8], F32, "ExternalInput")
    b_d = P.dram("ln_b", [128, 8], F32, "ExternalInput")
    out_d = P.dram("x2", [128, 8, NSEG * SEGLEN], F32, "ExternalOutput")

    onesm = P.sb([128, 128], F32)
    P.memset(onesm[:], 1.0 / 1024.0)
    hm = P.sb([128, NSEG]); cw = P.sb([128, 3, NFC]); cb = P.sb([128, NFC]); g = P.sb([128, 8]); b = P.sb([128, 8])
    for t_, d_ in ((hm, hmask_d), (cw, cw_d), (cb, cb_d), (g, g_d), (b, b_d)):
        P.dma("sp", t_[:], d_[:])

    wup = [P.sb([128, 2 * DFF], BF16) for _ in range(8)]
    wdn = [P.sb([128, D], BF16) for _ in range(NFC)]
    stg = [P.sb([128, 1408], F32) for _ in range(2)]
    dst, src = [], []
    for kc in range(8):
        for j in range(4):
            dst.append(wup[kc][:, j * 1408:(j + 1) * 1408]); src.append(wup_d[:, kc, j * 1408:(j + 1) * 1408])
    for c in range(NFC):
        dst.append(wdn[c][:, :]); src.append(wdn_d[:, c, :])
    load_cast(P, dst, src, stg)

    xf = [P.sb([128, 8, TT], F32) for _ in range(2)]
    xb = [P.sb([128, 8, TT], BF16) for _ in range(2)]
    H = P.sb([128, NFC, TT], BF16)
    G = [P.sb([128, TT + 2], F32) for _ in range(2)]
    T = [P.sb([128, TT], F32) for _ in range(2)]
    sq = [P.sb([128, TT], F32) for _ in range(2)]
    stat = [P.sb([128, TT], F32) for _ in range(3)]
    carry = P.sb([128, NFC, 2], F32)
    xh = P.sb([128, 8, 2], F32); xhb = P.sb([128, 8, 2], BF16)
    pg = [P.ps([128, 512]) for _ in range(2)]
    pv = [P.ps([128, 512]) for _ in range(2)]
    pd = [P.ps([128, 512]) for _ in range(2)]
    pl = [P.ps([128, 512]) for _ in range(2)]

    NTL = SEGLEN // TT
    tiles = [(s, t) for s in range(NSEG) for t in range(NTL)]
    outs = []

    def load_tile(i):
        s, t = tiles[i]
        P.dma("sp", xf[i % 2][:], x_in[:, :, s, 2 + t * TT: 2 + (t + 1) * TT])
        P.copy(xb[i % 2][:], xf[i % 2][:], e="pool")

    load_tile(0)
    for i, (s, t) in enumerate(tiles):
        X, XB = xf[i % 2], xb[i % 2]
        if t == 0:
            P.dma("sp", xh[:], x_in[:, :, s, 0:2])
            P.copy(xhb[:], xh[:], e="pool")
            for c in range(NFC):
                pp = pg[c % 2]
                for kc in range(8):
                    P.mm(pp[:, 0:2], wup[kc][:, c * 128:(c + 1) * 128], xhb[:, kc, :], start=(kc == 0), stop=(kc == 7))
                P.ts(carry[:, c, :], pp[:, 0:2], hm[:, s:s + 1], ALU.mult)
        if i + 1 < len(tiles):
            load_tile(i + 1)
        for c in range(NFC):
            pgc, pvc = pg[c % 2], pv[c % 2]
            for kc in range(8):
                P.mm(pgc[:, 0:TT], wup[kc][:, c * 128:(c + 1) * 128], XB[:, kc, :], start=(kc == 0), stop=(kc == 7))
            for kc in range(8):
                P.mm(pvc[:, 0:TT], wup[kc][:, DFF + c * 128: DFF + (c + 1) * 128], XB[:, kc, :], start=(kc == 0), stop=(kc == 7))
            Gc, Tc = G[c % 2], T[c % 2]
            P.copy(Gc[:, 2:TT + 2], pgc[:, 0:TT], e="act")
            P.copy(Gc[:, 0:2], carry[:, c, :], e="pool")
            P.ts(Tc[:], Gc[:, 0:TT], cw[:, 0, c:c + 1], ALU.mult, cb[:, c:c + 1], ALU.add)
            P.stt(Tc[:], Gc[:, 1:TT + 1], cw[:, 1, c:c + 1], Tc[:], ALU.mult, ALU.add)
            P.stt(Tc[:], Gc[:, 2:TT + 2], cw[:, 2, c:c + 1], Tc[:], ALU.mult, ALU.add)
            P.copy(carry[:, c, :], Gc[:, TT:TT + 2], e="pool")
            P.act(Tc[:], Tc[:], AF.Silu)
            P.tt(H[:, c, :], Tc[:], pvc[:, 0:TT], ALU.mult)
        for oc in range(8):
            pp = pd[oc % 2]
            for c in range(NFC):
                P.mm(pp[:, 0:TT], wdn[c][:, oc * 128:(oc + 1) * 128], H[:, c, :], start=(c == 0), stop=(c == NFC - 1))
            P.stt(X[:, oc, :], X[:, oc, :], ALPHA, pp[:, 0:TT], ALU.mult, ALU.add)
        ln_inplace(P, X, TT, onesm, g, b, pl, sq, stat)
        outs.append(P.dma("pool", out_d[:, :, s * SEGLEN + t * TT: s * SEGLEN + (t + 1) * TT], X[:]))
    return P.finish(outs)


def build_gmlp(NTOK=4096):
    TT = 128
    P = Prog()
    x_in = P.dram("x", [128, 8, NTOK], F32, "ExternalInput")
    win_d = P.dram("w_in", [128, 8, 2048], F32, "ExternalInput")
    wout_d = P.dram("w_out", [128, 8, 1024], F32, "ExternalInput")
    wsT_d = P.dram("wsT", [128, 8, 128], F32, "ExternalInput")
    bu_d = P.dram("b_u", [128, 8], F32, "ExternalInput")
    bv_d = P.dram("b_v", [1, 1024], F32, "ExternalInput")
    lg_d = P.dram("cln_g", [1, 1024], F32, "ExternalInput")
    lb_d = P.dram("cln_b", [1, 1024], F32, "ExternalInput")
    bs_d = P.dram("b_s", [1, 1024], F32, "ExternalInput")
    g_d = P.dram("ln_g", [128, 8], F32, "ExternalInput")
    b_d = P.dram("ln_b", [128, 8], F32, "ExternalInput")
    out_d = P.dram("x1", [128, 8, NTOK], F32, "ExternalOutput")

    onesm = P.sb([128, 128], F32)
    P.memset(onesm[:], 1.0 / 1024.0)
    bu = P.sb([128, 8]); g = P.sb([128, 8]); b = P.sb([128, 8])
    bv = P.sb([128, 1024]); lg = P.sb([128, 1024]); lb = P.sb([128, 1024]); bs = P.sb([128, 1024])
    for t_, d_ in ((bu, bu_d), (g, g_d), (b, b_d)):
        P.dma("sp", t_[:], d_[:])
    for t_, d_ in ((bv, bv_d), (lg, lg_d), (lb, lb_d), (bs, bs_d)):
        P.dma("sp", t_[:], d_[0:1, :].to_broadcast([128, 1024]))
    mask = P.sb([128, 128], F32)
    P.memset(mask[:], 1.0, e="pool")
    P.emit("pool", lambda E: E.affine_select(mask.h[:], mask.h[:], [[1, 128]], ALU.is_ge, 0.0, base=0, channel_multiplier=-1),
           [mask[:]], [mask[:]])
    wsf = P.sb([128, 8, 128], F32)
    P.dma("sp", wsf[:], wsT_d[:])
    wsb = P.sb([128, 8, 128], BF16)
    for gi in range(8):
        P.tt(wsb[:, gi, :], wsf[:, gi, :], mask[:], ALU.mult)

    win = [P.sb([128, 2048], BF16) for _ in range(8)]
    wout = [P.sb([128, 1024], BF16) for _ in range(8)]
    stg = [P.sb([128, 2048], F32) for _ in range(2)]
    load_cast(P, [w[:, :] for w in win] + [w[:, :] for w in wout],
              [win_d[:, kc, :] for kc in range(8)] + [wout_d[:, kc, :] for kc in range(8)], stg)

    xf = [P.sb([128, 8, TT], F32) for _ in range(2)]
    xb = [P.sb([128, 8, TT], BF16) for _ in range(2)]
    U = P.sb([128, 8, TT], F32)
    V1 = P.sb([128, 1024], F32)
    junk = P.sb([128, 1024], F32)
    VN = P.sb([128, 1024], BF16)
    GU = P.sb([128, 8, TT], BF16)
    mx = [P.sb([128, TT], F32) for _ in range(2)]
    st = P.sb([128, 4], F32)
    sq = [P.sb([128, TT], F32) for _ in range(2)]
    stat = [P.sb([128, TT], F32) for _ in range(3)]
    pu = [P.ps([128, 512]) for _ in range(2)]
    pvv = [P.ps([128, 512]) for _ in range(2)]
    pm = [P.ps([128, 512]) for _ in range(2)]
    pl = [P.ps([128, 512]) for _ in range(2)]

    NTL = NTOK // TT
    outs = []

    def load_tile(i):
        P.dma("sp", xf[i % 2][:], x_in[:, :, i * TT:(i + 1) * TT])
        P.copy(xb[i % 2][:], xf[i % 2][:], e="pool")

    load_tile(0)
    for i in range(NTL):
        X, XB = xf[i % 2], xb[i % 2]
        if i + 1 < NTL:
            load_tile(i + 1)
        for oc in range(8):
            pp = pu[oc % 2]
            for kc in range(8):
                P.mm(pp[:, 0:TT], win[kc][:, oc * 128:(oc + 1) * 128], XB[:, kc, :], start=(kc == 0), stop=(kc == 7))
            P.act(U[:, oc, :], pp[:, 0:TT], AF.Gelu, bias=bu[:, oc:oc + 1])
        for hf in range(2):
            pp = pvv[hf]
            for kc in range(8):
                P.mm(pp[:, :], XB[:, kc, :], win[kc][:, 1024 + hf * 512: 1024 + (hf + 1) * 512], start=(kc == 0), stop=(kc == 7))
            P.tt(V1[:, hf * 512:(hf + 1) * 512], pp[:, :], bv[:, hf * 512:(hf + 1) * 512], ALU.add)
        P.act(V1[:], V1[:], AF.Gelu)
        P.reduce(st[:, 0:1], V1[:], ALU.add)
        P.ts(st[:, 1:2], st[:, 0:1], -1.0 / 1024.0, ALU.mult)
        P.ts(V1[:], V1[:], st[:, 1:2], ALU.add)
        P.stt(junk[:], V1[:], 1.0, V1[:], ALU.mult, ALU.mult, accum_out=st[:, 2:3])
        P.act(st[:, 3:4], st[:, 2:3], AF.Sqrt, bias=P.const(1e-5)[:, 0:1], scale=1.0 / 1024.0)
        P.recip(st[:, 3:4], st[:, 3:4])
        P.stt(V1[:], V1[:], st[:, 3:4], lg[:], ALU.mult, ALU.mult)
        P.tt(VN[:], V1[:], lb[:], ALU.add, e="pool")
        for gi in range(8):
            pp = pm[gi % 2]
            P.mm(pp[:, 0:TT], VN[:, gi * 128:(gi + 1) * 128], wsb[:, gi, :])
            m = mx[gi % 2]
            P.tt(m[:], pp[:, 0:TT], bs[:, gi * 128:(gi + 1) * 128], ALU.add)
            P.tt(GU[:, gi, :], m[:], U[:, gi, :], ALU.mult, e="pool")
        for oc in range(8):
            pp = pu[oc % 2]
            for c in range(8):
                P.mm(pp[:, 0:TT], wout[c][:, oc * 128:(oc + 1) * 128], GU[:, c, :], start=(c == 0), stop=(c == 7))
            P.stt(X[:, oc, :], X[:, oc, :], ALPHA, pp[:, 0:TT], ALU.mult, ALU.add)
        ln_inplace(P, X, TT, onesm, g, b, pl, sq, stat)
        outs.append(P.dma("pool", out_d[:, :, i * TT:(i + 1) * TT], X[:]))
    return P.finish(outs)


def build_abin(NTOK=4096, TT=256):
    P = Prog()
    x_in = P.dram("x", [128, 8, NTOK + 1], F32, "ExternalInput")
    win_d = P.dram("w_in", [128, 8, 3328], F32, "ExternalInput")
    pp_d = P.dram("pp", [128, 34], F32, "ExternalInput")
    w2_d = P.dram("w2p", [128, 512], F32, "ExternalInput")
    a2_d = P.dram("a2p", [128, 512], F32, "ExternalInput")
    g2_d = P.dram("g2", [128, 512], F32, "ExternalInput")
    hblk_d = P.dram("hblk", [128, 128], F32, "ExternalInput")
    rm_d = P.dram("rotm", [128, 128], F32, "ExternalInput")
    cq_d = P.dram("cosq", [128, NTOK], F32, "ExternalInput")
    sq_d = P.dram("sinq", [128, NTOK], F32, "ExternalInput")
    ck_d = P.dram("cosk", [128, NTOK], F32, "ExternalInput")
    sk_d = P.dram("sink", [128, NTOK], F32, "ExternalInput")
    names = ["r", "w", "k", "v", "av", "b", "g", "bonus", "dq", "dk", "dv"]
    od = {n: P.dram("o_" + n, [128, 4, NTOK], F32, "ExternalOutput") for n in names}

    pp = P.sb([128, 34]); w2 = P.sb([128, 512]); a2 = P.sb([128, 512]); g2 = P.sb([128, 512])
    hblk = P.sb([128, 128]); rotm = P.sb([128, 128])
    for t_, d_ in ((pp, pp_d), (w2, w2_d), (a2, a2_d), (g2, g2_d), (hblk, hblk_d), (rotm, rm_d)):
        P.dma("sp", t_[:], d_[:])
    MU, W0, A0, KK_, KA, RK = 0, 14, 18, 22, 26, 30
    win = [P.sb([128, 3328], BF16) for _ in range(8)]
    stg = [P.sb([128, 1664], F32) for _ in range(2)]
    dst, src = [], []
    for kc in range(8):
        for j in range(2):
            dst.append(win[kc][:, j * 1664:(j + 1) * 1664]); src.append(win_d[:, kc, j * 1664:(j + 1) * 1664])
    load_cast(P, dst, src, stg)

    xf = [P.sb([128, 8, TT + 1], F32) for _ in range(2)]
    xb = [P.sb([128, 8, TT + 1], BF16) for _ in range(2)]
    prs = [P.sb([128, TT + 1], F32) for _ in range(2)]
    dd = [P.sb([128, TT], F32) for _ in range(2)]
    XS = P.sb([128, 14, TT], F32)
    TW = P.sb([128, TT], F32); SG = P.sb([128, TT], F32)
    O = {n: [P.sb([128, 4, TT], F32) for _ in range(1)] for n in names}
    tmp = [P.sb([128, TT], F32) for _ in range(4)]
    tabs = [P.sb([128, 4, TT], F32) for _ in range(2)]
    ps = [P.ps([128, 512]) for _ in range(8)]
    pi = [0]

    def nps():
        pi[0] += 1
        return ps[pi[0] % 8]

    NTL = NTOK // TT
    outs = []

    def load_tile(i):
        P.dma("sp", xf[i % 2][:], x_in[:, :, i * TT: i * TT + TT + 1])
        P.copy(xb[i % 2][:], xf[i % 2][:], e="pool")
        for j, d_ in enumerate((cq_d, sq_d, ck_d, sk_d)):
            P.dma("sp", tabs[i % 2][:, j, :], d_[:, i * TT:(i + 1) * TT])

    load_tile(0)
    for i in range(NTL):
        XB = xb[i % 2]
        TB = tabs[i % 2]
        if i + 1 < NTL:
            load_tile(i + 1)
        ob = {n: O[n][0] for n in names}
        for c in range(14):
            p_ = nps()
            for kc in range(8):
                P.mm(p_[:, 0:TT + 1], win[kc][:, c * 128:(c + 1) * 128], XB[:, kc, :], start=(kc == 0), stop=(kc == 7))
            s_ = prs[c % 2]; d_ = dd[c % 2]
            P.copy(s_[:], p_[:, 0:TT + 1], e="act")
            P.tt(d_[:], s_[:, 0:TT], s_[:, 1:TT + 1], ALU.subtract)
            if c < 4:
                dst_ = ob["r"][:, c, :]
            elif 8 <= c < 12:
                dst_ = ob["v"][:, c - 8, :]
            else:
                dst_ = XS[:, c, :]
            P.stt(dst_, d_[:], pp[:, MU + c:MU + c + 1], s_[:, 1:TT + 1], ALU.mult, ALU.add)
        P.act(TW[:], XS[:, 12, :], AF.Tanh)
        P.act(SG[:], XS[:, 13, :], AF.Sigmoid)
        for j in range(4):
            p_ = nps()
            P.mm(p_[:, 0:TT], w2[:, j * 128:(j + 1) * 128], TW[:])
            t_ = tmp[0]
            P.act(t_[:], p_[:, 0:TT], AF.Sigmoid, bias=pp[:, W0 + j:W0 + j + 1])
            P.ts(ob["w"][:, j, :], t_[:], -0.6065306597126334, ALU.mult, e="pool")
            p_ = nps()
            P.mm(p_[:, 0:TT], a2[:, j * 128:(j + 1) * 128], XS[:, 12, :])
            A_ = tmp[1]
            P.act(A_[:], p_[:, 0:TT], AF.Sigmoid, bias=pp[:, A0 + j:A0 + j + 1])
            p_ = nps()
            P.mm(p_[:, 0:TT], g2[:, j * 128:(j + 1) * 128], SG[:])
            P.copy(ob["g"][:, j, :], p_[:, 0:TT], e="act")
            kraw = XS[:, 4 + j, :]
            KKt = tmp[2]
            P.ts(KKt[:], kraw, pp[:, KK_ + j:KK_ + j + 1], ALU.mult)
            s2 = tmp[3]
            P.tt(s2[:], KKt[:], KKt[:], ALU.mult, e="pool")
            p_ = nps()
            P.mm(p_[:, 0:TT], hblk[:], s2[:])
            P.act(s2[:], p_[:, 0:TT], AF.Sqrt)
            P.ts(s2[:], s2[:], 1e-12, ALU.max)
            P.recip(s2[:], s2[:])
            P.tt(KKt[:], KKt[:], s2[:], ALU.mult)
            P.ts(ob["av"][:, j, :], KKt[:], -1.0, ALU.mult, e="pool")
            P.tt(ob["b"][:, j, :], KKt[:], A_[:], ALU.mult)
            P.ts(A_[:], A_[:], -1.0, ALU.add, pp[:, KA + j:KA + j + 1], ALU.mult)
            P.stt(ob["k"][:, j, :], A_[:], 1.0, kraw, ALU.add, ALU.mult)
            P.stt(s2[:], ob["r"][:, j, :], pp[:, RK + j:RK + j + 1], ob["k"][:, j, :], ALU.mult, ALU.mult)
            p_ = nps()
            P.mm(p_[:, 0:TT], hblk[:], s2[:])
            P.tt(ob["bonus"][:, j, :], p_[:, 0:TT], ob["v"][:, j, :], ALU.mult)
        for c in range(12):
            p_ = nps()
            col = 1792 + c * 128
            for kc in range(8):
                P.mm(p_[:, 0:TT], win[kc][:, col:col + 128], XB[:, kc, 1:TT + 1], start=(kc == 0), stop=(kc == 7))
            if c >= 8:
                P.copy(ob["dv"][:, c - 8, :], p_[:, 0:TT], e="act")
                continue
            isq = c < 4
            j = c % 4
            qs = tmp[c % 2]
            P.copy(qs[:], p_[:, 0:TT], e="act")
            p2 = nps()
            P.mm(p2[:, 0:TT], rotm[:], qs[:])
            t2 = tmp[2 + c % 2]
            P.tt(t2[:], p2[:, 0:TT], TB[:, 1 if isq else 3, :], ALU.mult)
            P.tt(qs[:], qs[:], TB[:, 0 if isq else 2, :], ALU.mult, e="pool")
            P.tt((ob["dq"] if isq else ob["dk"])[:, j, :], qs[:], t2[:], ALU.add)
        for n in names:
            outs.append(P.dma("pool", od[n][:, :, i * TT:(i + 1) * TT], ob[n][:]))
    return P.finish(outs)


def build_scan(NH=2, NCH=256):
    L = 64
    P = Prog()
    fmp_d = P.dram("fmp", [NH, NCH, 64, 256], F32, "ExternalInput")
    tmp_d = P.dram("tmp", [NH, NCH, 64, 256], F32, "ExternalInput")
    cd = {n: P.dram(n, s, F32, "ExternalInput") for n, s in
          (("TRI2", [64, 128]), ("TGT", [64, 64]), ("MASK2", [64, 128]), ("MASKT", [64, 64]), ("IDENT", [64, 64]))}
    y_d = P.dram("y", [NH, 64, NCH * L], F32, "ExternalOutput")
    C = {}
    for n, d_ in cd.items():
        C[n] = P.sb(list(d_.h.shape), F32)
        P.dma("sp", C[n][:], d_[:])
    NS = 2
    def mk():
        return dict(FM=P.sb([64, 256]), TM=P.sb([64, 256]), E12=P.sb([64, 128]), E3=P.sb([64, 64]), E4=P.sb([64, 64]),
                    AR=P.sb([64, 128]), BK=P.sb([64, 128]), BKT=P.sb([64, 128]), AB=P.sb([64, 128]), AK=P.sb([64, 128]),
                    X=P.sb([64, 64]), XT=P.sb([64, 64]), Tm=P.sb([64, 64]), Psb=P.sb([64, 64]), U=P.sb([64, 64]))
    sets = [[mk() for _ in range(NS)] for _ in range(NH)]
    ST = [P.sb([64, 64]) for _ in range(NH)]
    for h in range(NH):
        P.memset(ST[h][:], 0.0)
    YB = [[P.sb([64, 512]) for _ in range(2)] for _ in range(NH)]
    banks = [P.ps([128, 512]) for _ in range(8)]
    pi = [0]

    def nps():
        pi[0] += 1
        return banks[pi[0] % 8]

    def load(h, c):
        s = sets[h][c % NS]
        P.dma("sp", s["FM"][:], fmp_d[h, c])
        P.dma("sp", s["TM"][:], tmp_d[h, c])

    def prep(h, c):
        s = sets[h][c % NS]
        FM, TM = s["FM"], s["TM"]
        lw = TM[:, 128:192]
        cps = nps(); sfx = nps()
        P.mm(cps[0:64, 0:128], lw, C["TRI2"][:])
        P.mm(sfx[0:64, 0:64], C["TGT"][:], lw)
        P.act(s["E12"][:], cps[0:64, 0:128], AF.Exp)
        P.act(s["E3"][:], cps[0:64, 0:64], AF.Exp, scale=-1.0)
        P.act(s["E4"][:], sfx[0:64, 0:64], AF.Exp)
        P.tt(s["AR"][:, 0:64], FM[:, 128:192], s["E12"][:, 64:128], ALU.mult, e="pool")
        P.tt(s["AR"][:, 64:128], FM[:, 0:64], s["E12"][:, 0:64], ALU.mult, e="pool")
        P.tt(s["BK"][:, 0:64], FM[:, 192:256], s["E3"][:], ALU.mult, e="pool")
        P.tt(s["BK"][:, 64:128], FM[:, 64:128], s["E3"][:], ALU.mult, e="pool")
        P.tt(s["BKT"][:, 0:64], TM[:, 0:64], s["E4"][:], ALU.mult, e="pool")
        P.tt(s["BKT"][:, 64:128], TM[:, 64:128], s["E4"][:], ALU.mult, e="pool")
        pb = nps(); pk = nps(); pn = nps()
        P.mm(pb[0:64, 0:128], s["BK"][:, 0:64], s["AR"][:])
        P.mm(pk[0:64, 0:128], s["BK"][:, 64:128], s["AR"][:])
        P.mm(pn[0:64, 0:64], s["AR"][:, 0:64], s["BK"][:, 0:64])
        P.tt(s["AB"][:], pb[0:64, 0:128], C["MASK2"][:], ALU.mult)
        P.tt(s["AK"][:], pk[0:64, 0:128], C["MASK2"][:], ALU.mult)
        P.tt(s["XT"][:], pn[0:64, 0:64], C["MASKT"][:], ALU.mult)
        P.copy(s["X"][:], s["AB"][:, 0:64], e="pool")
        P.tt(s["Tm"][:], s["AB"][:, 0:64], C["IDENT"][:], ALU.add, e="pool")
        for j in range(1, 6):
            pxt = nps()
            P.mm(pxt[0:64, 0:64], s["X"][:], s["XT"][:])
            if j < 5:
                px = nps()
                P.mm(px[0:64, 0:64], s["XT"][:], s["X"][:])
                P.copy(s["X"][:], px[0:64, 0:64], e="act")
            P.copy(s["XT"][:], pxt[0:64, 0:64])
            pt = nps()
            P.mm(pt[0:64, 0:64], s["XT"][:], s["Tm"][:])
            P.tt(s["Tm"][:], pt[0:64, 0:64], s["Tm"][:], ALU.add)

    outs = []

    def rec(h, c):
        s = sets[h][c % NS]
        V_ = s["TM"][:, 192:256]
        pp = nps()
        P.mm(pp[0:64, 0:64], s["AR"][:, 0:64], ST[h][:], start=True, stop=False)
        P.mm(pp[0:64, 0:64], s["AK"][:, 0:64], V_, start=False, stop=True)
        P.copy(s["Psb"][:], pp[0:64, 0:64], e="act")
        pu = nps()
        P.mm(pu[0:64, 0:64], s["Tm"][:], s["Psb"][:])
        P.copy(s["U"][:], pu[0:64, 0:64], e="act")
        py = nps()
        P.mm(py[0:64, 0:64], ST[h][:], s["AR"][:, 64:128], start=True, stop=False)
        P.mm(py[0:64, 0:64], s["U"][:], s["AB"][:, 64:128], start=False, stop=False)
        P.mm(py[0:64, 0:64], V_, s["AK"][:, 64:128], start=False, stop=True)
        pd_ = nps()
        P.mm(pd_[0:64, 0:64], s["BKT"][:, 0:64], s["U"][:], start=True, stop=False)
        P.mm(pd_[0:64, 0:64], s["BKT"][:, 64:128], V_, start=False, stop=True)
        P.stt(ST[h][:], ST[h][:], s["E12"][:, 63:64], pd_[0:64, 0:64], ALU.mult, ALU.add)
        yb = YB[h][(c // 8) % 2]
        P.copy(yb[:, (c % 8) * 64:(c % 8 + 1) * 64], py[0:64, 0:64], e="act")
        if c % 8 == 7 or c == NCH - 1:
            c0 = (c // 8) * 8
            outs.append(P.dma("pool", y_d[h, :, c0 * L:(c + 1) * L], yb[:, 0:(c - c0 + 1) * L]))

    for h in range(NH):
        load(h, 0)
    for h in range(NH):
        prep(h, 0)
    for c in range(NCH):
        for h in range(NH):
            if c + 1 < NCH:
                load(h, c + 1)
                prep(h, c + 1)
            rec(h, c)
    return P.finish(outs)


def build_attn(T=16384):
    NB = T // 128
    NG = T // 512
    P = Prog()
    q_d = P.dram("qT", [64, 2, T], F32, "ExternalInput")
    k_d = P.dram("kT", [64, 2, T], F32, "ExternalInput")
    v_d = P.dram("v", [128, NB, 128], F32, "ExternalInput")
    lam_d = P.dram("lamv", [1, 256], F32, "ExternalInput")
    li_d = P.dram("lam_init", [1, 1], F32, "ExternalInput")
    sg_d = P.dram("subln_g", [1, 128], F32, "ExternalInput")
    tri_d = P.dram("tri", [128, 128], F32, "ExternalInput")
    sel_d = P.dram("sel", [64, 65], F32, "ExternalInput")
    o_d = P.dram("o", [NB, 128, 128], F32, "ExternalOutput")

    lamv = P.sb([128, 256]); li = P.sb([128, 1]); gsc = P.sb([128, 128]); trif = P.sb([128, 128]); sel = P.sb([64, 65])
    P.dma("sp", lamv[:], lam_d[0:1, :].to_broadcast([128, 256]))
    P.dma("sp", li[:], li_d[0:1, :].to_broadcast([128, 1]))
    P.dma("sp", gsc[:], sg_d[0:1, :].to_broadcast([128, 128]))
    P.dma("sp", trif[:], tri_d[:])
    P.dma("sp", sel[:], sel_d[:])
    tri = P.sb([128, 128], BF16)
    P.copy(tri[:], trif[:])
    sc = P.sb([128, 8]); junk = P.sb([128, 128])
    P.stt(junk[:, 0:64], lamv[:, 0:64], 1.0, lamv[:, 64:128], ALU.mult, ALU.mult, accum_out=sc[:, 0:1])
    P.stt(junk[:, 0:64], lamv[:, 128:192], 1.0, lamv[:, 192:256], ALU.mult, ALU.mult, accum_out=sc[:, 1:2])
    P.act(sc[:, 2:4], sc[:, 0:2], AF.Exp)
    P.tt(sc[:, 4:5], sc[:, 2:3], sc[:, 3:4], ALU.subtract)
    P.tt(sc[:, 4:5], sc[:, 4:5], li[:], ALU.add)
    P.ts(sc[:, 5:6], sc[:, 4:5], -1.0, ALU.mult)
    P.ts(sc[:, 6:7], li[:], -1.0, ALU.mult, 1.0, ALU.add)
    P.ts(gsc[:], gsc[:], sc[:, 6:7], ALU.mult)
    nlam = sc[:, 5:6]

    Ka = [P.sb([65, T], BF16) for _ in range(2)]
    Va = P.sb([128, NB, 129], BF16)
    for c in range(2):
        P.memset(Ka[c][64:65, :], 1.0, e="pool")
    P.memset(Va[:, :, 128:129], 1.0, e="pool")
    kst = P.sb([65, 8])
    P.memset(kst[64:65, :], 0.0)
    stg = [P.sb([128, 2048], F32) for _ in range(2)]
    sqb = [P.sb([64, 1024], F32) for _ in range(2)]
    pss = [P.ps([128, 512]) for _ in range(3)]
    pso = [P.ps([128, 512]) for _ in range(4)]
    pmisc = P.ps([128, 512])
    for i in range(NG):
        st = stg[i % 2]
        kf = st[0:64, 0:1024].rearrange("p (c t) -> p c t", c=2)
        P.dma("sp", kf, k_d[:, :, i * 512:(i + 1) * 512])
        s2 = sqb[i % 2]
        P.tt(s2[:], st[0:64, 0:1024], st[0:64, 0:1024], ALU.mult, e="pool")
        for c in range(2):
            P.copy(Ka[c][0:64, i * 512:(i + 1) * 512], st[0:64, c * 512:(c + 1) * 512], e="act")
            P.mm(pmisc[0:65, 0:512], sel[:], s2[:, c * 512:(c + 1) * 512])
            P.reduce(kst[64:65, 2 + c:3 + c], pmisc[64:65, 0:512], ALU.max)
            P.tt(kst[64:65, c:c + 1], kst[64:65, c:c + 1], kst[64:65, 2 + c:3 + c], ALU.max)
    VP = min(16, NB)
    for i in range(NB // VP):
        st = stg[i % 2]
        P.dma("sp", st[:, 0:VP * 128].rearrange("p (n d) -> p n d", d=128), v_d[:, i * VP:(i + 1) * VP, :])
        P.copy(Va[:, i * VP:(i + 1) * VP, 0:128], st[:, 0:VP * 128].rearrange("p (n d) -> p n d", d=128), e=("dve" if i % 2 else "pool"))
    P.act(kst[64:65, 4:6], kst[64:65, 0:2], AF.Sqrt)
    P.ts(kst[64:65, 6:8], kst[64:65, 4:6], -1.0, ALU.mult)

    qf = [P.sb([64, 2, 512], F32) for _ in range(2)]
    qsq = P.sb([64, 2, 512], F32)
    nq = P.sb([65, 512], F32)
    Qa = [[P.sb([65, 512], BF16) for _ in range(2)] for _ in range(2)]
    PT = [P.sb([128, 512], BF16) for _ in range(3)]
    res = [P.sb([128, 128], F32) for _ in range(4)]
    ob = [P.sb([128, 128], F32) for _ in range(2)]
    st2 = P.sb([128, 8], F32)
    outs = []
    P.dma("sp", qf[0][:], q_d[:, :, 0:512])
    cnt = 0
    for qg in range(NG):
        Qf = qf[qg % 2]
        if qg + 1 < NG:
            P.dma("sp", qf[(qg + 1) % 2][:], q_d[:, :, (qg + 1) * 512:(qg + 2) * 512])
        P.tt(qsq[:], Qf[:], Qf[:], ALU.mult, e="pool")
        Q = Qa[qg % 2]
        for c in range(2):
            P.copy(Q[c][0:64, :], Qf[:, c, :], e="pool")
            P.mm(pmisc[0:65, 0:512], sel[:], qsq[:, c, :])
            P.act(nq[64:65, :], pmisc[64:65, 0:512], AF.Sqrt)
            P.ts(Q[c][64:65, :], nq[64:65, :], kst[64:65, 6 + c:7 + c], ALU.mult)
        for c in range(2):
            nkb = (qg + 1) * 4
            for kb in range(nkb):
                j = kb - qg * 4
                q0 = max(j, 0) * 128
                ps_ = pss[cnt % 3]; pt = PT[cnt % 3]; cnt += 1
                P.mm(ps_[:, q0:512], Ka[c][:, kb * 128:(kb + 1) * 128], Q[c][:, q0:512])
                P.act(pt[:, q0:512], ps_[:, q0:512], AF.Exp)
                if j >= 0:
                    P.tt(pt[:, q0:q0 + 128], pt[:, q0:q0 + 128], tri[:], ALU.mult, e="dve")
                for qb in range(max(j, 0), 4):
                    P.mm(pso[qb][:, 0:129], pt[:, qb * 128:(qb + 1) * 128], Va[:, kb, :],
                         start=(kb == 0), stop=(kb == qg * 4 + qb))
            for qb in range(4):
                oo = pso[qb]
                if c == 0:
                    P.recip(st2[:, qb:qb + 1], oo[:, 128:129])
                    P.ts(res[qb][:], oo[:, 0:128], st2[:, qb:qb + 1], ALU.mult)
                else:
                    a_ = ob[qb % 2]
                    P.recip(st2[:, 4:5], oo[:, 128:129])
                    P.tt(st2[:, 5:6], st2[:, 4:5], nlam, ALU.mult)
                    P.stt(a_[:], oo[:, 0:128], st2[:, 5:6], res[qb][:], ALU.mult, ALU.add)
                    P.stt(junk[:], a_[:], 1.0, a_[:], ALU.mult, ALU.mult, accum_out=st2[:, 6:7])
                    P.act(st2[:, 7:8], st2[:, 6:7], AF.Sqrt, bias=P.const(1e-5)[:, 0:1], scale=1.0 / 128.0)
                    P.recip(st2[:, 7:8], st2[:, 7:8])
                    P.stt(a_[:], a_[:], st2[:, 7:8], gsc[:], ALU.mult, ALU.mult)
                    outs.append(P.dma("pool", o_d[qg * 4 + qb], a_[:]))
    return P.finish(outs)


def build_about(NTOK=4096, TT=256):
    P = Prog()
    x_in = P.dram("x", [128, 8, NTOK], F32, "ExternalInput")
    y_d = P.dram("y", [128, 4, NTOK], F32, "ExternalInput")
    bo_d = P.dram("bonus", [128, 4, NTOK], F32, "ExternalInput")
    g_d = P.dram("g", [128, 4, NTOK], F32, "ExternalInput")
    od_d = P.dram("od", [128, 4, NTOK], F32, "ExternalInput")
    wout_d = P.dram("w_out", [128, 8, 1024], F32, "ExternalInput")
    pp_d = P.dram("pp", [128, 8], F32, "ExternalInput")
    hb_d = P.dram("hblk64", [128, 128], F32, "ExternalInput")
    lg_d = P.dram("ln_g", [128, 8], F32, "ExternalInput")
    lb_d = P.dram("ln_b", [128, 8], F32, "ExternalInput")
    out_d = P.dram("x1", [128, 8, NTOK], F32, "ExternalOutput")

    onesm = P.sb([128, 128], F32)
    P.memset(onesm[:], 1.0 / 1024.0)
    pp = P.sb([128, 8]); hb = P.sb([128, 128]); lg = P.sb([128, 8]); lb = P.sb([128, 8])
    for t_, d_ in ((pp, pp_d), (hb, hb_d), (lg, lg_d), (lb, lb_d)):
        P.dma("sp", t_[:], d_[:])
    wout = [P.sb([128, 1024], BF16) for _ in range(8)]
    stg = [P.sb([128, 1024], F32) for _ in range(2)]
    load_cast(P, [w[:, :] for w in wout], [wout_d[:, c, :] for c in range(8)], stg)

    xf = [P.sb([128, 8, TT], F32) for _ in range(2)]
    yf = [P.sb([128, 4, TT], F32) for _ in range(2)]
    bf = [P.sb([128, 4, TT], F32) for _ in range(2)]
    gf = [P.sb([128, 4, TT], F32) for _ in range(2)]
    of = [P.sb([128, 4, TT], F32) for _ in range(2)]
    MIX = P.sb([128, 8, TT], BF16)
    dt_ = [P.sb([128, TT], F32) for _ in range(2)]
    s2 = [P.sb([128, TT], F32) for _ in range(2)]
    rs = [P.sb([128, TT], F32) for _ in range(2)]
    sq = [P.sb([128, TT], F32) for _ in range(2)]
    stat = [P.sb([128, TT], F32) for _ in range(3)]
    pm = [P.ps([128, 512]) for _ in range(2)]
    pvr = [P.ps([128, 512]) for _ in range(2)]
    po = [P.ps([128, 512]) for _ in range(2)]
    pl = [P.ps([128, 512]) for _ in range(2)]
    NTL = NTOK // TT
    outs = []

    def load_tile(i):
        sl = slice(i * TT, (i + 1) * TT)
        P.dma("sp", xf[i % 2][:], x_in[:, :, sl])
        P.dma("sp", yf[i % 2][:], y_d[:, :, sl])
        P.dma("sp", bf[i % 2][:], bo_d[:, :, sl])
        P.dma("sp", gf[i % 2][:], g_d[:, :, sl])
        P.dma("sp", of[i % 2][:], od_d[:, :, sl])

    load_tile(0)
    for i in range(NTL):
        X, Y, B, G, O_ = xf[i % 2], yf[i % 2], bf[i % 2], gf[i % 2], of[i % 2]
        if i + 1 < NTL:
            load_tile(i + 1)
        for j in range(4):
            d_ = dt_[j % 2]; q_ = s2[j % 2]; r_ = rs[j % 2]
            P.mm(pm[j % 2][:, 0:TT], hb[:], Y[:, j, :])
            P.tt(d_[:], Y[:, j, :], pm[j % 2][:, 0:TT], ALU.subtract)
            P.tt(q_[:], d_[:], d_[:], ALU.mult, e="pool")
            P.mm(pvr[j % 2][:, 0:TT], hb[:], q_[:])
            P.act(r_[:], pvr[j % 2][:, 0:TT], AF.Sqrt, bias=P.const(64e-5)[:, 0:1])
            P.recip(r_[:], r_[:])
            P.tt(d_[:], d_[:], r_[:], ALU.mult)
            P.act(d_[:], d_[:], AF.Identity, bias=pp[:, 4 + j:5 + j], scale=pp[:, j:j + 1])
            P.tt(d_[:], d_[:], B[:, j, :], ALU.add, e="pool")
            P.tt(MIX[:, j, :], d_[:], G[:, j, :], ALU.mult)
            P.copy(MIX[:, 4 + j, :], O_[:, j, :], e="pool")
        for oc in range(8):
            p_ = po[oc % 2]
            for c in range(8):
                P.mm(p_[:, 0:TT], wout[c][:, oc * 128:(oc + 1) * 128], MIX[:, c, :], start=(c == 0), stop=(c == 7))
            P.stt(X[:, oc, :], X[:, oc, :], ALPHA, p_[:, 0:TT], ALU.mult, ALU.add)
        ln_inplace(P, X, TT, onesm, lg, lb, pl, sq, stat)
        outs.append(P.dma("pool", out_d[:, :, i * TT:(i + 1) * TT], X[:]))
    return P.finish(outs)


NTOK = 4096
SEQ = 16384
_PROGS = {}


def _prog(name, fn):
    if name not in _PROGS:
        _PROGS[name] = fn()
    return _PROGS[name]


def _run(nc, in_maps):
    return run_bass_kernel_spmd(nc, in_maps, core_ids=list(range(8))).results


def _fm(a):
    T, C = a.shape
    return np.ascontiguousarray(a.T.reshape(C // 128, 128, T).transpose(1, 0, 2))


def _col(v, n):
    return np.ascontiguousarray(np.asarray(v, np.float32).reshape(n, 128).T)


def _wl(w):
    K, N = w.shape
    return np.ascontiguousarray(w.reshape(K // 128, 128, N).transpose(1, 0, 2))


def _rope_tabs(pos):
    inv = (500000.0 ** (-np.arange(0, 16, 2, dtype=np.float32) / 16)).astype(np.float32)
    ang = pos.astype(np.float32)[:, None] * inv[None, :]
    c, s = np.cos(ang).astype(np.float32), np.sin(ang).astype(np.float32)
    C = np.ones((128, len(pos)), np.float32)
    S = np.zeros((128, len(pos)), np.float32)
    for comp in range(2):
        for half in range(2):
            lo = comp * 64 + half * 8
            C[lo:lo + 8] = c.T
            S[lo:lo + 8] = s.T
    return C * np.float32(0.125), S * np.float32(0.125), C, S


def _rotm():
    R = np.zeros((128, 128), np.float32)
    for comp in range(2):
        for p in range(8):
            R[comp * 64 + p + 8, comp * 64 + p] = -1.0
            R[comp * 64 + p, comp * 64 + p + 8] = 1.0
    return R


def _scan_consts(L=64):
    i = np.arange(L)[:, None]
    t = np.arange(L)[None, :]
    incl = (i <= t).astype(np.float32)
    strict = (i < t).astype(np.float32)
    gt = (i > t).astype(np.float32)
    return {"TRI2": np.concatenate([incl, strict], 1), "TGT": gt, "MASK2": np.concatenate([strict, incl], 1),
            "MASKT": np.ascontiguousarray(strict.T), "IDENT": np.eye(L, dtype=np.float32)}


def _ab_layer(xs, inp, i, j):
    f32 = np.float32
    T = SEQ
    w_in = inp["ab_w_in"][j]
    pp = np.concatenate([_col(inp["ab_shift_mu"][j], 14), _col(inp["ab_w0"][j], 4), _col(inp["ab_a0"][j], 4),
                         _col(inp["ab_k_k"][j], 4), _col(inp["ab_k_a"][j], 4), _col(inp["ab_r_k"][j].reshape(-1), 4)], axis=1)
    w2p = np.zeros((128, 512), f32); w2p[:64] = inp["ab_w2"][j]
    a2p = np.zeros((128, 512), f32); a2p[64:] = inp["ab_a2"][j]
    hb = np.zeros((128, 128), f32); hb[:64, :64] = 1; hb[64:, 64:] = 1
    wi = {"w_in": _wl(w_in), "pp": np.ascontiguousarray(pp), "w2p": w2p, "a2p": a2p,
          "g2": np.ascontiguousarray(inp["ab_g2"][j]), "hblk": hb, "rotm": _rotm()}
    in_maps = []
    for c in range(8):
        b, q = divmod(c, 4)
        halo = xs[c - 1][:, :, -1:] if q > 0 else np.zeros((128, 8, 1), f32)
        cq, sq, ck, sk = _rope_tabs(np.arange(NTOK) + q * NTOK)
        in_maps.append({"x": np.ascontiguousarray(np.concatenate([halo, xs[c]], axis=2)),
                        "cosq": cq, "sinq": sq, "cosk": ck, "sink": sk, **wi})
    resA = _run(_prog("abin", lambda: build_abin(NTOK)), in_maps)
    sc = _scan_consts()
    in_maps = []
    for c in range(8):
        b, hp = divmod(c, 4)
        def gat(name):
            return np.concatenate([resA[4 * b + q]["o_" + name][:, hp, :] for q in range(4)], axis=1).reshape(2, 64, T // 64, 64)
        r, k, av, bb, lw, v = (gat(n) for n in ("r", "k", "av", "b", "w", "v"))
        fm_ = lambda X: X.transpose(0, 2, 1, 3)
        tm_ = lambda X: X.transpose(0, 2, 3, 1)
        fmp = np.ascontiguousarray(np.concatenate([fm_(r), fm_(k), fm_(av), fm_(bb)], axis=3))
        tmp = np.ascontiguousarray(np.concatenate([tm_(bb), tm_(k), tm_(lw), tm_(v)], axis=3))
        in_maps.append({"fmp": fmp, "tmp": tmp, **sc})
    resS = _run(_prog("scan", lambda: build_scan(2, T // 64)), in_maps)
    kk_ = np.arange(128)[:, None]; qq_ = np.arange(128)[None, :]
    sel = np.zeros((64, 65), f32); sel[:, 64] = 1
    lam_init = 0.8 - 0.6 * math.exp(-0.3 * i)
    ac = {"tri": (kk_ <= qq_).astype(f32), "sel": sel,
          "lamv": np.concatenate([inp["ab_lam_q1"][j], inp["ab_lam_k1"][j], inp["ab_lam_q2"][j], inp["ab_lam_k2"][j]]).reshape(1, 256).astype(f32),
          "lam_init": np.full((1, 1), lam_init, f32), "subln_g": np.ascontiguousarray(inp["ab_subln_g"][j].reshape(1, 128))}
    in_maps = []
    for c in range(8):
        b, h = divmod(c, 4)
        def gat(name):
            return np.concatenate([resA[4 * b + q]["o_" + name][:, h, :] for q in range(4)], axis=1)
        qT = np.ascontiguousarray(gat("dq").reshape(2, 64, T).transpose(1, 0, 2))
        kT = np.ascontiguousarray(gat("dk").reshape(2, 64, T).transpose(1, 0, 2))
        v = np.ascontiguousarray(gat("dv").T.reshape(T // 128, 128, 128).transpose(1, 0, 2))
        in_maps.append({"qT": qT, "kT": kT, "v": v, **ac})
    resT = _run(_prog("attn", lambda: build_attn(T)), in_maps)
    hb64 = hb / f32(64.0)
    wo = {"w_out": _wl(inp["ab_w_out"][j]),
          "pp": np.ascontiguousarray(np.concatenate([_col(inp["ab_lnx_g"][j], 4), _col(inp["ab_lnx_b"][j], 4)], 1)),
          "hblk64": hb64, "ln_g": _col(inp["ln1_g"][i], 8), "ln_b": _col(inp["ln1_b"][i], 8)}
    in_maps = []
    for c in range(8):
        b, q = divmod(c, 4)
        sl = slice(q * NTOK, (q + 1) * NTOK)
        y = np.ascontiguousarray(np.stack([resS[4 * b + hp]["y"].reshape(128, T)[:, sl] for hp in range(4)], axis=1))
        od = np.ascontiguousarray(np.stack([resT[4 * b + h]["o"].reshape(T, 128)[sl].T for h in range(4)], axis=1))
        in_maps.append({"x": xs[c], "y": y, "bonus": resA[c]["o_bonus"], "g": resA[c]["o_g"], "od": od, **wo})
    resB = _run(_prog("about", lambda: build_about(NTOK)), in_maps)
    return [r["x1"] for r in resB]


def _c_layer(xs, inp, i, j):
    b_in = inp["c_b_in"][j]
    wi = {"w_in": _wl(inp["c_w_in"][j]), "w_out": _wl(inp["c_w_out"][j]),
          "wsT": np.ascontiguousarray(inp["c_w_s"][j].transpose(2, 0, 1)),
          "b_u": _col(b_in[:1024], 8), "b_v": np.ascontiguousarray(b_in[1024:].reshape(1, 1024)),
          "cln_g": np.ascontiguousarray(inp["c_ln_g"][j].reshape(1, 1024)),
          "cln_b": np.ascontiguousarray(inp["c_ln_b"][j].reshape(1, 1024)),
          "b_s": np.ascontiguousarray(inp["c_b_s"][j].reshape(1, 1024)),
          "ln_g": _col(inp["ln1_g"][i], 8), "ln_b": _col(inp["ln1_b"][i], 8)}
    res = _run(_prog("gmlp", lambda: build_gmlp(NTOK)), [{"x": xs[c], **wi} for c in range(8)])
    return [r["x1"] for r in res]


def _ffn_layer(xs, inp, i):
    f32 = np.float32
    wi = {"w_up": _wl(inp["ffn_w_up"][i]), "w_down": _wl(inp["ffn_w_down"][i]),
          "conv_w": np.ascontiguousarray(inp["ffn_conv_w"][i].reshape(3, 22, 128).transpose(2, 0, 1)),
          "conv_b": _col(inp["ffn_conv_b"][i], 22), "ln_g": _col(inp["ln2_g"][i], 8), "ln_b": _col(inp["ln2_b"][i], 8)}
    in_maps = []
    for c in range(8):
        b, q = divmod(c, 4)
        halo = xs[c - 1][:, :, -2:] if q > 0 else np.zeros((128, 8, 2), f32)
        xin = np.ascontiguousarray(np.concatenate([halo, xs[c]], axis=2)[:, :, None, :])
        in_maps.append({"x1": xin, "hmask": np.full((128, 1), 1.0 if q > 0 else 0.0, f32), **wi})
    res = _run(_prog("ffn", lambda: build_ffn(1, NTOK)), in_maps)
    return [r["x2"] for r in res]


def kernel(**inp):
    inp = {k: np.asarray(v, np.float32) for k, v in inp.items()}
    x = inp["x"]
    xs = [_fm(x[c // 4, (c % 4) * NTOK:(c % 4 + 1) * NTOK]) for c in range(8)]
    for i in range(4):
        j = i // 2
        if i % 2 == 0:
            xs = _ab_layer(xs, inp, i, j)
        else:
            xs = _c_layer(xs, inp, i, j)
        xs = _ffn_layer(xs, inp, i)
    out = np.empty((2, SEQ, 1024), np.float32)
    for c in range(8):
        out[c // 4, (c % 4) * NTOK:(c % 4 + 1) * NTOK] = xs[c].transpose(1, 0, 2).reshape(1024, NTOK).T
    return out
```

```python
import math
import numpy as np
from contextlib import ExitStack
import concourse.bass as bass
import concourse.mybir as mybir
from concourse.bass_utils import run_bass_kernel_spmd

F32 = mybir.dt.float32
BF16 = mybir.dt.bfloat16
AF = mybir.ActivationFunctionType
ALU = mybir.AluOpType
AX = mybir.AxisListType

EPOCH = 12000
NDSLOT = 24


class Tk:
    def __init__(self, h, name):
        self.h = h
        self.name = name
        self.lw = None
        self.rd = {}

    def __getitem__(self, idx):
        return V(self, self.h[idx])

    def ap(self):
        return V(self, self.h[:])


class V:
    def __init__(self, tk, ap):
        self.tk = tk
        self.ap = ap

    def __getitem__(self, idx):
        return V(self.tk, self.ap[idx])

    def rearrange(self, *a, **k):
        return V(self.tk, self.ap.rearrange(*a, **k))

    def bitcast(self, dt):
        return V(self.tk, self.ap.bitcast(dt))

    def to_broadcast(self, shape):
        return V(self.tk, self.ap.to_broadcast(shape))


def _ap(x):
    return x.ap if isinstance(x, V) else x


class Prog:
    def __init__(self, name="k"):
        self.nc = bass.Bass("TRN2", target_bir_lowering=False)
        self.es = ExitStack()
        nc = self.nc
        self.engs = {"pe": nc.tensor, "act": nc.scalar, "dve": nc.vector,
                     "pool": nc.gpsimd, "sp": nc.sync}
        self.cnt = {k: 0 for k in self.engs}
        self.sems = {k: [] for k in self.engs}
        self.waited = {k: {} for k in self.engs}
        self.dsem = [self.es.enter_context(nc.semaphore(f"d{i}")) for i in range(NDSLOT)]
        self.dval = [0] * NDSLOT
        self.dslot = 0
        self.nt = 0
        self.out_tokens = []

    def sb(self, shape, dt=F32, name=None):
        self.nt += 1
        name = name or f"t{self.nt}"
        h = self.es.enter_context(self.nc.sbuf_tensor(name, list(shape), dt))
        return Tk(h, name)

    def ps(self, shape, dt=F32, name=None):
        self.nt += 1
        name = name or f"p{self.nt}"
        h = self.es.enter_context(self.nc.psum_tensor(name, list(shape), dt))
        return Tk(h, name)

    def dram(self, name, shape, dt=F32, kind="Internal"):
        h = self.nc.dram_tensor(name, list(shape), dt, kind=kind)
        return Tk(h, name)

    def _sem(self, key, val):
        if isinstance(key, tuple):
            return self.dsem[key[1]], val
        ep = (val - 1) // EPOCH
        lst = self.sems[key]
        while len(lst) <= ep:
            lst.append(self.es.enter_context(self.nc.semaphore(f"s_{key}_{len(lst)}")))
        return lst[ep], (val - 1) % EPOCH + 1

    def _deps(self, e, reads, writes, pe_acc=False):
        deps = {}

        def add(tok):
            if tok is None:
                return
            k, v = tok
            if deps.get(k, 0) < v:
                deps[k] = v

        for x in reads:
            if isinstance(x, V):
                add(x.tk.lw)
        for x in writes:
            if isinstance(x, V):
                lw = x.tk.lw
                if not (pe_acc and lw is not None and lw[0] == "pe"):
                    add(lw)
                for k, v in x.tk.rd.items():
                    add((k, v))
        w = self.waited[e]
        eng = self.engs[e]
        for k, v in deps.items():
            if w.get(k, 0) >= v:
                continue
            w[k] = v
            sem, sv = self._sem(k, v)
            eng.wait_ge(sem, sv)

    def _mark(self, tok, reads, writes):
        k, v = tok
        for x in reads:
            if isinstance(x, V):
                if x.tk.rd.get(k, 0) < v:
                    x.tk.rd[k] = v
        for x in writes:
            if isinstance(x, V):
                x.tk.lw = tok
                x.tk.rd = {}

    def emit(self, e, fn, reads, writes, pe_acc=False):
        self._deps(e, reads, writes, pe_acc)
        ins = fn(self.engs[e])
        self.cnt[e] += 1
        tok = (e, self.cnt[e])
        sem, sv = self._sem(e, self.cnt[e])
        ins.then_inc(sem, 1)
        self._mark(tok, reads, writes)
        return tok

    def dma(self, q, out, in_, **kw):
        reads, writes = [in_], [out]
        self._deps(q, reads, writes)
        slot = self.dslot
        self.dslot = (slot + 1) % NDSLOT
        key = ("d", slot)
        prev = self.dval[slot]
        w = self.waited[q]
        if prev > 0 and w.get(key, 0) < prev:
            w[key] = prev
            self.engs[q].wait_ge(self.dsem[slot], prev)
        ins = self.engs[q].dma_start(out=_ap(out), in_=_ap(in_), **kw)
        ins.then_inc(self.dsem[slot], 16)
        self.dval[slot] += 16
        tok = (key, self.dval[slot])
        self._mark(tok, reads, writes)
        return tok

    def wait_tok(self, e, tok):
        k, v = tok
        w = self.waited[e]
        if w.get(k, 0) >= v:
            return
        w[k] = v
        sem, sv = self._sem(k, v)
        self.engs[e].wait_ge(sem, sv)

    def mm(self, out, lhsT, rhs, start=True, stop=True):
        return self.emit("pe", lambda E: E.matmul(_ap(out), _ap(lhsT), _ap(rhs), start=start, stop=stop),
                         [lhsT, rhs], [out], pe_acc=True)

    def transpose(self, out, in_, ident):
        return self.emit("pe", lambda E: E.transpose(_ap(out), _ap(in_), _ap(ident)),
                         [in_, ident], [out], pe_acc=True)

    def act(self, out, in_, func, bias=None, scale=1.0, accum_out=None, e="act"):
        reads = [in_]
        kw = {}
        if bias is not None:
            kw["bias"] = _ap(bias)
            reads.append(bias)
        if not isinstance(scale, (int, float)):
            reads.append(scale)
        kw["scale"] = _ap(scale)
        writes = [out]
        if accum_out is not None:
            kw["accum_out"] = _ap(accum_out)
            writes.append(accum_out)
        return self.emit(e, lambda E: E.activation(_ap(out), _ap(in_), func, **kw), reads, writes)

    def tt(self, out, in0, in1, op, e="dve"):
        return self.emit(e, lambda E: E.tensor_tensor(_ap(out), _ap(in0), _ap(in1), op), [in0, in1], [out])

    def ts(self, out, in0, s1, op0, s2=None, op1=None, e="dve", accum_out=None):
        reads = [in0, s1, s2]
        kw = {}
        writes = [out]
        if op1 is not None:
            kw["op1"] = op1
        if accum_out is not None:
            kw["accum_out"] = _ap(accum_out)
            writes.append(accum_out)
        return self.emit(e, lambda E: E.tensor_scalar(_ap(out), _ap(in0), _ap(s1), _ap(s2), op0, **kw),
                         reads, writes)

    def stt(self, out, in0, scalar, in1, op0, op1, e="dve", accum_out=None):
        kw = {}
        writes = [out]
        if accum_out is not None:
            kw["accum_out"] = _ap(accum_out)
            writes.append(accum_out)
        return self.emit(e, lambda E: E.scalar_tensor_tensor(_ap(out), _ap(in0), _ap(scalar), _ap(in1), op0, op1, **kw),
                         [in0, scalar, in1], writes)

    def copy(self, out, in_, e="dve"):
        if e == "act":
            return self.emit(e, lambda E: E.copy(_ap(out), _ap(in_)), [in_], [out])
        return self.emit(e, lambda E: E.tensor_copy(_ap(out), _ap(in_)), [in_], [out])

    def memset(self, out, val, e="dve"):
        return self.emit(e, lambda E: E.memset(_ap(out), val), [], [out])

    def reduce(self, out, in_, op, axis=AX.X, e="dve"):
        return self.emit(e, lambda E: E.tensor_reduce(_ap(out), _ap(in_), axis, op), [in_], [out])

    def const(self, val):
        if not hasattr(self, "_consts"):
            self._consts = {}
        if val not in self._consts:
            t = self.sb([128, 1], F32)
            self.memset(t[:], float(val), e="pool")
            self._consts[val] = t
        return self._consts[val]

    def recip(self, out, in_):
        return self.emit("dve", lambda E: E.reciprocal(_ap(out), _ap(in_)), [in_], [out])

    def bn_stats(self, out, in_):
        return self.emit("dve", lambda E: E.bn_stats(_ap(out), _ap(in_)), [in_], [out])

    def bn_aggr(self, out, in_):
        return self.emit("dve", lambda E: E.bn_aggr(_ap(out), _ap(in_)), [in_], [out])

    def finish(self, toks):
        for t in toks:
            self.wait_tok("sp", t)
        self.es.close()
        return self.nc


ALPHA = 8.0 ** 0.25
D = 1024
DFF = 2816
NFC = 22


def load_cast(P, dst_views, src_views, stg, i0=0):
    for i, (d, s) in enumerate(zip(dst_views, src_views)):
        st = stg[(i0 + i) % len(stg)]
        n = s.ap.shape[-1]
        P.dma("sp", st[:, 0:n], s)
        P.copy(d, st[:, 0:n], e=("dve" if (i0 + i) % 2 else "pool"))
    return i0 + len(dst_views)


def ln_inplace(P, z, TT, onesm, g, b, pl, sq, stat, eps=1e-5, out_bf=None):
    for c in range(8):
        P.mm(pl[0][:, 0:TT], onesm[:], z[:, c, :], start=(c == 0), stop=(c == 7))
        s = sq[c % 2]
        P.act(s[:, 0:TT], z[:, c, :], AF.Square)
        P.mm(pl[1][:, 0:TT], onesm[:], s[:, 0:TT], start=(c == 0), stop=(c == 7))
    mean, msq, rstd = stat
    P.copy(mean[:, 0:TT], pl[0][:, 0:TT], e="act")
    P.tt(msq[:, 0:TT], mean[:, 0:TT], mean[:, 0:TT], ALU.mult, e="pool")
    P.tt(rstd[:, 0:TT], pl[1][:, 0:TT], msq[:, 0:TT], ALU.subtract)
    P.act(rstd[:, 0:TT], rstd[:, 0:TT], AF.Sqrt, bias=P.const(eps)[:, 0:1])
    P.recip(rstd[:, 0:TT], rstd[:, 0:TT])
    for c in range(8):
        P.tt(z[:, c, :], z[:, c, :], mean[:, 0:TT], ALU.subtract)
        P.tt(z[:, c, :], z[:, c, :], rstd[:, 0:TT], ALU.mult, e="pool")
        P.act(z[:, c, :], z[:, c, :], AF.Identity, bias=b[:, c:c + 1], scale=g[:, c:c + 1])


def build_ffn(NSEG=2, SEGLEN=2048, TT=256):
    P = Prog()
    x_in = P.dram("x1", [128, 8, NSEG, SEGLEN + 2], F32, "ExternalInput")
    hmask_d = P.dram("hmask", [128, NSEG], F32, "ExternalInput")
    wup_d = P.dram("w_up", [128, 8, 2 * DFF], F32, "ExternalInput")
    wdn_d = P.dram("w_down", [128, NFC, D], F32, "ExternalInput")
    cw_d = P.dram("conv_w", [128, 3, NFC], F32, "ExternalInput")
    cb_d = P.dram("conv_b", [128, NFC], F32, "ExternalInput")
    g_d = P.dram("ln_g", [128, 8], F32, "ExternalInput")
    b_d = P.dram("ln_b", [128, 8], F32, "ExternalInput")
    out_d = P.dram("x2", [128, 8, NSEG * SEGLEN], F32, "ExternalOutput")

    onesm = P.sb([128, 128], F32)
    P.memset(onesm[:], 1.0 / 1024.0)
    hm = P.sb([128, NSEG]); cw = P.sb([128, 3, NFC]); cb = P.sb([128, NFC]); g = P.sb([128, 8]); b = P.sb([128, 8])
    for t_, d_ in ((hm, hmask_d), (cw, cw_d), (cb, cb_d), (g, g_d), (b, b_d)):
        P.dma("sp", t_[:], d_[:])

    wup = [P.sb([128, 2 * DFF], BF16) for _ in range(8)]
    wdn = [P.sb([128, D], BF16) for _ in range(NFC)]
    stg = [P.sb([128, 1408], F32) for _ in range(2)]
    dst, src = [], []
    for kc in range(8):
        for j in range(4):
            dst.append(wup[kc][:, j * 1408:(j + 1) * 1408]); src.append(wup_d[:, kc, j * 1408:(j + 1) * 1408])
    for c in range(NFC):
        dst.append(wdn[c][:, :]); src.append(wdn_d[:, c, :])
    load_cast(P, dst, src, stg)

    xf = [P.sb([128, 8, TT], F32) for _ in range(2)]
    xb = [P.sb([128, 8, TT], BF16) for _ in range(2)]
    H = P.sb([128, NFC, TT], BF16)
    G = [P.sb([128, TT + 2], F32) for _ in range(2)]
    T = [P.sb([128, TT], F32) for _ in range(2)]
    sq = [P.sb([128, TT], F32) for _ in range(2)]
    stat = [P.sb([128, TT], F32) for _ in range(3)]
    carry = P.sb([128, NFC, 2], F32)
    xh = P.sb([128, 8, 2], F32); xhb = P.sb([128, 8, 2], BF16)
    pg = [P.ps([128, 512]) for _ in range(2)]
    pv = [P.ps([128, 512]) for _ in range(2)]
    pd = [P.ps([128, 512]) for _ in range(2)]
    pl = [P.ps([128, 512]) for _ in range(2)]

    NTL = SEGLEN // TT
    tiles = [(s, t) for s in range(NSEG) for t in range(NTL)]
    outs = []

    def load_tile(i):
        s, t = tiles[i]
        P.dma("sp", xf[i % 2][:], x_in[:, :, s, 2 + t * TT: 2 + (t + 1) * TT])
        P.copy(xb[i % 2][:], xf[i % 2][:], e="pool")

    load_tile(0)
    for i, (s, t) in enumerate(tiles):
        X, XB = xf[i % 2], xb[i % 2]
        if t == 0:
            P.dma("sp", xh[:], x_in[:, :, s, 0:2])
            P.copy(xhb[:], xh[:], e="pool")
            for c in range(NFC):
                pp = pg[c % 2]
                for kc in range(8):
                    P.mm(pp[:, 0:2], wup[kc][:, c * 128:(c + 1) * 128], xhb[:, kc, :], start=(kc == 0), stop=(kc == 7))
                P.ts(carry[:, c, :], pp[:, 0:2], hm[:, s:s + 1], ALU.mult)
        if i + 1 < len(tiles):
            load_tile(i + 1)
        for c in range(NFC):
            pgc, pvc = pg[c % 2], pv[c % 2]
            for kc in range(8):
                P.mm(pgc[:, 0:TT], wup[kc][:, c * 128:(c + 1) * 128], XB[:, kc, :], start=(kc == 0), stop=(kc == 7))
            for kc in range(8):
                P.mm(pvc[:, 0:TT], wup[kc][:, DFF + c * 128: DFF + (c + 1) * 128], XB[:, kc, :], start=(kc == 0), stop=(kc == 7))
            Gc, Tc = G[c % 2], T[c % 2]
            P.copy(Gc[:, 2:TT + 2], pgc[:, 0:TT], e="act")
            P.copy(Gc[:, 0:2], carry[:, c, :], e="pool")
            P.ts(Tc[:], Gc[:, 0:TT], cw[:, 0, c:c + 1], ALU.mult, cb[:, c:c + 1], ALU.add)
            P.stt(Tc[:], Gc[:, 1:TT + 1], cw[:, 1, c:c + 1], Tc[:], ALU.mult, ALU.add)
            P.stt(Tc[:], Gc[:, 2:TT + 2], cw[:, 2, c:c + 1], Tc[:], ALU.mult, ALU.add)
            P.copy(carry[:, c, :], Gc[:, TT:TT + 2], e="pool")
            P.act(Tc[:], Tc[:], AF.Silu)
            P.tt(H[:, c, :], Tc[:], pvc[:, 0:TT], ALU.mult)
        for oc in range(8):
            pp = pd[oc % 2]
            for c in range(NFC):
                P.mm(pp[:, 0:TT], wdn[c][:, oc * 128:(oc + 1) * 128], H[:, c, :], start=(c == 0), stop=(c == NFC - 1))
            P.stt(X[:, oc, :], X[:, oc, :], ALPHA, pp[:, 0:TT], ALU.mult, ALU.add)
        ln_inplace(P, X, TT, onesm, g, b, pl, sq, stat)
        outs.append(P.dma("sp", out_d[:, :, s * SEGLEN + t * TT: s * SEGLEN + (t + 1) * TT], X[:]))
    return P.finish(outs)


def build_gmlp(NTOK=4096):
    TT = 128
    P = Prog()
    x_in = P.dram("x", [128, 8, NTOK], F32, "ExternalInput")
    win_d = P.dram("w_in", [128, 8, 2048], F32, "ExternalInput")
    wout_d = P.dram("w_out", [128, 8, 1024], F32, "ExternalInput")
    wsT_d = P.dram("wsT", [128, 8, 128], F32, "ExternalInput")
    bu_d = P.dram("b_u", [128, 8], F32, "ExternalInput")
    bv_d = P.dram("b_v", [1, 1024], F32, "ExternalInput")
    lg_d = P.dram("cln_g", [1, 1024], F32, "ExternalInput")
    lb_d = P.dram("cln_b", [1, 1024], F32, "ExternalInput")
    bs_d = P.dram("b_s", [1, 1024], F32, "ExternalInput")
    g_d = P.dram("ln_g", [128, 8], F32, "ExternalInput")
    b_d = P.dram("ln_b", [128, 8], F32, "ExternalInput")
    out_d = P.dram("x1", [128, 8, NTOK], F32, "ExternalOutput")

    onesm = P.sb([128, 128], F32)
    P.memset(onesm[:], 1.0 / 1024.0)
    bu = P.sb([128, 8]); g = P.sb([128, 8]); b = P.sb([128, 8])
    bv = P.sb([128, 1024]); lg = P.sb([128, 1024]); lb = P.sb([128, 1024]); bs = P.sb([128, 1024])
    for t_, d_ in ((bu, bu_d), (g, g_d), (b, b_d)):
        P.dma("sp", t_[:], d_[:])
    for t_, d_ in ((bv, bv_d), (lg, lg_d), (lb, lb_d), (bs, bs_d)):
        P.dma("sp", t_[:], d_[0:1, :].to_broadcast([128, 1024]))
    mask = P.sb([128, 128], F32)
    P.memset(mask[:], 1.0, e="pool")
    P.emit("pool", lambda E: E.affine_select(mask.h[:], mask.h[:], [[1, 128]], ALU.is_ge, 0.0, base=0, channel_multiplier=-1),
           [mask[:]], [mask[:]])
    wsf = P.sb([128, 8, 128], F32)
    P.dma("sp", wsf[:], wsT_d[:])
    wsb = P.sb([128, 8, 128], BF16)
    for gi in range(8):
        P.tt(wsb[:, gi, :], wsf[:, gi, :], mask[:], ALU.mult)

    win = [P.sb([128, 2048], BF16) for _ in range(8)]
    wout = [P.sb([128, 1024], BF16) for _ in range(8)]
    stg = [P.sb([128, 2048], F32) for _ in range(2)]
    load_cast(P, [w[:, :] for w in win] + [w[:, :] for w in wout],
              [win_d[:, kc, :] for kc in range(8)] + [wout_d[:, kc, :] for kc in range(8)], stg)

    xf = [P.sb([128, 8, TT], F32) for _ in range(2)]
    xb = [P.sb([128, 8, TT], BF16) for _ in range(2)]
    U = P.sb([128, 8, TT], F32)
    V1 = P.sb([128, 1024], F32)
    junk = P.sb([128, 1024], F32)
    VN = P.sb([128, 1024], BF16)
    GU = P.sb([128, 8, TT], BF16)
    mx = [P.sb([128, TT], F32) for _ in range(2)]
    st = P.sb([128, 4], F32)
    sq = [P.sb([128, TT], F32) for _ in range(2)]
    stat = [P.sb([128, TT], F32) for _ in range(3)]
    pu = [P.ps([128, 512]) for _ in range(2)]
    pvv = [P.ps([128, 512]) for _ in range(2)]
    pm = [P.ps([128, 512]) for _ in range(2)]
    pl = [P.ps([128, 512]) for _ in range(2)]

    NTL = NTOK // TT
    outs = []

    def load_tile(i):
        P.dma("sp", xf[i % 2][:], x_in[:, :, i * TT:(i + 1) * TT])
        P.copy(xb[i % 2][:], xf[i % 2][:], e="pool")

    load_tile(0)
    for i in range(NTL):
        X, XB = xf[i % 2], xb[i % 2]
        if i + 1 < NTL:
            load_tile(i + 1)
        for oc in range(8):
            pp = pu[oc % 2]
            for kc in range(8):
                P.mm(pp[:, 0:TT], win[kc][:, oc * 128:(oc + 1) * 128], XB[:, kc, :], start=(kc == 0), stop=(kc == 7))
            P.act(U[:, oc, :], pp[:, 0:TT], AF.Gelu, bias=bu[:, oc:oc + 1])
        for hf in range(2):
            pp = pvv[hf]
            for kc in range(8):
                P.mm(pp[:, :], XB[:, kc, :], win[kc][:, 1024 + hf * 512: 1024 + (hf + 1) * 512], start=(kc == 0), stop=(kc == 7))
            P.tt(V1[:, hf * 512:(hf + 1) * 512], pp[:, :], bv[:, hf * 512:(hf + 1) * 512], ALU.add)
        P.act(V1[:], V1[:], AF.Gelu)
        P.reduce(st[:, 0:1], V1[:], ALU.add)
        P.ts(st[:, 1:2], st[:, 0:1], -1.0 / 1024.0, ALU.mult)
        P.ts(V1[:], V1[:], st[:, 1:2], ALU.add)
        P.stt(junk[:], V1[:], 1.0, V1[:], ALU.mult, ALU.mult, accum_out=st[:, 2:3])
        P.act(st[:, 3:4], st[:, 2:3], AF.Sqrt, bias=P.const(1e-5)[:, 0:1], scale=1.0 / 1024.0)
        P.recip(st[:, 3:4], st[:, 3:4])
        P.stt(V1[:], V1[:], st[:, 3:4], lg[:], ALU.mult, ALU.mult)
        P.tt(VN[:], V1[:], lb[:], ALU.add, e="pool")
        for gi in range(8):
            pp = pm[gi % 2]
            P.mm(pp[:, 0:TT], VN[:, gi * 128:(gi + 1) * 128], wsb[:, gi, :])
            m = mx[gi % 2]
            P.tt(m[:], pp[:, 0:TT], bs[:, gi * 128:(gi + 1) * 128], ALU.add)
            P.tt(GU[:, gi, :], m[:], U[:, gi, :], ALU.mult, e="pool")
        for oc in range(8):
            pp = pu[oc % 2]
            for c in range(8):
                P.mm(pp[:, 0:TT], wout[c][:, oc * 128:(oc + 1) * 128], GU[:, c, :], start=(c == 0), stop=(c == 7))
            P.stt(X[:, oc, :], X[:, oc, :], ALPHA, pp[:, 0:TT], ALU.mult, ALU.add)
        ln_inplace(P, X, TT, onesm, g, b, pl, sq, stat)
        outs.append(P.dma("sp", out_d[:, :, i * TT:(i + 1) * TT], X[:]))
    return P.finish(outs)


def build_abin(NTOK=4096, TT=256):
    P = Prog()
    x_in = P.dram("x", [128, 8, NTOK + 1], F32, "ExternalInput")
    win_d = P.dram("w_in", [128, 8, 3328], F32, "ExternalInput")
    pp_d = P.dram("pp", [128, 34], F32, "ExternalInput")
    w2_d = P.dram("w2p", [128, 512], F32, "ExternalInput")
    a2_d = P.dram("a2p", [128, 512], F32, "ExternalInput")
    g2_d = P.dram("g2", [128, 512], F32, "ExternalInput")
    hblk_d = P.dram("hblk", [128, 128], F32, "ExternalInput")
    rm_d = P.dram("rotm", [128, 128], F32, "ExternalInput")
    cq_d = P.dram("cosq", [128, NTOK], F32, "ExternalInput")
    sq_d = P.dram("sinq", [128, NTOK], F32, "ExternalInput")
    ck_d = P.dram("cosk", [128, NTOK], F32, "ExternalInput")
    sk_d = P.dram("sink", [128, NTOK], F32, "ExternalInput")
    names = ["r", "w", "k", "v", "av", "b", "g", "bonus", "dq", "dk", "dv"]
    od = {n: P.dram("o_" + n, [128, 4, NTOK], F32, "ExternalOutput") for n in names}

    pp = P.sb([128, 34]); w2 = P.sb([128, 512]); a2 = P.sb([128, 512]); g2 = P.sb([128, 512])
    hblk = P.sb([128, 128]); rotm = P.sb([128, 128])
    for t_, d_ in ((pp, pp_d), (w2, w2_d), (a2, a2_d), (g2, g2_d), (hblk, hblk_d), (rotm, rm_d)):
        P.dma("sp", t_[:], d_[:])
    MU, W0, A0, KK_, KA, RK = 0, 14, 18, 22, 26, 30
    win = [P.sb([128, 3328], BF16) for _ in range(8)]
    stg = [P.sb([128, 1664], F32) for _ in range(2)]
    dst, src = [], []
    for kc in range(8):
        for j in range(2):
            dst.append(win[kc][:, j * 1664:(j + 1) * 1664]); src.append(win_d[:, kc, j * 1664:(j + 1) * 1664])
    load_cast(P, dst, src, stg)

    xf = [P.sb([128, 8, TT + 1], F32) for _ in range(2)]
    xb = [P.sb([128, 8, TT + 1], BF16) for _ in range(2)]
    prs = [P.sb([128, TT + 1], F32) for _ in range(2)]
    dd = [P.sb([128, TT], F32) for _ in range(2)]
    XS = P.sb([128, 14, TT], F32)
    TW = P.sb([128, TT], F32); SG = P.sb([128, TT], F32)
    O = {n: [P.sb([128, 4, TT], F32) for _ in range(1)] for n in names}
    tmp = [P.sb([128, TT], F32) for _ in range(4)]
    tabs = [P.sb([128, 4, TT], F32) for _ in range(2)]
    ps = [P.ps([128, 512]) for _ in range(8)]
    pi = [0]

    def nps():
        pi[0] += 1
        return ps[pi[0] % 8]

    NTL = NTOK // TT
    outs = []

    def load_tile(i):
        P.dma("sp", xf[i % 2][:], x_in[:, :, i * TT: i * TT + TT + 1])
        P.copy(xb[i % 2][:], xf[i % 2][:], e="pool")
        for j, d_ in enumerate((cq_d, sq_d, ck_d, sk_d)):
            P.dma("sp", tabs[i % 2][:, j, :], d_[:, i * TT:(i + 1) * TT])

    load_tile(0)
    for i in range(NTL):
        XB = xb[i % 2]
        TB = tabs[i % 2]
        if i + 1 < NTL:
            load_tile(i + 1)
        ob = {n: O[n][0] for n in names}
        for c in range(14):
            p_ = nps()
            for kc in range(8):
                P.mm(p_[:, 0:TT + 1], win[kc][:, c * 128:(c + 1) * 128], XB[:, kc, :], start=(kc == 0), stop=(kc == 7))
            s_ = prs[c % 2]; d_ = dd[c % 2]
            P.copy(s_[:], p_[:, 0:TT + 1], e="act")
            P.tt(d_[:], s_[:, 0:TT], s_[:, 1:TT + 1], ALU.subtract)
            if c < 4:
                dst_ = ob["r"][:, c, :]
            elif 8 <= c < 12:
                dst_ = ob["v"][:, c - 8, :]
            else:
                dst_ = XS[:, c, :]
            P.stt(dst_, d_[:], pp[:, MU + c:MU + c + 1], s_[:, 1:TT + 1], ALU.mult, ALU.add)
        P.act(TW[:], XS[:, 12, :], AF.Tanh)
        P.act(SG[:], XS[:, 13, :], AF.Sigmoid)
        for j in range(4):
            p_ = nps()
            P.mm(p_[:, 0:TT], w2[:, j * 128:(j + 1) * 128], TW[:])
            t_ = tmp[0]
            P.act(t_[:], p_[:, 0:TT], AF.Sigmoid, bias=pp[:, W0 + j:W0 + j + 1])
            P.ts(ob["w"][:, j, :], t_[:], -0.6065306597126334, ALU.mult, e="pool")
            p_ = nps()
            P.mm(p_[:, 0:TT], a2[:, j * 128:(j + 1) * 128], XS[:, 12, :])
            A_ = tmp[1]
            P.act(A_[:], p_[:, 0:TT], AF.Sigmoid, bias=pp[:, A0 + j:A0 + j + 1])
            p_ = nps()
            P.mm(p_[:, 0:TT], g2[:, j * 128:(j + 1) * 128], SG[:])
            P.copy(ob["g"][:, j, :], p_[:, 0:TT], e="act")
            kraw = XS[:, 4 + j, :]
            KKt = tmp[2]
            P.ts(KKt[:], kraw, pp[:, KK_ + j:KK_ + j + 1], ALU.mult)
            s2 = tmp[3]
            P.tt(s2[:], KKt[:], KKt[:], ALU.mult, e="pool")
            p_ = nps()
            P.mm(p_[:, 0:TT], hblk[:], s2[:])
            P.act(s2[:], p_[:, 0:TT], AF.Sqrt)
            P.ts(s2[:], s2[:], 1e-12, ALU.max)
            P.recip(s2[:], s2[:])
            P.tt(KKt[:], KKt[:], s2[:], ALU.mult)
            P.ts(ob["av"][:, j, :], KKt[:], -1.0, ALU.mult, e="pool")
            P.tt(ob["b"][:, j, :], KKt[:], A_[:], ALU.mult)
            P.ts(A_[:], A_[:], -1.0, ALU.add, pp[:, KA + j:KA + j + 1], ALU.mult)
            P.stt(ob["k"][:, j, :], A_[:], 1.0, kraw, ALU.add, ALU.mult)
            P.stt(s2[:], ob["r"][:, j, :], pp[:, RK + j:RK + j + 1], ob["k"][:, j, :], ALU.mult, ALU.mult)
            p_ = nps()
            P.mm(p_[:, 0:TT], hblk[:], s2[:])
            P.tt(ob["bonus"][:, j, :], p_[:, 0:TT], ob["v"][:, j, :], ALU.mult)
        for c in range(12):
            p_ = nps()
            col = 1792 + c * 128
            for kc in range(8):
                P.mm(p_[:, 0:TT], win[kc][:, col:col + 128], XB[:, kc, 1:TT + 1], start=(kc == 0), stop=(kc == 7))
            if c >= 8:
                P.copy(ob["dv"][:, c - 8, :], p_[:, 0:TT], e="act")
                continue
            isq = c < 4
            j = c % 4
            qs = tmp[c % 2]
            P.copy(qs[:], p_[:, 0:TT], e="act")
            p2 = nps()
            P.mm(p2[:, 0:TT], rotm[:], qs[:])
            t2 = tmp[2 + c % 2]
            P.tt(t2[:], p2[:, 0:TT], TB[:, 1 if isq else 3, :], ALU.mult)
            P.tt(qs[:], qs[:], TB[:, 0 if isq else 2, :], ALU.mult, e="pool")
            P.tt((ob["dq"] if isq else ob["dk"])[:, j, :], qs[:], t2[:], ALU.add)
        for n in names:
            outs.append(P.dma("sp", od[n][:, :, i * TT:(i + 1) * TT], ob[n][:]))
    return P.finish(outs)


def build_scan(NH=2, NCH=256):
    L = 64
    P = Prog()
    fmp_d = P.dram("fmp", [NH, NCH, 64, 256], F32, "ExternalInput")
    tmp_d = P.dram("tmp", [NH, NCH, 64, 256], F32, "ExternalInput")
    cd = {n: P.dram(n, s, F32, "ExternalInput") for n, s in
          (("TRI2", [64, 128]), ("TGT", [64, 64]), ("MASK2", [64, 128]), ("MASKT", [64, 64]), ("IDENT", [64, 64]))}
    y_d = P.dram("y", [NH, 64, NCH * L], F32, "ExternalOutput")
    C = {}
    for n, d_ in cd.items():
        C[n] = P.sb(list(d_.h.shape), F32)
        P.dma("sp", C[n][:], d_[:])
    NS = 2
    def mk():
        return dict(FM=P.sb([64, 256]), TM=P.sb([64, 256]), E12=P.sb([64, 128]), E3=P.sb([64, 64]), E4=P.sb([64, 64]),
                    AR=P.sb([64, 128]), BK=P.sb([64, 128]), BKT=P.sb([64, 128]), AB=P.sb([64, 128]), AK=P.sb([64, 128]),
                    X=P.sb([64, 64]), XT=P.sb([64, 64]), Tm=P.sb([64, 64]), Psb=P.sb([64, 64]), U=P.sb([64, 64]))
    sets = [[mk() for _ in range(NS)] for _ in range(NH)]
    ST = [P.sb([64, 64]) for _ in range(NH)]
    for h in range(NH):
        P.memset(ST[h][:], 0.0)
    YB = [[P.sb([64, 512]) for _ in range(2)] for _ in range(NH)]
    banks = [P.ps([128, 512]) for _ in range(8)]
    pi = [0]

    def nps():
        pi[0] += 1
        return banks[pi[0] % 8]

    def load(h, c):
        s = sets[h][c % NS]
        P.dma("sp", s["FM"][:], fmp_d[h, c])
        P.dma("sp", s["TM"][:], tmp_d[h, c])

    def prep(h, c):
        s = sets[h][c % NS]
        FM, TM = s["FM"], s["TM"]
        lw = TM[:, 128:192]
        cps = nps(); sfx = nps()
        P.mm(cps[0:64, 0:128], lw, C["TRI2"][:])
        P.mm(sfx[0:64, 0:64], C["TGT"][:], lw)
        P.act(s["E12"][:], cps[0:64, 0:128], AF.Exp)
        P.act(s["E3"][:], cps[0:64, 0:64], AF.Exp, scale=-1.0)
        P.act(s["E4"][:], sfx[0:64, 0:64], AF.Exp)
        P.tt(s["AR"][:, 0:64], FM[:, 128:192], s["E12"][:, 64:128], ALU.mult, e="pool")
        P.tt(s["AR"][:, 64:128], FM[:, 0:64], s["E12"][:, 0:64], ALU.mult, e="pool")
        P.tt(s["BK"][:, 0:64], FM[:, 192:256], s["E3"][:], ALU.mult, e="pool")
        P.tt(s["BK"][:, 64:128], FM[:, 64:128], s["E3"][:], ALU.mult, e="pool")
        P.tt(s["BKT"][:, 0:64], TM[:, 0:64], s["E4"][:], ALU.mult, e="pool")
        P.tt(s["BKT"][:, 64:128], TM[:, 64:128], s["E4"][:], ALU.mult, e="pool")
        pb = nps(); pk = nps(); pn = nps()
        P.mm(pb[0:64, 0:128], s["BK"][:, 0:64], s["AR"][:])
        P.mm(pk[0:64, 0:128], s["BK"][:, 64:128], s["AR"][:])
        P.mm(pn[0:64, 0:64], s["AR"][:, 0:64], s["BK"][:, 0:64])
        P.tt(s["AB"][:], pb[0:64, 0:128], C["MASK2"][:], ALU.mult)
        P.tt(s["AK"][:], pk[0:64, 0:128], C["MASK2"][:], ALU.mult)
        P.tt(s["XT"][:], pn[0:64, 0:64], C["MASKT"][:], ALU.mult)
        P.copy(s["X"][:], s["AB"][:, 0:64], e="pool")
        P.tt(s["Tm"][:], s["AB"][:, 0:64], C["IDENT"][:], ALU.add, e="pool")
        for j in range(1, 6):
            pxt = nps()
            P.mm(pxt[0:64, 0:64], s["X"][:], s["XT"][:])
            if j < 5:
                px = nps()
                P.mm(px[0:64, 0:64], s["XT"][:], s["X"][:])
                P.copy(s["X"][:], px[0:64, 0:64], e="act")
            P.copy(s["XT"][:], pxt[0:64, 0:64])
            pt = nps()
            P.mm(pt[0:64, 0:64], s["XT"][:], s["Tm"][:])
            P.tt(s["Tm"][:], pt[0:64, 0:64], s["Tm"][:], ALU.add)

    outs = []

    def rec(h, c):
        s = sets[h][c % NS]
        V_ = s["TM"][:, 192:256]
        pp = nps()
        P.mm(pp[0:64, 0:64], s["AR"][:, 0:64], ST[h][:], start=True, stop=False)
        P.mm(pp[0:64, 0:64], s["AK"][:, 0:64], V_, start=False, stop=True)
        P.copy(s["Psb"][:], pp[0:64, 0:64], e="act")
        pu = nps()
        P.mm(pu[0:64, 0:64], s["Tm"][:], s["Psb"][:])
        P.copy(s["U"][:], pu[0:64, 0:64], e="act")
        py = nps()
        P.mm(py[0:64, 0:64], ST[h][:], s["AR"][:, 64:128], start=True, stop=False)
        P.mm(py[0:64, 0:64], s["U"][:], s["AB"][:, 64:128], start=False, stop=False)
        P.mm(py[0:64, 0:64], V_, s["AK"][:, 64:128], start=False, stop=True)
        pd_ = nps()
        P.mm(pd_[0:64, 0:64], s["BKT"][:, 0:64], s["U"][:], start=True, stop=False)
        P.mm(pd_[0:64, 0:64], s["BKT"][:, 64:128], V_, start=False, stop=True)
        P.stt(ST[h][:], ST[h][:], s["E12"][:, 63:64], pd_[0:64, 0:64], ALU.mult, ALU.add)
        yb = YB[h][(c // 8) % 2]
        P.copy(yb[:, (c % 8) * 64:(c % 8 + 1) * 64], py[0:64, 0:64], e="act")
        if c % 8 == 7 or c == NCH - 1:
            c0 = (c // 8) * 8
            outs.append(P.dma("sp", y_d[h, :, c0 * L:(c + 1) * L], yb[:, 0:(c - c0 + 1) * L]))

    for h in range(NH):
        load(h, 0)
    for h in range(NH):
        prep(h, 0)
    for c in range(NCH):
        for h in range(NH):
            if c + 1 < NCH:
                load(h, c + 1)
                prep(h, c + 1)
            rec(h, c)
    return P.finish(outs)


def build_attn(T=16384):
    NB = T // 128
    NG = T // 512
    P = Prog()
    q_d = P.dram("qT", [64, 2, T], F32, "ExternalInput")
    k_d = P.dram("kT", [64, 2, T], F32, "ExternalInput")
    v_d = P.dram("v", [128, NB, 128], F32, "ExternalInput")
    lam_d = P.dram("lamv", [1, 256], F32, "ExternalInput")
    li_d = P.dram("lam_init", [1, 1], F32, "ExternalInput")
    sg_d = P.dram("subln_g", [1, 128], F32, "ExternalInput")
    tri_d = P.dram("tri", [128, 128], F32, "ExternalInput")
    sel_d = P.dram("sel", [64, 65], F32, "ExternalInput")
    o_d = P.dram("o", [NB, 128, 128], F32, "ExternalOutput")

    lamv = P.sb([128, 256]); li = P.sb([128, 1]); gsc = P.sb([128, 128]); trif = P.sb([128, 128]); sel = P.sb([64, 65])
    P.dma("sp", lamv[:], lam_d[0:1, :].to_broadcast([128, 256]))
    P.dma("sp", li[:], li_d[0:1, :].to_broadcast([128, 1]))
    P.dma("sp", gsc[:], sg_d[0:1, :].to_broadcast([128, 128]))
    P.dma("sp", trif[:], tri_d[:])
    P.dma("sp", sel[:], sel_d[:])
    tri = P.sb([128, 128], BF16)
    P.copy(tri[:], trif[:])
    sc = P.sb([128, 8]); junk = P.sb([128, 128])
    P.stt(junk[:, 0:64], lamv[:, 0:64], 1.0, lamv[:, 64:128], ALU.mult, ALU.mult, accum_out=sc[:, 0:1])
    P.stt(junk[:, 0:64], lamv[:, 128:192], 1.0, lamv[:, 192:256], ALU.mult, ALU.mult, accum_out=sc[:, 1:2])
    P.act(sc[:, 2:4], sc[:, 0:2], AF.Exp)
    P.tt(sc[:, 4:5], sc[:, 2:3], sc[:, 3:4], ALU.subtract)
    P.tt(sc[:, 4:5], sc[:, 4:5], li[:], ALU.add)
    P.ts(sc[:, 5:6], sc[:, 4:5], -1.0, ALU.mult)
    P.ts(sc[:, 6:7], li[:], -1.0, ALU.mult, 1.0, ALU.add)
    P.ts(gsc[:], gsc[:], sc[:, 6:7], ALU.mult)
    nlam = sc[:, 5:6]

    Ka = [P.sb([65, T], BF16) for _ in range(2)]
    Va = P.sb([128, NB, 129], BF16)
    for c in range(2):
        P.memset(Ka[c][64:65, :], 1.0, e="pool")
    P.memset(Va[:, :, 128:129], 1.0, e="pool")
    kst = P.sb([65, 8])
    P.memset(kst[64:65, :], 0.0)
    stg = [P.sb([128, 2048], F32) for _ in range(2)]
    sqb = [P.sb([64, 1024], F32) for _ in range(2)]
    pss = [P.ps([128, 512]) for _ in range(3)]
    pso = [P.ps([128, 512]) for _ in range(4)]
    pmisc = P.ps([128, 512])
    for i in range(NG):
        st = stg[i % 2]
        kf = st[0:64, 0:1024].rearrange("p (c t) -> p c t", c=2)
        P.dma("sp", kf, k_d[:, :, i * 512:(i + 1) * 512])
        s2 = sqb[i % 2]
        P.tt(s2[:], st[0:64, 0:1024], st[0:64, 0:1024], ALU.mult, e="pool")
        for c in range(2):
            P.copy(Ka[c][0:64, i * 512:(i + 1) * 512], st[0:64, c * 512:(c + 1) * 512], e="act")
            P.mm(pmisc[0:65, 0:512], sel[:], s2[:, c * 512:(c + 1) * 512])
            P.reduce(kst[64:65, 2 + c:3 + c], pmisc[64:65, 0:512], ALU.max)
            P.tt(kst[64:65, c:c + 1], kst[64:65, c:c + 1], kst[64:65, 2 + c:3 + c], ALU.max)
    VP = min(16, NB)
    for i in range(NB // VP):
        st = stg[i % 2]
        P.dma("sp", st[:, 0:VP * 128].rearrange("p (n d) -> p n d", d=128), v_d[:, i * VP:(i + 1) * VP, :])
        P.copy(Va[:, i * VP:(i + 1) * VP, 0:128], st[:, 0:VP * 128].rearrange("p (n d) -> p n d", d=128), e=("dve" if i % 2 else "pool"))
    P.act(kst[64:65, 4:6], kst[64:65, 0:2], AF.Sqrt)
    P.ts(kst[64:65, 6:8], kst[64:65, 4:6], -1.0, ALU.mult)

    qf = [P.sb([64, 2, 512], F32) for _ in range(2)]
    qsq = P.sb([64, 2, 512], F32)
    nq = P.sb([65, 512], F32)
    Qa = [[P.sb([65, 512], BF16) for _ in range(2)] for _ in range(2)]
    PT = [P.sb([128, 512], BF16) for _ in range(3)]
    res = [P.sb([128, 128], F32) for _ in range(4)]
    ob = [P.sb([128, 128], F32) for _ in range(2)]
    st2 = P.sb([128, 8], F32)
    outs = []
    P.dma("sp", qf[0][:], q_d[:, :, 0:512])
    cnt = 0
    for qg in range(NG):
        Qf = qf[qg % 2]
        if qg + 1 < NG:
            P.dma("sp", qf[(qg + 1) % 2][:], q_d[:, :, (qg + 1) * 512:(qg + 2) * 512])
        P.tt(qsq[:], Qf[:], Qf[:], ALU.mult, e="pool")
        Q = Qa[qg % 2]
        for c in range(2):
            P.copy(Q[c][0:64, :], Qf[:, c, :], e="pool")
            P.mm(pmisc[0:65, 0:512], sel[:], qsq[:, c, :])
            P.act(nq[64:65, :], pmisc[64:65, 0:512], AF.Sqrt)
            P.ts(Q[c][64:65, :], nq[64:65, :], kst[64:65, 6 + c:7 + c], ALU.mult)
        for c in range(2):
            nkb = (qg + 1) * 4
            for kb in range(nkb):
                j = kb - qg * 4
                q0 = max(j, 0) * 128
                ps_ = pss[cnt % 3]; pt = PT[cnt % 3]; cnt += 1
                P.mm(ps_[:, q0:512], Ka[c][:, kb * 128:(kb + 1) * 128], Q[c][:, q0:512])
                P.act(pt[:, q0:512], ps_[:, q0:512], AF.Exp)
                if j >= 0:
                    P.tt(pt[:, q0:q0 + 128], pt[:, q0:q0 + 128], tri[:], ALU.mult, e="dve")
                for qb in range(max(j, 0), 4):
                    P.mm(pso[qb][:, 0:129], pt[:, qb * 128:(qb + 1) * 128], Va[:, kb, :],
                         start=(kb == 0), stop=(kb == qg * 4 + qb))
            for qb in range(4):
                oo = pso[qb]
                if c == 0:
                    P.recip(st2[:, qb:qb + 1], oo[:, 128:129])
                    P.ts(res[qb][:], oo[:, 0:128], st2[:, qb:qb + 1], ALU.mult)
                else:
                    a_ = ob[qb % 2]
                    P.recip(st2[:, 4:5], oo[:, 128:129])
                    P.tt(st2[:, 5:6], st2[:, 4:5], nlam, ALU.mult)
                    P.stt(a_[:], oo[:, 0:128], st2[:, 5:6], res[qb][:], ALU.mult, ALU.add)
                    P.stt(junk[:], a_[:], 1.0, a_[:], ALU.mult, ALU.mult, accum_out=st2[:, 6:7])
                    P.act(st2[:, 7:8], st2[:, 6:7], AF.Sqrt, bias=P.const(1e-5)[:, 0:1], scale=1.0 / 128.0)
                    P.recip(st2[:, 7:8], st2[:, 7:8])
                    P.stt(a_[:], a_[:], st2[:, 7:8], gsc[:], ALU.mult, ALU.mult)
                    outs.append(P.dma("sp", o_d[qg * 4 + qb], a_[:]))
    return P.finish(outs)


def build_about(NTOK=4096, TT=256):
    P = Prog()
    x_in = P.dram("x", [128, 8, NTOK], F32, "ExternalInput")
    y_d = P.dram("y", [128, 4, NTOK], F32, "ExternalInput")
    bo_d = P.dram("bonus", [128, 4, NTOK], F32, "ExternalInput")
    g_d = P.dram("g", [128, 4, NTOK], F32, "ExternalInput")
    od_d = P.dram("od", [128, 4, NTOK], F32, "ExternalInput")
    wout_d = P.dram("w_out", [128, 8, 1024], F32, "ExternalInput")
    pp_d = P.dram("pp", [128, 8], F32, "ExternalInput")
    hb_d = P.dram("hblk64", [128, 128], F32, "ExternalInput")
    lg_d = P.dram("ln_g", [128, 8], F32, "ExternalInput")
    lb_d = P.dram("ln_b", [128, 8], F32, "ExternalInput")
    out_d = P.dram("x1", [128, 8, NTOK], F32, "ExternalOutput")

    onesm = P.sb([128, 128], F32)
    P.memset(onesm[:], 1.0 / 1024.0)
    pp = P.sb([128, 8]); hb = P.sb([128, 128]); lg = P.sb([128, 8]); lb = P.sb([128, 8])
    for t_, d_ in ((pp, pp_d), (hb, hb_d), (lg, lg_d), (lb, lb_d)):
        P.dma("sp", t_[:], d_[:])
    wout = [P.sb([128, 1024], BF16) for _ in range(8)]
    stg = [P.sb([128, 1024], F32) for _ in range(2)]
    load_cast(P, [w[:, :] for w in wout], [wout_d[:, c, :] for c in range(8)], stg)

    xf = [P.sb([128, 8, TT], F32) for _ in range(2)]
    yf = [P.sb([128, 4, TT], F32) for _ in range(2)]
    bf = [P.sb([128, 4, TT], F32) for _ in range(2)]
    gf = [P.sb([128, 4, TT], F32) for _ in range(2)]
    of = [P.sb([128, 4, TT], F32) for _ in range(2)]
    MIX = P.sb([128, 8, TT], BF16)
    dt_ = [P.sb([128, TT], F32) for _ in range(2)]
    s2 = [P.sb([128, TT], F32) for _ in range(2)]
    rs = [P.sb([128, TT], F32) for _ in range(2)]
    sq = [P.sb([128, TT], F32) for _ in range(2)]
    stat = [P.sb([128, TT], F32) for _ in range(3)]
    pm = [P.ps([128, 512]) for _ in range(2)]
    pvr = [P.ps([128, 512]) for _ in range(2)]
    po = [P.ps([128, 512]) for _ in range(2)]
    pl = [P.ps([128, 512]) for _ in range(2)]
    NTL = NTOK // TT
    outs = []

    def load_tile(i):
        sl = slice(i * TT, (i + 1) * TT)
        P.dma("sp", xf[i % 2][:], x_in[:, :, sl])
        P.dma("sp", yf[i % 2][:], y_d[:, :, sl])
        P.dma("sp", bf[i % 2][:], bo_d[:, :, sl])
        P.dma("sp", gf[i % 2][:], g_d[:, :, sl])
        P.dma("sp", of[i % 2][:], od_d[:, :, sl])

    load_tile(0)
    for i in range(NTL):
        X, Y, B, G, O_ = xf[i % 2], yf[i % 2], bf[i % 2], gf[i % 2], of[i % 2]
        if i + 1 < NTL:
            load_tile(i + 1)
        for j in range(4):
            d_ = dt_[j % 2]; q_ = s2[j % 2]; r_ = rs[j % 2]
            P.mm(pm[j % 2][:, 0:TT], hb[:], Y[:, j, :])
            P.tt(d_[:], Y[:, j, :], pm[j % 2][:, 0:TT], ALU.subtract)
            P.tt(q_[:], d_[:], d_[:], ALU.mult, e="pool")
            P.mm(pvr[j % 2][:, 0:TT], hb[:], q_[:])
            P.act(r_[:], pvr[j % 2][:, 0:TT], AF.Sqrt, bias=P.const(64e-5)[:, 0:1])
            P.recip(r_[:], r_[:])
            P.tt(d_[:], d_[:], r_[:], ALU.mult)
            P.act(d_[:], d_[:], AF.Identity, bias=pp[:, 4 + j:5 + j], scale=pp[:, j:j + 1])
            P.tt(d_[:], d_[:], B[:, j, :], ALU.add, e="pool")
            P.tt(MIX[:, j, :], d_[:], G[:, j, :], ALU.mult)
            P.copy(MIX[:, 4 + j, :], O_[:, j, :], e="pool")
        for oc in range(8):
            p_ = po[oc % 2]
            for c in range(8):
                P.mm(p_[:, 0:TT], wout[c][:, oc * 128:(oc + 1) * 128], MIX[:, c, :], start=(c == 0), stop=(c == 7))
            P.stt(X[:, oc, :], X[:, oc, :], ALPHA, p_[:, 0:TT], ALU.mult, ALU.add)
        ln_inplace(P, X, TT, onesm, lg, lb, pl, sq, stat)
        outs.append(P.dma("sp", out_d[:, :, i * TT:(i + 1) * TT], X[:]))
    return P.finish(outs)


NTOK = 4096
SEQ = 16384
_PROGS = {}


def _prog(name, fn):
    if name not in _PROGS:
        _PROGS[name] = fn()
    return _PROGS[name]


def _run(nc, in_maps):
    return run_bass_kernel_spmd(nc, in_maps, core_ids=list(range(8))).results


def _fm(a):
    T, C = a.shape
    return np.ascontiguousarray(a.T.reshape(C // 128, 128, T).transpose(1, 0, 2))


def _col(v, n):
    return np.ascontiguousarray(np.asarray(v, np.float32).reshape(n, 128).T)


def _wl(w):
    K, N = w.shape
    return np.ascontiguousarray(w.reshape(K // 128, 128, N).transpose(1, 0, 2))


def _rope_tabs(pos):
    inv = (500000.0 ** (-np.arange(0, 16, 2, dtype=np.float32) / 16)).astype(np.float32)
    ang = pos.astype(np.float32)[:, None] * inv[None, :]
    c, s = np.cos(ang).astype(np.float32), np.sin(ang).astype(np.float32)
    C = np.ones((128, len(pos)), np.float32)
    S = np.zeros((128, len(pos)), np.float32)
    for comp in range(2):
        for half in range(2):
            lo = comp * 64 + half * 8
            C[lo:lo + 8] = c.T
            S[lo:lo + 8] = s.T
    return C * np.float32(0.125), S * np.float32(0.125), C, S


def _rotm():
    R = np.zeros((128, 128), np.float32)
    for comp in range(2):
        for p in range(8):
            R[comp * 64 + p + 8, comp * 64 + p] = -1.0
            R[comp * 64 + p, comp * 64 + p + 8] = 1.0
    return R


def _scan_consts(L=64):
    i = np.arange(L)[:, None]
    t = np.arange(L)[None, :]
    incl = (i <= t).astype(np.float32)
    strict = (i < t).astype(np.float32)
    gt = (i > t).astype(np.float32)
    return {"TRI2": np.concatenate([incl, strict], 1), "TGT": gt, "MASK2": np.concatenate([strict, incl], 1),
            "MASKT": np.ascontiguousarray(strict.T), "IDENT": np.eye(L, dtype=np.float32)}


def _ab_layer(xs, inp, i, j):
    f32 = np.float32
    T = SEQ
    w_in = inp["ab_w_in"][j]
    pp = np.concatenate([_col(inp["ab_shift_mu"][j], 14), _col(inp["ab_w0"][j], 4), _col(inp["ab_a0"][j], 4),
                         _col(inp["ab_k_k"][j], 4), _col(inp["ab_k_a"][j], 4), _col(inp["ab_r_k"][j].reshape(-1), 4)], axis=1)
    w2p = np.zeros((128, 512), f32); w2p[:64] = inp["ab_w2"][j]
    a2p = np.zeros((128, 512), f32); a2p[64:] = inp["ab_a2"][j]
    hb = np.zeros((128, 128), f32); hb[:64, :64] = 1; hb[64:, 64:] = 1
    wi = {"w_in": _wl(w_in), "pp": np.ascontiguousarray(pp), "w2p": w2p, "a2p": a2p,
          "g2": np.ascontiguousarray(inp["ab_g2"][j]), "hblk": hb, "rotm": _rotm()}
    in_maps = []
    for c in range(8):
        b, q = divmod(c, 4)
        halo = xs[c - 1][:, :, -1:] if q > 0 else np.zeros((128, 8, 1), f32)
        cq, sq, ck, sk = _rope_tabs(np.arange(NTOK) + q * NTOK)
        in_maps.append({"x": np.ascontiguousarray(np.concatenate([halo, xs[c]], axis=2)),
                        "cosq": cq, "sinq": sq, "cosk": ck, "sink": sk, **wi})
    resA = _run(_prog("abin", lambda: build_abin(NTOK)), in_maps)
    sc = _scan_consts()
    in_maps = []
    for c in range(8):
        b, hp = divmod(c, 4)
        def gat(name):
            return np.concatenate([resA[4 * b + q]["o_" + name][:, hp, :] for q in range(4)], axis=1).reshape(2, 64, T // 64, 64)
        r, k, av, bb, lw, v = (gat(n) for n in ("r", "k", "av", "b", "w", "v"))
        fm_ = lambda X: X.transpose(0, 2, 1, 3)
        tm_ = lambda X: X.transpose(0, 2, 3, 1)
        fmp = np.ascontiguousarray(np.concatenate([fm_(r), fm_(k), fm_(av), fm_(bb)], axis=3))
        tmp = np.ascontiguousarray(np.concatenate([tm_(bb), tm_(k), tm_(lw), tm_(v)], axis=3))
        in_maps.append({"fmp": fmp, "tmp": tmp, **sc})
    resS = _run(_prog("scan", lambda: build_scan(2, T // 64)), in_maps)
    kk_ = np.arange(128)[:, None]; qq_ = np.arange(128)[None, :]
    sel = np.zeros((64, 65), f32); sel[:, 64] = 1
    lam_init = 0.8 - 0.6 * math.exp(-0.3 * i)
    ac = {"tri": (kk_ <= qq_).astype(f32), "sel": sel,
          "lamv": np.concatenate([inp["ab_lam_q1"][j], inp["ab_lam_k1"][j], inp["ab_lam_q2"][j], inp["ab_lam_k2"][j]]).reshape(1, 256).astype(f32),
          "lam_init": np.full((1, 1), lam_init, f32), "subln_g": np.ascontiguousarray(inp["ab_subln_g"][j].reshape(1, 128))}
    in_maps = []
    for c in range(8):
        b, h = divmod(c, 4)
        def gat(name):
            return np.concatenate([resA[4 * b + q]["o_" + name][:, h, :] for q in range(4)], axis=1)
        qT = np.ascontiguousarray(gat("dq").reshape(2, 64, T).transpose(1, 0, 2))
        kT = np.ascontiguousarray(gat("dk").reshape(2, 64, T).transpose(1, 0, 2))
        v = np.ascontiguousarray(gat("dv").T.reshape(T // 128, 128, 128).transpose(1, 0, 2))
        in_maps.append({"qT": qT, "kT": kT, "v": v, **ac})
    resT = _run(_prog("attn", lambda: build_attn(T)), in_maps)
    hb64 = hb / f32(64.0)
    wo = {"w_out": _wl(inp["ab_w_out"][j]),
          "pp": np.ascontiguousarray(np.concatenate([_col(inp["ab_lnx_g"][j], 4), _col(inp["ab_lnx_b"][j], 4)], 1)),
          "hblk64": hb64, "ln_g": _col(inp["ln1_g"][i], 8), "ln_b": _col(inp["ln1_b"][i], 8)}
    in_maps = []
    for c in range(8):
        b, q = divmod(c, 4)
        sl = slice(q * NTOK, (q + 1) * NTOK)
        y = np.ascontiguousarray(np.stack([resS[4 * b + hp]["y"].reshape(128, T)[:, sl] for hp in range(4)], axis=1))
        od = np.ascontiguousarray(np.stack([resT[4 * b + h]["o"].reshape(T, 128)[sl].T for h in range(4)], axis=1))
        in_maps.append({"x": xs[c], "y": y, "bonus": resA[c]["o_bonus"], "g": resA[c]["o_g"], "od": od, **wo})
    resB = _run(_prog("about", lambda: build_about(NTOK)), in_maps)
    return [r["x1"] for r in resB]


def _c_layer(xs, inp, i, j):
    b_in = inp["c_b_in"][j]
    wi = {"w_in": _wl(inp["c_w_in"][j]), "w_out": _wl(inp["c_w_out"][j]),
          "wsT": np.ascontiguousarray(inp["c_w_s"][j].transpose(2, 0, 1)),
          "b_u": _col(b_in[:1024], 8), "b_v": np.ascontiguousarray(b_in[1024:].reshape(1, 1024)),
          "cln_g": np.ascontiguousarray(inp["c_ln_g"][j].reshape(1, 1024)),
          "cln_b": np.ascontiguousarray(inp["c_ln_b"][j].reshape(1, 1024)),
          "b_s": np.ascontiguousarray(inp["c_b_s"][j].reshape(1, 1024)),
          "ln_g": _col(inp["ln1_g"][i], 8), "ln_b": _col(inp["ln1_b"][i], 8)}
    res = _run(_prog("gmlp", lambda: build_gmlp(NTOK)), [{"x": xs[c], **wi} for c in range(8)])
    return [r["x1"] for r in res]


def _ffn_layer(xs, inp, i):
    f32 = np.float32
    wi = {"w_up": _wl(inp["ffn_w_up"][i]), "w_down": _wl(inp["ffn_w_down"][i]),
          "conv_w": np.ascontiguousarray(inp["ffn_conv_w"][i].reshape(3, 22, 128).transpose(2, 0, 1)),
          "conv_b": _col(inp["ffn_conv_b"][i], 22), "ln_g": _col(inp["ln2_g"][i], 8), "ln_b": _col(inp["ln2_b"][i], 8)}
    in_maps = []
    for c in range(8):
        b, q = divmod(c, 4)
        halo = xs[c - 1][:, :, -2:] if q > 0 else np.zeros((128, 8, 2), f32)
        xin = np.ascontiguousarray(np.concatenate([halo, xs[c]], axis=2)[:, :, None, :])
        in_maps.append({"x1": xin, "hmask": np.full((128, 1), 1.0 if q > 0 else 0.0, f32), **wi})
    res = _run(_prog("ffn", lambda: build_ffn(1, NTOK)), in_maps)
    return [r["x2"] for r in res]


def kernel(**inp):
    inp = {k: np.asarray(v, np.float32) for k, v in inp.items()}
    x = inp["x"]
    xs = [_fm(x[c // 4, (c % 4) * NTOK:(c % 4 + 1) * NTOK]) for c in range(8)]
    for i in range(4):
        j = i // 2
        if i % 2 == 0:
            xs = _ab_layer(xs, inp, i, j)
        else:
            xs = _c_layer(xs, inp, i, j)
        xs = _ffn_layer(xs, inp, i)
    out = np.empty((2, SEQ, 1024), np.float32)
    for c in range(8):
        out[c // 4, (c % 4) * NTOK:(c % 4 + 1) * NTOK] = xs[c].transpose(1, 0, 2).reshape(1024, NTOK).T
    return out
```

```python
import math
import numpy as np
from contextlib import ExitStack
import concourse.bass as bass
import concourse.mybir as mybir
from concourse.bass_utils import run_bass_kernel_spmd

F32 = mybir.dt.float32
BF16 = mybir.dt.bfloat16
AF = mybir.ActivationFunctionType
ALU = mybir.AluOpType
AX = mybir.AxisListType

EPOCH = 12000
NDSLOT = 24


class Tk:
    def __init__(self, h, name):
        self.h = h
        self.name = name
        self.lw = None
        self.rd = {}

    def __getitem__(self, idx):
        return V(self, self.h[idx])

    def ap(self):
        return V(self, self.h[:])


class V:
    def __init__(self, tk, ap):
        self.tk = tk
        self.ap = ap

    def __getitem__(self, idx):
        return V(self.tk, self.ap[idx])

    def rearrange(self, *a, **k):
        return V(self.tk, self.ap.rearrange(*a, **k))

    def bitcast(self, dt):
        return V(self.tk, self.ap.bitcast(dt))

    def to_broadcast(self, shape):
        return V(self.tk, self.ap.to_broadcast(shape))


def _ap(x):
    return x.ap if isinstance(x, V) else x


class Prog:
    def __init__(self, name="k"):
        self.nc = bass.Bass("TRN2", target_bir_lowering=False)
        self.es = ExitStack()
        nc = self.nc
        self.engs = {"pe": nc.tensor, "act": nc.scalar, "dve": nc.vector,
                     "pool": nc.gpsimd, "sp": nc.sync}
        self.cnt = {k: 0 for k in self.engs}
        self.sems = {k: [] for k in self.engs}
        self.waited = {k: {} for k in self.engs}
        self.dsem = [self.es.enter_context(nc.semaphore(f"d{i}")) for i in range(NDSLOT)]
        self.dval = [0] * NDSLOT
        self.dslot = 0
        self.nt = 0
        self.out_tokens = []
        self.pes = None
        self.csems = []

    def sb(self, shape, dt=F32, name=None):
        self.nt += 1
        name = name or f"t{self.nt}"
        h = (self.pes or self.es).enter_context(self.nc.sbuf_tensor(name, list(shape), dt))
        return Tk(h, name)

    def ps(self, shape, dt=F32, name=None):
        self.nt += 1
        name = name or f"p{self.nt}"
        h = (self.pes or self.es).enter_context(self.nc.psum_tensor(name, list(shape), dt))
        return Tk(h, name)

    def dram(self, name, shape, dt=F32, kind="Internal"):
        h = self.nc.dram_tensor(name, list(shape), dt, kind=kind)
        return Tk(h, name)

    def _sem(self, key, val):
        if isinstance(key, tuple):
            if key[0] == "c":
                return self.csems[key[1]], val
            return self.dsem[key[1]], val
        ep = (val - 1) // EPOCH
        lst = self.sems[key]
        while len(lst) <= ep:
            lst.append(self.es.enter_context(self.nc.semaphore(f"s_{key}_{len(lst)}")))
        return lst[ep], (val - 1) % EPOCH + 1

    def _deps(self, e, reads, writes, pe_acc=False):
        deps = {}

        def add(tok):
            if tok is None:
                return
            k, v = tok
            if deps.get(k, 0) < v:
                deps[k] = v

        for x in reads:
            if isinstance(x, V):
                add(x.tk.lw)
        for x in writes:
            if isinstance(x, V):
                lw = x.tk.lw
                if not (pe_acc and lw is not None and lw[0] == "pe"):
                    add(lw)
                for k, v in x.tk.rd.items():
                    add((k, v))
        w = self.waited[e]
        eng = self.engs[e]
        for k, v in deps.items():
            if w.get(k, 0) >= v:
                continue
            w[k] = v
            sem, sv = self._sem(k, v)
            eng.wait_ge(sem, sv)

    def _mark(self, tok, reads, writes):
        k, v = tok
        for x in reads:
            if isinstance(x, V):
                if x.tk.rd.get(k, 0) < v:
                    x.tk.rd[k] = v
        for x in writes:
            if isinstance(x, V):
                x.tk.lw = tok
                x.tk.rd = {}

    def emit(self, e, fn, reads, writes, pe_acc=False):
        self._deps(e, reads, writes, pe_acc)
        ins = fn(self.engs[e])
        self.cnt[e] += 1
        tok = (e, self.cnt[e])
        sem, sv = self._sem(e, self.cnt[e])
        ins.then_inc(sem, 1)
        self._mark(tok, reads, writes)
        return tok

    def dma(self, q, out, in_, **kw):
        reads, writes = [in_], [out]
        self._deps(q, reads, writes)
        slot = self.dslot
        self.dslot = (slot + 1) % NDSLOT
        key = ("d", slot)
        prev = self.dval[slot]
        w = self.waited[q]
        if prev > 0 and w.get(key, 0) < prev:
            w[key] = prev
            self.engs[q].wait_ge(self.dsem[slot], prev)
        ins = self.engs[q].dma_start(out=_ap(out), in_=_ap(in_), **kw)
        ins.then_inc(self.dsem[slot], 16)
        self.dval[slot] += 16
        tok = (key, self.dval[slot])
        self._mark(tok, reads, writes)
        return tok

    def phase_begin(self):
        self.pes = ExitStack()
        self._consts = {}

    def barrier(self):
        toks = [(k, v) for k, v in self.cnt.items() if v > 0]
        toks += [(("d", i), v) for i, v in enumerate(self.dval) if v > 0]
        for e in self.engs:
            for t in toks:
                if t[0] != e:
                    self.wait_tok(e, t)

    def phase_end(self):
        self.barrier()
        self.pes.close()
        self.pes = None
        self._consts = {}

    def allgather(self, src_h, dst_h, groups):
        if not self.csems:
            self.csems.append(self.es.enter_context(self.nc.semaphore("ccsem")))
            self.ccount = 0
        ins = self.nc.gpsimd.collective_compute("AllGather", ALU.bypass, replica_groups=groups,
                                                ins=[src_h.ap().opt()], outs=[dst_h.ap().opt()])
        ins.then_inc(self.csems[0])
        self.ccount += 1
        for e in self.engs:
            self.engs[e].wait_ge(self.csems[0], self.ccount)

    def wait_tok(self, e, tok):
        k, v = tok
        w = self.waited[e]
        if w.get(k, 0) >= v:
            return
        w[k] = v
        sem, sv = self._sem(k, v)
        self.engs[e].wait_ge(sem, sv)

    def mm(self, out, lhsT, rhs, start=True, stop=True):
        return self.emit("pe", lambda E: E.matmul(_ap(out), _ap(lhsT), _ap(rhs), start=start, stop=stop),
                         [lhsT, rhs], [out], pe_acc=True)

    def transpose(self, out, in_, ident):
        return self.emit("pe", lambda E: E.transpose(_ap(out), _ap(in_), _ap(ident)),
                         [in_, ident], [out], pe_acc=True)

    def act(self, out, in_, func, bias=None, scale=1.0, accum_out=None, e="act"):
        reads = [in_]
        kw = {}
        if bias is not None:
            kw["bias"] = _ap(bias)
            reads.append(bias)
        if not isinstance(scale, (int, float)):
            reads.append(scale)
        kw["scale"] = _ap(scale)
        writes = [out]
        if accum_out is not None:
            kw["accum_out"] = _ap(accum_out)
            writes.append(accum_out)
        return self.emit(e, lambda E: E.activation(_ap(out), _ap(in_), func, **kw), reads, writes)

    def tt(self, out, in0, in1, op, e="dve"):
        return self.emit(e, lambda E: E.tensor_tensor(_ap(out), _ap(in0), _ap(in1), op), [in0, in1], [out])

    def ts(self, out, in0, s1, op0, s2=None, op1=None, e="dve", accum_out=None):
        reads = [in0, s1, s2]
        kw = {}
        writes = [out]
        if op1 is not None:
            kw["op1"] = op1
        if accum_out is not None:
            kw["accum_out"] = _ap(accum_out)
            writes.append(accum_out)
        return self.emit(e, lambda E: E.tensor_scalar(_ap(out), _ap(in0), _ap(s1), _ap(s2), op0, **kw),
                         reads, writes)

    def stt(self, out, in0, scalar, in1, op0, op1, e="dve", accum_out=None):
        kw = {}
        writes = [out]
        if accum_out is not None:
            kw["accum_out"] = _ap(accum_out)
            writes.append(accum_out)
        return self.emit(e, lambda E: E.scalar_tensor_tensor(_ap(out), _ap(in0), _ap(scalar), _ap(in1), op0, op1, **kw),
                         [in0, scalar, in1], writes)

    def copy(self, out, in_, e="dve"):
        if e == "act":
            return self.emit(e, lambda E: E.copy(_ap(out), _ap(in_)), [in_], [out])
        return self.emit(e, lambda E: E.tensor_copy(_ap(out), _ap(in_)), [in_], [out])

    def memset(self, out, val, e="dve"):
        return self.emit(e, lambda E: E.memset(_ap(out), val), [], [out])

    def reduce(self, out, in_, op, axis=AX.X, e="dve"):
        return self.emit(e, lambda E: E.tensor_reduce(_ap(out), _ap(in_), axis, op), [in_], [out])

    def const(self, val):
        if not hasattr(self, "_consts"):
            self._consts = {}
        if val not in self._consts:
            t = self.sb([128, 1], F32)
            self.memset(t[:], float(val), e="pool")
            self._consts[val] = t
        return self._consts[val]

    def recip(self, out, in_):
        return self.emit("dve", lambda E: E.reciprocal(_ap(out), _ap(in_)), [in_], [out])

    def bn_stats(self, out, in_):
        return self.emit("dve", lambda E: E.bn_stats(_ap(out), _ap(in_)), [in_], [out])

    def bn_aggr(self, out, in_):
        return self.emit("dve", lambda E: E.bn_aggr(_ap(out), _ap(in_)), [in_], [out])

    def finish(self, toks):
        for t in toks:
            self.wait_tok("sp", t)
        self.es.close()
        return self.nc


ALPHA = 8.0 ** 0.25
D = 1024
DFF = 2816
NFC = 22


def load_cast(P, dst_views, src_views, stg, i0=0):
    for i, (d, s) in enumerate(zip(dst_views, src_views)):
        st = stg[(i0 + i) % len(stg)]
        n = _ap(s).shape[-1]
        P.dma("sp", st[:, 0:n], s)
        P.copy(d, st[:, 0:n], e=("dve" if (i0 + i) % 2 else "pool"))
    return i0 + len(dst_views)


def ln_inplace(P, z, TT, onesm, g, b, pl, sq, stat, eps=1e-5, out_bf=None):
    for c in range(8):
        P.mm(pl[0][:, 0:TT], onesm[:], z[:, c, :], start=(c == 0), stop=(c == 7))
        s = sq[c % 2]
        P.act(s[:, 0:TT], z[:, c, :], AF.Square)
        P.mm(pl[1][:, 0:TT], onesm[:], s[:, 0:TT], start=(c == 0), stop=(c == 7))
    mean, msq, rstd = stat
    P.copy(mean[:, 0:TT], pl[0][:, 0:TT], e="act")
    P.tt(msq[:, 0:TT], mean[:, 0:TT], mean[:, 0:TT], ALU.mult, e="pool")
    P.tt(rstd[:, 0:TT], pl[1][:, 0:TT], msq[:, 0:TT], ALU.subtract)
    P.act(rstd[:, 0:TT], rstd[:, 0:TT], AF.Sqrt, bias=P.const(eps)[:, 0:1])
    P.recip(rstd[:, 0:TT], rstd[:, 0:TT])
    for c in range(8):
        P.tt(z[:, c, :], z[:, c, :], mean[:, 0:TT], ALU.subtract)
        P.tt(z[:, c, :], z[:, c, :], rstd[:, 0:TT], ALU.mult, e="pool")
        P.act(z[:, c, :], z[:, c, :], AF.Identity, bias=b[:, c:c + 1], scale=g[:, c:c + 1])


T_SEQ = 16384
NTOK = 4096
GROUPS = [[0, 1, 2, 3], [4, 5, 6, 7]]


def emit_cast_x(P, x_ap, xg_ap):
    P.phase_begin()
    TT = 512
    xf = [P.sb([128, 8, TT], F32) for _ in range(2)]
    xb = [P.sb([128, 8, TT], BF16) for _ in range(2)]
    for i in range(NTOK // TT):
        P.dma("sp", xf[i % 2][:], x_ap[:, :, i * TT:(i + 1) * TT])
        P.copy(xb[i % 2][:], xf[i % 2][:], e=("dve" if i % 2 else "pool"))
        P.dma("sp", xg_ap[i][:, :, :], xb[i % 2][:])
    P.phase_end()


def emit_abin_h(P, xg_all, Wd, S, TT=256):
    T = T_SEQ
    P.phase_begin()
    pp = P.sb([128, 10]); w2 = P.sb([128, 128]); a2 = P.sb([128, 128]); g2 = P.sb([128, 128])
    hblk = P.sb([128, 128]); rotm = P.sb([128, 128]); ident = P.sb([128, 128])
    for t_, n in ((pp, "pp"), (w2, "w2p"), (a2, "a2p"), (g2, "g2"), (hblk, "hblk"), (rotm, "rotm"), (ident, "ident")):
        P.dma("sp", t_[:], Wd[n])
    MU, W0, A0, KK_, KA, RK = 0, 5, 6, 7, 8, 9
    win = [P.sb([128, 1024], BF16) for _ in range(8)]
    stg = [P.sb([128, 1024], F32) for _ in range(2)]
    load_cast(P, [w[:, :] for w in win], [Wd["w_in"][:, kc, :] for kc in range(8)], stg)

    xb = [P.sb([128, 8, TT + 1], BF16) for _ in range(2)]
    prs = [P.sb([128, TT + 1], F32) for _ in range(2)]
    dd = [P.sb([128, TT], F32) for _ in range(2)]
    XS = P.sb([128, 2, TT], F32)
    KR = P.sb([128, TT], F32)
    FM4 = [P.sb([128, 4, TT], F32) for _ in range(2)]
    LW = [P.sb([128, TT], F32) for _ in range(2)]
    OV = [P.sb([128, TT], F32) for _ in range(2)]
    OG = [P.sb([128, TT], F32) for _ in range(2)]
    OB = [P.sb([128, TT], F32) for _ in range(2)]
    DQ = [P.sb([128, TT], F32) for _ in range(2)]
    DK = [P.sb([128, TT], F32) for _ in range(2)]
    DVt = P.sb([128, TT], F32)
    TM4 = [P.sb([128, 4, 128], F32) for _ in range(2)]
    VT = [P.sb([128, 128], F32) for _ in range(2)]
    TW = P.sb([128, TT], F32); SG = P.sb([128, TT], F32)
    tmp = [P.sb([128, TT], F32) for _ in range(4)]
    tabs = [P.sb([128, 4, TT], F32) for _ in range(2)]
    ps = [P.ps([128, 512]) for _ in range(8)]
    pi = [0]

    def nps():
        pi[0] += 1
        return ps[pi[0] % 8]

    NTL = T // TT
    TPR = NTOK // TT

    def load_tile(i):
        r, lo = divmod(i, TPR)
        lo *= TT
        X = xb[i % 2]
        P.dma("sp", X[:, :, 1:TT + 1], xg_all[lo // 512][r, :, :, lo % 512:lo % 512 + TT])
        if i == 0:
            P.memset(X[:, :, 0:1], 0.0, e="pool")
        else:
            P.copy(X[:, :, 0:1], xb[(i - 1) % 2][:, :, TT:TT + 1], e="pool")
        for j, n in enumerate(("cosq", "sinq", "cosk", "sink")):
            P.dma("sp", tabs[i % 2][:, j, :], Wd[n][:, i * TT:(i + 1) * TT])

    load_tile(0)
    for i in range(NTL):
        XB = xb[i % 2]; TB = tabs[i % 2]
        if i + 1 < NTL:
            load_tile(i + 1)
        F4 = FM4[i % 2]; lw_ = LW[i % 2]; ov = OV[i % 2]; og = OG[i % 2]; obn = OB[i % 2]; dq = DQ[i % 2]; dk = DK[i % 2]
        for c in range(5):
            p_ = nps()
            for kc in range(8):
                P.mm(p_[:, 0:TT + 1], win[kc][:, c * 128:(c + 1) * 128], XB[:, kc, :], start=(kc == 0), stop=(kc == 7))
            s_ = prs[c % 2]; d_ = dd[c % 2]
            P.copy(s_[:], p_[:, 0:TT + 1], e="act")
            P.tt(d_[:], s_[:, 0:TT], s_[:, 1:TT + 1], ALU.subtract)
            dst_ = (F4[:, 0, :], KR[:], ov[:], XS[:, 0, :], XS[:, 1, :])[c]
            P.stt(dst_, d_[:], pp[:, MU + c:MU + c + 1], s_[:, 1:TT + 1], ALU.mult, ALU.add)
        P.act(TW[:], XS[:, 0, :], AF.Tanh)
        P.act(SG[:], XS[:, 1, :], AF.Sigmoid)
        p_ = nps()
        P.mm(p_[:, 0:TT], w2[:], TW[:])
        t_ = tmp[0]
        P.act(t_[:], p_[:, 0:TT], AF.Sigmoid, bias=pp[:, W0:W0 + 1])
        P.ts(lw_[:], t_[:], -0.6065306597126334, ALU.mult, e="pool")
        p_ = nps()
        P.mm(p_[:, 0:TT], a2[:], XS[:, 0, :])
        A_ = tmp[1]
        P.act(A_[:], p_[:, 0:TT], AF.Sigmoid, bias=pp[:, A0:A0 + 1])
        p_ = nps()
        P.mm(p_[:, 0:TT], g2[:], SG[:])
        P.copy(og[:], p_[:, 0:TT], e="act")
        KKt = tmp[2]
        P.ts(KKt[:], KR[:], pp[:, KK_:KK_ + 1], ALU.mult)
        s2 = tmp[3]
        P.tt(s2[:], KKt[:], KKt[:], ALU.mult, e="pool")
        p_ = nps()
        P.mm(p_[:, 0:TT], hblk[:], s2[:])
        P.act(s2[:], p_[:, 0:TT], AF.Sqrt)
        P.ts(s2[:], s2[:], 1e-12, ALU.max)
        P.recip(s2[:], s2[:])
        P.tt(KKt[:], KKt[:], s2[:], ALU.mult)
        P.ts(F4[:, 2, :], KKt[:], -1.0, ALU.mult, e="pool")
        P.tt(F4[:, 3, :], KKt[:], A_[:], ALU.mult)
        P.ts(A_[:], A_[:], -1.0, ALU.add, pp[:, KA:KA + 1], ALU.mult)
        P.stt(F4[:, 1, :], A_[:], 1.0, KR[:], ALU.add, ALU.mult)
        P.stt(s2[:], F4[:, 0, :], pp[:, RK:RK + 1], F4[:, 1, :], ALU.mult, ALU.mult)
        p_ = nps()
        P.mm(p_[:, 0:TT], hblk[:], s2[:])
        P.tt(obn[:], p_[:, 0:TT], ov[:], ALU.mult)
        for c in range(3):
            p_ = nps()
            col = 640 + c * 128
            for kc in range(8):
                P.mm(p_[:, 0:TT], win[kc][:, col:col + 128], XB[:, kc, 1:TT + 1], start=(kc == 0), stop=(kc == 7))
            if c == 2:
                P.copy(DVt[:], p_[:, 0:TT], e="act")
                continue
            isq = c == 0
            qs = tmp[c % 2]
            P.copy(qs[:], p_[:, 0:TT], e="act")
            p2 = nps()
            P.mm(p2[:, 0:TT], rotm[:], qs[:])
            t2 = tmp[2 + c % 2]
            P.tt(t2[:], p2[:, 0:TT], TB[:, 1 if isq else 3, :], ALU.mult)
            P.tt(qs[:], qs[:], TB[:, 0 if isq else 2, :], ALU.mult, e="pool")
            P.tt((dq if isq else dk)[:], qs[:], t2[:], ALU.add)
        c0 = i * (TT // 64)
        for h in range(2):
            for a in range(4):
                P.dma("sp", S["fmp"][h, c0:c0 + TT // 64, :, a * 64:(a + 1) * 64].rearrange("c k t -> k c t"),
                      F4[h * 64:(h + 1) * 64, a, :].rearrange("k (c t) -> k c t", t=64))
        for half in range(TT // 128):
            sl = slice(half * 128, (half + 1) * 128)
            pt = nps()
            for ai, src in enumerate((F4[:, 3, sl], F4[:, 1, sl], lw_[:, sl], ov[:, sl])):
                P.transpose(pt[:, ai * 128:(ai + 1) * 128], src, ident[:])
            tm = TM4[half % 2]
            P.copy(tm[:].rearrange("p a k -> p (a k)"), pt[:, 0:512], e=("act" if half % 2 else "dve"))
            for h in range(2):
                for cc in range(2):
                    P.dma("sp", S["tmp"][h, c0 + half * 2 + cc].rearrange("t (a k) -> t a k", a=4),
                          tm[cc * 64:(cc + 1) * 64, :, h * 64:(h + 1) * 64])
            pv_ = nps()
            P.transpose(pv_[:, 0:128], DVt[:, sl], ident[:])
            vt = VT[half % 2]
            P.copy(vt[:], pv_[:, 0:128], e="act")
            P.dma("sp", S["vtm"][:, i * (TT // 128) + half, :], vt[:])
        tsl = slice(i * TT, (i + 1) * TT)
        for cmp_ in range(2):
            P.dma("sp", S["qT"][:, cmp_, tsl], dq[cmp_ * 64:(cmp_ + 1) * 64, :])
            P.dma("sp", S["kT"][:, cmp_, tsl], dk[cmp_ * 64:(cmp_ + 1) * 64, :])
        P.dma("sp", S["g"][:, tsl], og[:])
        P.dma("sp", S["bonus"][:, tsl], obn[:])
    P.phase_end()


def emit_scan(P, S, Cd, NH=2, NCH=256):
    L = 64
    P.phase_begin()
    fmp_d, tmp_d, y_d = S["fmp"], S["tmp"], S["y"]
    C = {}
    for n in ("TRI2", "TGT", "MASK2", "MASKT", "IDENT"):
        C[n] = P.sb(list(Cd[n].shape), F32)
        P.dma("sp", C[n][:], Cd[n])
    NS = 2

    def mk():
        return dict(FM=P.sb([64, 256]), TM=P.sb([64, 256]), E12=P.sb([64, 128]), E3=P.sb([64, 64]), E4=P.sb([64, 64]),
                    AR=P.sb([64, 128]), BK=P.sb([64, 128]), BKT=P.sb([64, 128]), AB=P.sb([64, 128]), AK=P.sb([64, 128]),
                    X=P.sb([64, 64]), XT=P.sb([64, 64]), Tm=P.sb([64, 64]), Psb=P.sb([64, 64]), U=P.sb([64, 64]))
    sets = [[mk() for _ in range(NS)] for _ in range(NH)]
    ST = [P.sb([64, 64]) for _ in range(NH)]
    for h in range(NH):
        P.memset(ST[h][:], 0.0)
    YB = [[P.sb([64, 512]) for _ in range(2)] for _ in range(NH)]
    banks = [P.ps([128, 512]) for _ in range(8)]
    pi = [0]

    def nps():
        pi[0] += 1
        return banks[pi[0] % 8]

    def load(h, c):
        s = sets[h][c % NS]
        P.dma("sp", s["FM"][:], fmp_d[h, c])
        P.dma("sp", s["TM"][:], tmp_d[h, c])

    def prep(h, c):
        s = sets[h][c % NS]
        FM, TM = s["FM"], s["TM"]
        lw = TM[:, 128:192]
        cps = nps(); sfx = nps()
        P.mm(cps[0:64, 0:128], lw, C["TRI2"][:])
        P.mm(sfx[0:64, 0:64], C["TGT"][:], lw)
        P.act(s["E12"][:], cps[0:64, 0:128], AF.Exp)
        P.act(s["E3"][:], cps[0:64, 0:64], AF.Exp, scale=-1.0)
        P.act(s["E4"][:], sfx[0:64, 0:64], AF.Exp)
        P.tt(s["AR"][:, 0:64], FM[:, 128:192], s["E12"][:, 64:128], ALU.mult, e="pool")
        P.tt(s["AR"][:, 64:128], FM[:, 0:64], s["E12"][:, 0:64], ALU.mult, e="pool")
        P.tt(s["BK"][:, 0:64], FM[:, 192:256], s["E3"][:], ALU.mult, e="pool")
        P.tt(s["BK"][:, 64:128], FM[:, 64:128], s["E3"][:], ALU.mult, e="pool")
        P.tt(s["BKT"][:, 0:64], TM[:, 0:64], s["E4"][:], ALU.mult, e="pool")
        P.tt(s["BKT"][:, 64:128], TM[:, 64:128], s["E4"][:], ALU.mult, e="pool")
        pb = nps(); pk = nps(); pn = nps()
        P.mm(pb[0:64, 0:128], s["BK"][:, 0:64], s["AR"][:])
        P.mm(pk[0:64, 0:128], s["BK"][:, 64:128], s["AR"][:])
        P.mm(pn[0:64, 0:64], s["AR"][:, 0:64], s["BK"][:, 0:64])
        P.tt(s["AB"][:], pb[0:64, 0:128], C["MASK2"][:], ALU.mult)
        P.tt(s["AK"][:], pk[0:64, 0:128], C["MASK2"][:], ALU.mult)
        P.tt(s["XT"][:], pn[0:64, 0:64], C["MASKT"][:], ALU.mult)
        P.copy(s["X"][:], s["AB"][:, 0:64], e="pool")
        P.tt(s["Tm"][:], s["AB"][:, 0:64], C["IDENT"][:], ALU.add, e="pool")
        for j in range(1, 6):
            pxt = nps()
            P.mm(pxt[0:64, 0:64], s["X"][:], s["XT"][:])
            if j < 5:
                px = nps()
                P.mm(px[0:64, 0:64], s["XT"][:], s["X"][:])
                P.copy(s["X"][:], px[0:64, 0:64], e="act")
            P.copy(s["XT"][:], pxt[0:64, 0:64])
            pt = nps()
            P.mm(pt[0:64, 0:64], s["XT"][:], s["Tm"][:])
            P.tt(s["Tm"][:], pt[0:64, 0:64], s["Tm"][:], ALU.add)

    def rec(h, c):
        s = sets[h][c % NS]
        V_ = s["TM"][:, 192:256]
        pp = nps()
        P.mm(pp[0:64, 0:64], s["AR"][:, 0:64], ST[h][:], start=True, stop=False)
        P.mm(pp[0:64, 0:64], s["AK"][:, 0:64], V_, start=False, stop=True)
        P.copy(s["Psb"][:], pp[0:64, 0:64], e="act")
        pu = nps()
        P.mm(pu[0:64, 0:64], s["Tm"][:], s["Psb"][:])
        P.copy(s["U"][:], pu[0:64, 0:64], e="act")
        py = nps()
        P.mm(py[0:64, 0:64], ST[h][:], s["AR"][:, 64:128], start=True, stop=False)
        P.mm(py[0:64, 0:64], s["U"][:], s["AB"][:, 64:128], start=False, stop=False)
        P.mm(py[0:64, 0:64], V_, s["AK"][:, 64:128], start=False, stop=True)
        pd_ = nps()
        P.mm(pd_[0:64, 0:64], s["BKT"][:, 0:64], s["U"][:], start=True, stop=False)
        P.mm(pd_[0:64, 0:64], s["BKT"][:, 64:128], V_, start=False, stop=True)
        P.stt(ST[h][:], ST[h][:], s["E12"][:, 63:64], pd_[0:64, 0:64], ALU.mult, ALU.add)
        yb = YB[h][(c // 8) % 2]
        P.copy(yb[:, (c % 8) * 64:(c % 8 + 1) * 64], py[0:64, 0:64], e="act")
        if c % 8 == 7 or c == NCH - 1:
            c0 = (c // 8) * 8
            P.dma("sp", y_d[h, :, c0 * L:(c + 1) * L], yb[:, 0:(c - c0 + 1) * L])

    for h in range(NH):
        load(h, 0)
    for h in range(NH):
        prep(h, 0)
    for c in range(NCH):
        for h in range(NH):
            if c + 1 < NCH:
                load(h, c + 1)
                prep(h, c + 1)
            rec(h, c)
    P.phase_end()


def emit_attn(P, S, Wd, mix_src):
    T = T_SEQ
    NB = T // 128
    NG = T // 512
    P.phase_begin()
    q_d, k_d, v_d = S["qT"], S["kT"], S["vtm"]
    lamv = P.sb([128, 256]); li = P.sb([128, 1]); gsc = P.sb([128, 128]); trif = P.sb([128, 128]); sel = P.sb([64, 65])
    ident = P.sb([128, 128])
    P.dma("sp", lamv[:], Wd["lamv"][0:1, :].to_broadcast([128, 256]))
    P.dma("sp", li[:], Wd["lam_init"][0:1, :].to_broadcast([128, 1]))
    P.dma("sp", gsc[:], Wd["subln_g"][0:1, :].to_broadcast([128, 128]))
    P.dma("sp", trif[:], Wd["tri"])
    P.dma("sp", sel[:], Wd["sel"])
    P.dma("sp", ident[:], Wd["ident"])
    tri = P.sb([128, 128], BF16)
    P.copy(tri[:], trif[:])
    sc = P.sb([128, 8]); junk = P.sb([128, 128])
    P.stt(junk[:, 0:64], lamv[:, 0:64], 1.0, lamv[:, 64:128], ALU.mult, ALU.mult, accum_out=sc[:, 0:1])
    P.stt(junk[:, 0:64], lamv[:, 128:192], 1.0, lamv[:, 192:256], ALU.mult, ALU.mult, accum_out=sc[:, 1:2])
    P.act(sc[:, 2:4], sc[:, 0:2], AF.Exp)
    P.tt(sc[:, 4:5], sc[:, 2:3], sc[:, 3:4], ALU.subtract)
    P.tt(sc[:, 4:5], sc[:, 4:5], li[:], ALU.add)
    P.ts(sc[:, 5:6], sc[:, 4:5], -1.0, ALU.mult)
    P.ts(sc[:, 6:7], li[:], -1.0, ALU.mult, 1.0, ALU.add)
    P.ts(gsc[:], gsc[:], sc[:, 6:7], ALU.mult)
    nlam = sc[:, 5:6]
    Ka = [P.sb([65, T], BF16) for _ in range(2)]
    Va = P.sb([128, NB, 129], BF16)
    for c in range(2):
        P.memset(Ka[c][64:65, :], 1.0, e="pool")
    P.memset(Va[:, :, 128:129], 1.0, e="pool")
    kst = P.sb([65, 8])
    P.memset(kst[64:65, :], 0.0)
    stg = [P.sb([128, 2048], F32) for _ in range(2)]
    sqb = [P.sb([64, 1024], F32) for _ in range(2)]
    pss = [P.ps([128, 512]) for _ in range(3)]
    pso = [P.ps([128, 512]) for _ in range(4)]
    pmisc = P.ps([128, 512])
    for i in range(NG):
        st = stg[i % 2]
        kf = st[0:64, 0:1024].rearrange("p (c t) -> p c t", c=2)
        P.dma("sp", kf, k_d[:, :, i * 512:(i + 1) * 512])
        s2 = sqb[i % 2]
        P.tt(s2[:], st[0:64, 0:1024], st[0:64, 0:1024], ALU.mult, e="pool")
        for c in range(2):
            P.copy(Ka[c][0:64, i * 512:(i + 1) * 512], st[0:64, c * 512:(c + 1) * 512], e="act")
            P.mm(pmisc[0:65, 0:512], sel[:], s2[:, c * 512:(c + 1) * 512])
            P.reduce(kst[64:65, 2 + c:3 + c], pmisc[64:65, 0:512], ALU.max)
            P.tt(kst[64:65, c:c + 1], kst[64:65, c:c + 1], kst[64:65, 2 + c:3 + c], ALU.max)
    VP = 16
    for i in range(NB // VP):
        st = stg[i % 2]
        P.dma("sp", st[:, 0:VP * 128].rearrange("p (n d) -> p n d", d=128), v_d[:, i * VP:(i + 1) * VP, :])
        P.copy(Va[:, i * VP:(i + 1) * VP, 0:128], st[:, 0:VP * 128].rearrange("p (n d) -> p n d", d=128), e=("dve" if i % 2 else "pool"))
    P.act(kst[64:65, 4:6], kst[64:65, 0:2], AF.Sqrt)
    P.ts(kst[64:65, 6:8], kst[64:65, 4:6], -1.0, ALU.mult)

    qf = [P.sb([64, 2, 512], F32) for _ in range(2)]
    qsq = P.sb([64, 2, 512], F32)
    nq = P.sb([65, 512], F32)
    Qa = [[P.sb([65, 512], BF16) for _ in range(2)] for _ in range(2)]
    PT = [P.sb([128, 512], BF16) for _ in range(3)]
    res = [P.sb([128, 128], F32) for _ in range(4)]
    ob = [P.sb([128, 128], F32) for _ in range(2)]
    obT = [P.sb([128, 128], BF16) for _ in range(2)]
    st2 = P.sb([128, 8], F32)
    P.dma("sp", qf[0][:], q_d[:, :, 0:512])
    cnt = 0
    for qg in range(NG):
        Qf = qf[qg % 2]
        if qg + 1 < NG:
            P.dma("sp", qf[(qg + 1) % 2][:], q_d[:, :, (qg + 1) * 512:(qg + 2) * 512])
        P.tt(qsq[:], Qf[:], Qf[:], ALU.mult, e="pool")
        Q = Qa[qg % 2]
        for c in range(2):
            P.copy(Q[c][0:64, :], Qf[:, c, :], e="pool")
            P.mm(pmisc[0:65, 0:512], sel[:], qsq[:, c, :])
            P.act(nq[64:65, :], pmisc[64:65, 0:512], AF.Sqrt)
            P.ts(Q[c][64:65, :], nq[64:65, :], kst[64:65, 6 + c:7 + c], ALU.mult)
        for c in range(2):
            nkb = (qg + 1) * 4
            for kb in range(nkb):
                j = kb - qg * 4
                q0 = max(j, 0) * 128
                ps_ = pss[cnt % 3]; pt = PT[cnt % 3]; cnt += 1
                P.mm(ps_[:, q0:512], Ka[c][:, kb * 128:(kb + 1) * 128], Q[c][:, q0:512])
                P.act(pt[:, q0:512], ps_[:, q0:512], AF.Exp)
                if j >= 0:
                    P.tt(pt[:, q0:q0 + 128], pt[:, q0:q0 + 128], tri[:], ALU.mult, e="dve")
                for qb in range(max(j, 0), 4):
                    P.mm(pso[qb][:, 0:129], pt[:, qb * 128:(qb + 1) * 128], Va[:, kb, :],
                         start=(kb == 0), stop=(kb == qg * 4 + qb))
            for qb in range(4):
                oo = pso[qb]
                if c == 0:
                    P.recip(st2[:, qb:qb + 1], oo[:, 128:129])
                    P.ts(res[qb][:], oo[:, 0:128], st2[:, qb:qb + 1], ALU.mult)
                else:
                    a_ = ob[qb % 2]
                    P.recip(st2[:, 4:5], oo[:, 128:129])
                    P.tt(st2[:, 5:6], st2[:, 4:5], nlam, ALU.mult)
                    P.stt(a_[:], oo[:, 0:128], st2[:, 5:6], res[qb][:], ALU.mult, ALU.add)
                    P.stt(junk[:], a_[:], 1.0, a_[:], ALU.mult, ALU.mult, accum_out=st2[:, 6:7])
                    P.act(st2[:, 7:8], st2[:, 6:7], AF.Sqrt, bias=P.const(1e-5)[:, 0:1], scale=1.0 / 128.0)
                    P.recip(st2[:, 7:8], st2[:, 7:8])
                    P.stt(a_[:], a_[:], st2[:, 7:8], gsc[:], ALU.mult, ALU.mult)
                    P.transpose(pmisc[:, 0:128], a_[:], ident[:])
                    aT = obT[qb % 2]
                    P.copy(aT[:], pmisc[:, 0:128])
                    blk = qg * 4 + qb
                    P.dma("sp", mix_src[blk // 16][128:256, (blk % 16) * 128:(blk % 16 + 1) * 128], aT[:])
    P.phase_end()


def emit_post(P, S, Wd, mix_src, TT=512):
    T = T_SEQ
    P.phase_begin()
    pp = P.sb([128, 2]); hb = P.sb([128, 128])
    P.dma("sp", pp[:], Wd["ppx"])
    P.dma("sp", hb[:], Wd["hblk64"])
    yf = [P.sb([128, TT], F32) for _ in range(2)]
    bf = [P.sb([128, TT], F32) for _ in range(2)]
    gf = [P.sb([128, TT], F32) for _ in range(2)]
    d_ = [P.sb([128, TT], F32) for _ in range(2)]
    q_ = [P.sb([128, TT], F32) for _ in range(2)]
    r_ = [P.sb([128, TT], F32) for _ in range(2)]
    mo = [P.sb([128, TT], BF16) for _ in range(2)]
    pm = [P.ps([128, 512]) for _ in range(2)]
    pv = [P.ps([128, 512]) for _ in range(2)]
    y2 = S["y"].rearrange("h k t -> (h k) t")

    def load(i):
        sl = slice(i * TT, (i + 1) * TT)
        P.dma("sp", yf[i % 2][:], y2[:, sl])
        P.dma("sp", bf[i % 2][:], S["bonus"][:, sl])
        P.dma("sp", gf[i % 2][:], S["g"][:, sl])

    load(0)
    for i in range(T // TT):
        if i + 1 < T // TT:
            load(i + 1)
        Y, B, G = yf[i % 2], bf[i % 2], gf[i % 2]
        d = d_[i % 2]; q = q_[i % 2]; r = r_[i % 2]
        P.mm(pm[i % 2][:, 0:TT], hb[:], Y[:])
        P.tt(d[:], Y[:], pm[i % 2][:, 0:TT], ALU.subtract)
        P.tt(q[:], d[:], d[:], ALU.mult, e="pool")
        P.mm(pv[i % 2][:, 0:TT], hb[:], q[:])
        P.act(r[:], pv[i % 2][:, 0:TT], AF.Sqrt, bias=P.const(64e-5)[:, 0:1])
        P.recip(r[:], r[:])
        P.tt(d[:], d[:], r[:], ALU.mult)
        P.act(d[:], d[:], AF.Identity, bias=pp[:, 1:2], scale=pp[:, 0:1])
        P.tt(d[:], d[:], B[:], ALU.add, e="pool")
        P.tt(mo[i % 2][:], d[:], G[:], ALU.mult)
        P.dma("sp", mix_src[(i * TT) // 2048][0:128, (i * TT) % 2048:(i * TT) % 2048 + TT], mo[i % 2][:])
    P.phase_end()


def emit_outproj(P, x_ap, mix_all, Wd, x1_ap, halo_src, qoff, TT=256):
    P.phase_begin()
    onesm = P.sb([128, 128], F32)
    P.memset(onesm[:], 1.0 / 1024.0)
    lg = P.sb([128, 8]); lb = P.sb([128, 8])
    P.dma("sp", lg[:], Wd["ln_g"]); P.dma("sp", lb[:], Wd["ln_b"])
    wout = [P.sb([128, 1024], BF16) for _ in range(8)]
    stg = [P.sb([128, 1024], F32) for _ in range(2)]
    load_cast(P, [w[:, :] for w in wout], [Wd["w_out"][:, c, :] for c in range(8)], stg)
    xf = [P.sb([128, 8, TT], F32) for _ in range(2)]
    MIX = [P.sb([128, 8, TT], BF16) for _ in range(2)]
    sq = [P.sb([128, TT], F32) for _ in range(2)]
    stat = [P.sb([128, TT], F32) for _ in range(3)]
    po = [P.ps([128, 512]) for _ in range(2)]
    pl = [P.ps([128, 512]) for _ in range(2)]
    NTL = NTOK // TT

    selq = P.sb([128, 4])
    P.dma("sp", selq[:], qoff)
    CAND = [[P.sb([128, 8, TT], BF16) for _ in range(4)] for _ in range(2)]

    def load(i):
        P.dma("sp", xf[i % 2][:], x_ap[:, :, i * TT:(i + 1) * TT])
        for qq in range(4):
            for hf in range(2):
                g_ = qq * NTOK + i * TT
                P.dma("sp", CAND[i % 2][qq][:, hf * 4:(hf + 1) * 4, :],
                      mix_all[g_ // 2048][:, hf, :, g_ % 2048: g_ % 2048 + TT].rearrange("r p t -> p r t"))
        M_ = MIX[i % 2]
        P.ts(M_[:], CAND[i % 2][0][:], selq[:, 0:1], ALU.mult)
        for qq in range(1, 4):
            P.stt(M_[:], CAND[i % 2][qq][:], selq[:, qq:qq + 1], M_[:], ALU.mult, ALU.add)

    load(0)
    for i in range(NTL):
        X, M = xf[i % 2], MIX[i % 2]
        if i + 1 < NTL:
            load(i + 1)
        for oc in range(8):
            p_ = po[oc % 2]
            for c in range(8):
                P.mm(p_[:, 0:TT], wout[c][:, oc * 128:(oc + 1) * 128], M[:, c, :], start=(c == 0), stop=(c == 7))
            P.stt(X[:, oc, :], X[:, oc, :], ALPHA, p_[:, 0:TT], ALU.mult, ALU.add)
        ln_inplace(P, X, TT, onesm, lg, lb, pl, sq, stat)
        P.dma("sp", x1_ap[:, :, i * TT:(i + 1) * TT], X[:])
        if i == NTL - 1:
            P.dma("sp", halo_src.rearrange("p (c t) -> p c t", t=2), X[:, :, TT - 2:TT])
    P.phase_end()


def emit_gmlp(P, x_ap, Wd, x1_ap, halo_src):
    TT = 128
    P.phase_begin()
    onesm = P.sb([128, 128], F32)
    P.memset(onesm[:], 1.0 / 1024.0)
    bu = P.sb([128, 8]); g = P.sb([128, 8]); b = P.sb([128, 8])
    bv = P.sb([128, 1024]); lg = P.sb([128, 1024]); lb = P.sb([128, 1024]); bs = P.sb([128, 1024])
    for t_, n in ((bu, "b_u"), (g, "ln_g"), (b, "ln_b")):
        P.dma("sp", t_[:], Wd[n])
    for t_, n in ((bv, "b_v"), (lg, "cln_g"), (lb, "cln_b"), (bs, "b_s")):
        P.dma("sp", t_[:], Wd[n][0:1, :].to_broadcast([128, 1024]))
    mask = P.sb([128, 128], F32)
    P.dma("sp", mask[:], Wd["tri"])
    wsf = P.sb([128, 8, 128], F32)
    P.dma("sp", wsf[:], Wd["wsT"])
    wsb = P.sb([128, 8, 128], BF16)
    for gi in range(8):
        P.tt(wsb[:, gi, :], wsf[:, gi, :], mask[:], ALU.mult)
    win = [P.sb([128, 2048], BF16) for _ in range(8)]
    wout = [P.sb([128, 1024], BF16) for _ in range(8)]
    stg = [P.sb([128, 2048], F32) for _ in range(2)]
    load_cast(P, [w[:, :] for w in win] + [w[:, :] for w in wout],
              [Wd["w_in"][:, kc, :] for kc in range(8)] + [Wd["w_out"][:, kc, :] for kc in range(8)], stg)
    xf = [P.sb([128, 8, TT], F32) for _ in range(2)]
    xb = [P.sb([128, 8, TT], BF16) for _ in range(2)]
    U = P.sb([128, 8, TT], F32)
    V1 = P.sb([128, 1024], F32)
    junk = P.sb([128, 1024], F32)
    VN = P.sb([128, 1024], BF16)
    GU = P.sb([128, 8, TT], BF16)
    mx = [P.sb([128, TT], F32) for _ in range(2)]
    st = P.sb([128, 4], F32)
    sq = [P.sb([128, TT], F32) for _ in range(2)]
    stat = [P.sb([128, TT], F32) for _ in range(3)]
    pu = [P.ps([128, 512]) for _ in range(2)]
    pvv = [P.ps([128, 512]) for _ in range(2)]
    pm = [P.ps([128, 512]) for _ in range(2)]
    pl = [P.ps([128, 512]) for _ in range(2)]
    NTL = NTOK // TT

    def load_tile(i):
        P.dma("sp", xf[i % 2][:], x_ap[:, :, i * TT:(i + 1) * TT])
        P.copy(xb[i % 2][:], xf[i % 2][:], e="pool")

    load_tile(0)
    for i in range(NTL):
        X, XB = xf[i % 2], xb[i % 2]
        if i + 1 < NTL:
            load_tile(i + 1)
        for oc in range(8):
            pp_ = pu[oc % 2]
            for kc in range(8):
                P.mm(pp_[:, 0:TT], win[kc][:, oc * 128:(oc + 1) * 128], XB[:, kc, :], start=(kc == 0), stop=(kc == 7))
            P.act(U[:, oc, :], pp_[:, 0:TT], AF.Gelu, bias=bu[:, oc:oc + 1])
        for hf in range(2):
            pp_ = pvv[hf]
            for kc in range(8):
                P.mm(pp_[:, :], XB[:, kc, :], win[kc][:, 1024 + hf * 512: 1024 + (hf + 1) * 512], start=(kc == 0), stop=(kc == 7))
            P.tt(V1[:, hf * 512:(hf + 1) * 512], pp_[:, :], bv[:, hf * 512:(hf + 1) * 512], ALU.add)
        P.act(V1[:], V1[:], AF.Gelu)
        P.reduce(st[:, 0:1], V1[:], ALU.add)
        P.ts(st[:, 1:2], st[:, 0:1], -1.0 / 1024.0, ALU.mult)
        P.ts(V1[:], V1[:], st[:, 1:2], ALU.add)
        P.stt(junk[:], V1[:], 1.0, V1[:], ALU.mult, ALU.mult, accum_out=st[:, 2:3])
        P.act(st[:, 3:4], st[:, 2:3], AF.Sqrt, bias=P.const(1e-5)[:, 0:1], scale=1.0 / 1024.0)
        P.recip(st[:, 3:4], st[:, 3:4])
        P.stt(V1[:], V1[:], st[:, 3:4], lg[:], ALU.mult, ALU.mult)
        P.tt(VN[:], V1[:], lb[:], ALU.add, e="pool")
        for gi in range(8):
            pp_ = pm[gi % 2]
            P.mm(pp_[:, 0:TT], VN[:, gi * 128:(gi + 1) * 128], wsb[:, gi, :])
            m = mx[gi % 2]
            P.tt(m[:], pp_[:, 0:TT], bs[:, gi * 128:(gi + 1) * 128], ALU.add)
            P.tt(GU[:, gi, :], m[:], U[:, gi, :], ALU.mult, e="pool")
        for oc in range(8):
            pp_ = pu[oc % 2]
            for c in range(8):
                P.mm(pp_[:, 0:TT], wout[c][:, oc * 128:(oc + 1) * 128], GU[:, c, :], start=(c == 0), stop=(c == 7))
            P.stt(X[:, oc, :], X[:, oc, :], ALPHA, pp_[:, 0:TT], ALU.mult, ALU.add)
        ln_inplace(P, X, TT, onesm, g, b, pl, sq, stat)
        P.dma("sp", x1_ap[:, :, i * TT:(i + 1) * TT], X[:])
        if i == NTL - 1:
            P.dma("sp", halo_src.rearrange("p (c t) -> p c t", t=2), X[:, :, TT - 2:TT])
    P.phase_end()


def emit_ffn(P, x1_ap, halo_all, selp_ap, Wd, out_ap, xg_ap=None, TT=256):
    P.phase_begin()
    onesm = P.sb([128, 128], F32)
    P.memset(onesm[:], 1.0 / 1024.0)
    cw = P.sb([128, 3, NFC]); cb = P.sb([128, NFC]); g = P.sb([128, 8]); b = P.sb([128, 8]); selp = P.sb([128, 4])
    for t_, n in ((cw, "conv_w"), (cb, "conv_b"), (g, "ln_g"), (b, "ln_b")):
        P.dma("sp", t_[:], Wd[n])
    P.dma("sp", selp[:], selp_ap)
    wup = [P.sb([128, 2 * DFF], BF16) for _ in range(8)]
    wdn = [P.sb([128, D], BF16) for _ in range(NFC)]
    stg = [P.sb([128, 1408], F32) for _ in range(2)]
    dst, src = [], []
    for kc in range(8):
        for j in range(4):
            dst.append(wup[kc][:, j * 1408:(j + 1) * 1408]); src.append(Wd["w_up"][:, kc, j * 1408:(j + 1) * 1408])
    for c in range(NFC):
        dst.append(wdn[c][:, :]); src.append(Wd["w_down"][:, c, :])
    load_cast(P, dst, src, stg)
    xf = [P.sb([128, 8, TT], F32) for _ in range(2)]
    xb = [P.sb([128, 8, TT], BF16) for _ in range(2)]
    H = P.sb([128, NFC, TT], BF16)
    G = [P.sb([128, TT + 2], F32) for _ in range(2)]
    Tt = [P.sb([128, TT], F32) for _ in range(2)]
    sq = [P.sb([128, TT], F32) for _ in range(2)]
    stat = [P.sb([128, TT], F32) for _ in range(3)]
    carry = P.sb([128, NFC, 2], F32)
    HA = P.sb([128, 4, 16], F32); xh = P.sb([128, 16], F32); xhb = P.sb([128, 8, 2], BF16)
    pg = [P.ps([128, 512]) for _ in range(2)]
    pv = [P.ps([128, 512]) for _ in range(2)]
    pd = [P.ps([128, 512]) for _ in range(2)]
    pl = [P.ps([128, 512]) for _ in range(2)]
    NTL = NTOK // TT

    def load_tile(i):
        P.dma("sp", xf[i % 2][:], x1_ap[:, :, i * TT:(i + 1) * TT])
        P.copy(xb[i % 2][:], xf[i % 2][:], e="pool")

    load_tile(0)
    P.dma("sp", HA[:], halo_all.rearrange("r p f -> p r f"))
    P.ts(xh[:], HA[:, 0, :], selp[:, 0:1], ALU.mult)
    for r in range(1, 4):
        P.stt(xh[:], HA[:, r, :], selp[:, r:r + 1], xh[:], ALU.mult, ALU.add)
    P.copy(xhb[:].rearrange("p c t -> p (c t)"), xh[:])
    for c in range(NFC):
        pp_ = pg[c % 2]
        for kc in range(8):
            P.mm(pp_[:, 0:2], wup[kc][:, c * 128:(c + 1) * 128], xhb[:, kc, :], start=(kc == 0), stop=(kc == 7))
        P.copy(carry[:, c, :], pp_[:, 0:2])
    for i in range(NTL):
        X, XB = xf[i % 2], xb[i % 2]
        if i + 1 < NTL:
            load_tile(i + 1)
        for c in range(NFC):
            pgc, pvc = pg[c % 2], pv[c % 2]
            for kc in range(8):
                P.mm(pgc[:, 0:TT], wup[kc][:, c * 128:(c + 1) * 128], XB[:, kc, :], start=(kc == 0), stop=(kc == 7))
            for kc in range(8):
                P.mm(pvc[:, 0:TT], wup[kc][:, DFF + c * 128: DFF + (c + 1) * 128], XB[:, kc, :], start=(kc == 0), stop=(kc == 7))
            Gc, Tc = G[c % 2], Tt[c % 2]
            P.copy(Gc[:, 2:TT + 2], pgc[:, 0:TT], e="act")
            P.copy(Gc[:, 0:2], carry[:, c, :], e="pool")
            P.ts(Tc[:], Gc[:, 0:TT], cw[:, 0, c:c + 1], ALU.mult, cb[:, c:c + 1], ALU.add)
            P.stt(Tc[:], Gc[:, 1:TT + 1], cw[:, 1, c:c + 1], Tc[:], ALU.mult, ALU.add)
            P.stt(Tc[:], Gc[:, 2:TT + 2], cw[:, 2, c:c + 1], Tc[:], ALU.mult, ALU.add)
            P.copy(carry[:, c, :], Gc[:, TT:TT + 2], e="pool")
            P.act(Tc[:], Tc[:], AF.Silu)
            P.tt(H[:, c, :], Tc[:], pvc[:, 0:TT], ALU.mult)
        for oc in range(8):
            pp_ = pd[oc % 2]
            for c in range(NFC):
                P.mm(pp_[:, 0:TT], wdn[c][:, oc * 128:(oc + 1) * 128], H[:, c, :], start=(c == 0), stop=(c == NFC - 1))
            P.stt(X[:, oc, :], X[:, oc, :], ALPHA, pp_[:, 0:TT], ALU.mult, ALU.add)
        ln_inplace(P, X, TT, onesm, g, b, pl, sq, stat)
        P.dma("sp", out_ap[:, :, i * TT:(i + 1) * TT], X[:])
        if xg_ap is not None:
            P.copy(XB[:], X[:], e="pool")
            P.dma("sp", xg_ap[(i * TT) // 512][:, :, (i * TT) % 512:(i * TT) % 512 + TT], XB[:])
    P.phase_end()


def build_fused():
    P = Prog()
    nc = P.nc
    ext_shapes = {}

    def ext(name, shape):
        ext_shapes[name] = tuple(shape)
        return nc.dram_tensor(name, list(shape), F32, kind="ExternalInput")[:]

    x0 = ext("x0", [128, 8, NTOK])
    selp = ext("selp", [128, 4])
    G_ = {n: ext(n, s) for n, s in (("ident", [128, 128]), ("hblk", [128, 128]), ("hblk64", [128, 128]), ("rotm", [128, 128]),
                                    ("tri", [128, 128]), ("sel", [64, 65]), ("TRI2", [64, 128]), ("TGT", [64, 64]),
                                    ("MASK2", [64, 128]), ("MASKT", [64, 64]), ("IDENT", [64, 64]),
                                    ("cosq", [128, T_SEQ]), ("sinq", [128, T_SEQ]), ("cosk", [128, T_SEQ]), ("sink", [128, T_SEQ]))}
    Wa, Wc, Wf = [], [], []
    for k in range(2):
        d = dict(G_)
        for n, s in (("w_in", [128, 8, 1024]), ("pp", [128, 10]), ("w2p", [128, 128]), ("a2p", [128, 128]), ("g2", [128, 128]),
                     ("lamv", [1, 256]), ("lam_init", [1, 1]), ("subln_g", [1, 128]), ("ppx", [128, 2]),
                     ("w_out", [128, 8, 1024]), ("ln_g", [128, 8]), ("ln_b", [128, 8])):
            d[n] = ext(f"a{k}_{n}", s)
        Wa.append(d)
        d = dict(G_)
        for n, s in (("w_in", [128, 8, 2048]), ("w_out", [128, 8, 1024]), ("wsT", [128, 8, 128]), ("b_u", [128, 8]),
                     ("b_v", [1, 1024]), ("cln_g", [1, 1024]), ("cln_b", [1, 1024]), ("b_s", [1, 1024]),
                     ("ln_g", [128, 8]), ("ln_b", [128, 8])):
            d[n] = ext(f"c{k}_{n}", s)
        Wc.append(d)
    for i in range(4):
        d = {}
        for n, s in (("w_up", [128, 8, 2 * DFF]), ("w_down", [128, NFC, D]), ("conv_w", [128, 3, NFC]), ("conv_b", [128, NFC]),
                     ("ln_g", [128, 8]), ("ln_b", [128, 8])):
            d[n] = ext(f"f{i}_{n}", s)
        Wf.append(d)
    out_d = nc.dram_tensor("x_out", [128, 8, NTOK], F32, kind="ExternalOutput")[:]

    T = T_SEQ
    S = {"fmp": nc.dram_tensor("s_fmp", [2, T // 64, 64, 256], F32), "tmp": nc.dram_tensor("s_tmp", [2, T // 64, 64, 256], F32),
         "qT": nc.dram_tensor("s_qT", [64, 2, T], F32), "kT": nc.dram_tensor("s_kT", [64, 2, T], F32),
         "vtm": nc.dram_tensor("s_vtm", [128, T // 128, 128], F32), "g": nc.dram_tensor("s_g", [128, T], F32),
         "bonus": nc.dram_tensor("s_bonus", [128, T], F32), "y": nc.dram_tensor("s_y", [2, 64, T], F32)}
    S = {k: v[:] for k, v in S.items()}
    xs = [nc.dram_tensor(f"s_x{k}", [128, 8, NTOK], F32)[:] for k in range(2)]
    x1s = nc.dram_tensor("s_xone", [128, 8, NTOK], F32)[:]
    xgs = [[nc.dram_tensor(f"s_xgs{k}_{j}", [1024, 512], BF16) for j in range(8)] for k in range(2)]
    xga = [[nc.dram_tensor(f"s_xga{k}_{j}", [4096, 512], BF16) for j in range(8)] for k in range(2)]
    mxs = [[nc.dram_tensor(f"s_mxs{k}_{j}", [256, 2048], BF16) for j in range(8)] for k in range(2)]
    mxa = [[nc.dram_tensor(f"s_mxa{k}_{j}", [1024, 2048], BF16) for j in range(8)] for k in range(2)]
    hls = [nc.dram_tensor(f"s_hls{k}", [128, 16], F32) for k in range(4)]
    hla = [nc.dram_tensor(f"s_hla{k}", [512, 16], F32) for k in range(4)]
    qoff = ext("selq", [128, 4])

    def xg_view(hs):
        return [h[:].rearrange("(p c) t -> p c t", c=8) for h in hs]

    emit_cast_x(P, x0, xg_view(xgs[0]))
    x_cur = x0
    for i in range(4):
        k = i // 2
        if i % 2 == 0:
            for j in range(8):
                P.allgather(xgs[k][j], xga[k][j], GROUPS)
            emit_abin_h(P, [h[:].rearrange("(r p c) t -> r p c t", r=4, c=8) for h in xga[k]], Wa[k], S)
            emit_scan(P, S, G_)
            emit_attn(P, S, Wa[k], [h[:] for h in mxs[k]])
            emit_post(P, S, Wa[k], [h[:] for h in mxs[k]])
            for j in range(8):
                P.allgather(mxs[k][j], mxa[k][j], GROUPS)
            emit_outproj(P, x_cur, [h[:].rearrange("(r f p) t -> r f p t", r=4, f=2) for h in mxa[k]], Wa[k], x1s, hls[i][:], qoff)
        else:
            emit_gmlp(P, x_cur, Wc[k], x1s, hls[i][:])
        P.allgather(hls[i], hla[i], GROUPS)
        out = out_d if i == 3 else xs[i % 2]
        emit_ffn(P, x1s, hla[i][:].rearrange("(r p) f -> r p f", r=4), selp, Wf[i], out,
                 xg_ap=(xg_view(xgs[1]) if i == 1 else None))
        x_cur = out
    return P.finish([]), ext_shapes


def _fm(a):
    T, C = a.shape
    return np.ascontiguousarray(a.T.reshape(C // 128, 128, T).transpose(1, 0, 2))


def _col(v, n):
    return np.ascontiguousarray(np.asarray(v, np.float32).reshape(n, 128).T)


def _wl(w):
    K, N = w.shape
    return np.ascontiguousarray(w.reshape(K // 128, 128, N).transpose(1, 0, 2))


def _rope_tabs(pos):
    inv = (500000.0 ** (-np.arange(0, 16, 2, dtype=np.float32) / 16)).astype(np.float32)
    ang = pos.astype(np.float32)[:, None] * inv[None, :]
    c, s = np.cos(ang).astype(np.float32), np.sin(ang).astype(np.float32)
    C = np.ones((128, len(pos)), np.float32)
    S = np.zeros((128, len(pos)), np.float32)
    for comp in range(2):
        for half in range(2):
            lo = comp * 64 + half * 8
            C[lo:lo + 8] = c.T
            S[lo:lo + 8] = s.T
    return C * np.float32(0.125), S * np.float32(0.125), C, S


def _rotm():
    R = np.zeros((128, 128), np.float32)
    for comp in range(2):
        for p in range(8):
            R[comp * 64 + p + 8, comp * 64 + p] = -1.0
            R[comp * 64 + p, comp * 64 + p + 8] = 1.0
    return R


def _consts():
    f32 = np.float32
    L = 64
    i = np.arange(L)[:, None]; t = np.arange(L)[None, :]
    incl = (i <= t).astype(f32); strict = (i < t).astype(f32); gt = (i > t).astype(f32)
    hb = np.zeros((128, 128), f32); hb[:64, :64] = 1; hb[64:, 64:] = 1
    kk_ = np.arange(128)[:, None]; qq_ = np.arange(128)[None, :]
    sel = np.zeros((64, 65), f32); sel[:, 64] = 1
    cq, sq, ck, sk = _rope_tabs(np.arange(T_SEQ))
    return {"ident": np.eye(128, dtype=f32), "hblk": hb, "hblk64": hb / f32(64.0), "rotm": _rotm(),
            "tri": (kk_ <= qq_).astype(f32), "sel": sel,
            "TRI2": np.concatenate([incl, strict], 1), "TGT": gt, "MASK2": np.concatenate([strict, incl], 1),
            "MASKT": np.ascontiguousarray(strict.T), "IDENT": np.eye(L, dtype=f32),
            "cosq": cq, "sinq": sq, "ck_": None, "cosk": ck, "sink": sk}


_NC = []


def kernel(**inp):
    f32 = np.float32
    inp = {k: np.asarray(v, f32) for k, v in inp.items()}
    if not _NC:
        _NC.append(build_fused())
    nc, shapes = _NC[0]
    cst = _consts()
    cst.pop("ck_")
    shared = dict(cst)
    for k in range(2):
        i = 2 * k
        lam_init = 0.8 - 0.6 * math.exp(-0.3 * i)
        shared[f"a{k}_lamv"] = np.concatenate([inp["ab_lam_q1"][k], inp["ab_lam_k1"][k], inp["ab_lam_q2"][k], inp["ab_lam_k2"][k]]).reshape(1, 256).astype(f32)
        shared[f"a{k}_lam_init"] = np.full((1, 1), lam_init, f32)
        shared[f"a{k}_subln_g"] = np.ascontiguousarray(inp["ab_subln_g"][k].reshape(1, 128))
        shared[f"a{k}_w_out"] = _wl(inp["ab_w_out"][k])
        shared[f"a{k}_ln_g"] = _col(inp["ln1_g"][i], 8); shared[f"a{k}_ln_b"] = _col(inp["ln1_b"][i], 8)
        i = 2 * k + 1
        b_in = inp["c_b_in"][k]
        shared[f"c{k}_w_in"] = _wl(inp["c_w_in"][k]); shared[f"c{k}_w_out"] = _wl(inp["c_w_out"][k])
        shared[f"c{k}_wsT"] = np.ascontiguousarray(inp["c_w_s"][k].transpose(2, 0, 1))
        shared[f"c{k}_b_u"] = _col(b_in[:1024], 8); shared[f"c{k}_b_v"] = np.ascontiguousarray(b_in[1024:].reshape(1, 1024))
        shared[f"c{k}_cln_g"] = np.ascontiguousarray(inp["c_ln_g"][k].reshape(1, 1024))
        shared[f"c{k}_cln_b"] = np.ascontiguousarray(inp["c_ln_b"][k].reshape(1, 1024))
        shared[f"c{k}_b_s"] = np.ascontiguousarray(inp["c_b_s"][k].reshape(1, 1024))
        shared[f"c{k}_ln_g"] = _col(inp["ln1_g"][i], 8); shared[f"c{k}_ln_b"] = _col(inp["ln1_b"][i], 8)
    for i in range(4):
        shared[f"f{i}_w_up"] = _wl(inp["ffn_w_up"][i]); shared[f"f{i}_w_down"] = _wl(inp["ffn_w_down"][i])
        shared[f"f{i}_conv_w"] = np.ascontiguousarray(inp["ffn_conv_w"][i].reshape(3, 22, 128).transpose(2, 0, 1))
        shared[f"f{i}_conv_b"] = _col(inp["ffn_conv_b"][i], 22)
        shared[f"f{i}_ln_g"] = _col(inp["ln2_g"][i], 8); shared[f"f{i}_ln_b"] = _col(inp["ln2_b"][i], 8)
    in_maps = []
    x = inp["x"]
    for c in range(8):
        b, q = divmod(c, 4)
        hp = q
        m = dict(shared)
        m["x0"] = _fm(x[b, q * NTOK:(q + 1) * NTOK])
        sp = np.zeros((128, 4), f32)
        if q > 0:
            sp[:, q - 1] = 1.0
        m["selp"] = sp
        sq_ = np.zeros((128, 4), f32); sq_[:, q] = 1.0
        m["selq"] = sq_
        hs = slice(hp * 128, (hp + 1) * 128)
        cols = np.concatenate([np.arange(hp * 128, hp * 128 + 128), 512 + np.arange(hp * 128, hp * 128 + 128),
                               1024 + np.arange(hp * 128, hp * 128 + 128), np.arange(1536, 1792),
                               1792 + np.arange(hp * 128, hp * 128 + 128), 2304 + np.arange(hp * 128, hp * 128 + 128),
                               2816 + np.arange(hp * 128, hp * 128 + 128)])
        for k in range(2):
            m[f"a{k}_w_in"] = _wl(np.ascontiguousarray(inp["ab_w_in"][k][:, cols]))
            mu = inp["ab_shift_mu"][k][cols[:640]]
            m[f"a{k}_pp"] = np.ascontiguousarray(np.concatenate(
                [_col(mu, 5), _col(inp["ab_w0"][k][hs], 1), _col(inp["ab_a0"][k][hs], 1), _col(inp["ab_k_k"][k][hs], 1),
                 _col(inp["ab_k_a"][k][hs], 1), _col(inp["ab_r_k"][k].reshape(-1)[hs], 1)], axis=1))
            w2p = np.zeros((128, 128), f32); w2p[:64] = inp["ab_w2"][k][:, hs]
            a2p = np.zeros((128, 128), f32); a2p[64:] = inp["ab_a2"][k][:, hs]
            m[f"a{k}_w2p"] = w2p; m[f"a{k}_a2p"] = a2p
            m[f"a{k}_g2"] = np.ascontiguousarray(inp["ab_g2"][k][:, hs])
            m[f"a{k}_ppx"] = np.ascontiguousarray(np.concatenate([_col(inp["ab_lnx_g"][k][hs], 1), _col(inp["ab_lnx_b"][k][hs], 1)], axis=1))
        for n, s_ in shapes.items():
            assert m[n].shape == s_, (n, m[n].shape, s_)
        in_maps.append({n: m[n] for n in shapes})
    res = run_bass_kernel_spmd(nc, in_maps, core_ids=list(range(8))).results
    out = np.empty((2, T_SEQ, 1024), f32)
    for c in range(8):
        b, q = divmod(c, 4)
        out[b, q * NTOK:(q + 1) * NTOK] = res[c]["x_out"].transpose(1, 0, 2).reshape(1024, NTOK).T
    return out
```

```python
import math
import numpy as np
from contextlib import ExitStack
import concourse.bass as bass
import concourse.mybir as mybir
from concourse.bass_utils import run_bass_kernel_spmd

F32 = mybir.dt.float32
BF16 = mybir.dt.bfloat16
AF = mybir.ActivationFunctionType
ALU = mybir.AluOpType
AX = mybir.AxisListType

EPOCH = 12000
NDSLOT = 24


class Tk:
    def __init__(self, h, name):
        self.h = h
        self.name = name
        self.lw = None
        self.rd = {}

    def __getitem__(self, idx):
        return V(self, self.h[idx])

    def ap(self):
        return V(self, self.h[:])


class V:
    def __init__(self, tk, ap):
        self.tk = tk
        self.ap = ap

    def __getitem__(self, idx):
        return V(self.tk, self.ap[idx])

    def rearrange(self, *a, **k):
        return V(self.tk, self.ap.rearrange(*a, **k))

    def bitcast(self, dt):
        return V(self.tk, self.ap.bitcast(dt))

    def to_broadcast(self, shape):
        return V(self.tk, self.ap.to_broadcast(shape))


def _ap(x):
    return x.ap if isinstance(x, V) else x


class Prog:
    def __init__(self, name="k"):
        self.nc = bass.Bass("TRN2", target_bir_lowering=False)
        self.es = ExitStack()
        nc = self.nc
        self.engs = {"pe": nc.tensor, "act": nc.scalar, "dve": nc.vector,
                     "pool": nc.gpsimd, "sp": nc.sync}
        self.cnt = {k: 0 for k in self.engs}
        self.sems = {k: [] for k in self.engs}
        self.waited = {k: {} for k in self.engs}
        self.dsem = [self.es.enter_context(nc.semaphore(f"d{i}")) for i in range(NDSLOT)]
        self.dval = [0] * NDSLOT
        self.dslot = 0
        self.nt = 0
        self.out_tokens = []
        self.pes = None
        self.csems = []

    def sb(self, shape, dt=F32, name=None):
        self.nt += 1
        name = name or f"t{self.nt}"
        h = (self.pes or self.es).enter_context(self.nc.sbuf_tensor(name, list(shape), dt))
        return Tk(h, name)

    def ps(self, shape, dt=F32, name=None):
        self.nt += 1
        name = name or f"p{self.nt}"
        h = (self.pes or self.es).enter_context(self.nc.psum_tensor(name, list(shape), dt))
        return Tk(h, name)

    def dram(self, name, shape, dt=F32, kind="Internal"):
        h = self.nc.dram_tensor(name, list(shape), dt, kind=kind)
        return Tk(h, name)

    def _sem(self, key, val):
        if isinstance(key, tuple):
            if key[0] == "c":
                return self.csems[key[1]], val
            return self.dsem[key[1]], val
        ep = (val - 1) // EPOCH
        lst = self.sems[key]
        while len(lst) <= ep:
            lst.append(self.es.enter_context(self.nc.semaphore(f"s_{key}_{len(lst)}")))
        return lst[ep], (val - 1) % EPOCH + 1

    def _deps(self, e, reads, writes, pe_acc=False):
        deps = {}

        def add(tok):
            if tok is None:
                return
            k, v = tok
            if deps.get(k, 0) < v:
                deps[k] = v

        for x in reads:
            if isinstance(x, V):
                add(x.tk.lw)
        for x in writes:
            if isinstance(x, V):
                lw = x.tk.lw
                if not (pe_acc and lw is not None and lw[0] == "pe"):
                    add(lw)
                for k, v in x.tk.rd.items():
                    add((k, v))
        w = self.waited[e]
        eng = self.engs[e]
        for k, v in deps.items():
            if w.get(k, 0) >= v:
                continue
            w[k] = v
            sem, sv = self._sem(k, v)
            eng.wait_ge(sem, sv)

    def _mark(self, tok, reads, writes):
        k, v = tok
        for x in reads:
            if isinstance(x, V):
                if x.tk.rd.get(k, 0) < v:
                    x.tk.rd[k] = v
        for x in writes:
            if isinstance(x, V):
                x.tk.lw = tok
                x.tk.rd = {}

    def emit(self, e, fn, reads, writes, pe_acc=False):
        self._deps(e, reads, writes, pe_acc)
        ins = fn(self.engs[e])
        self.cnt[e] += 1
        tok = (e, self.cnt[e])
        sem, sv = self._sem(e, self.cnt[e])
        ins.then_inc(sem, 1)
        self._mark(tok, reads, writes)
        return tok

    def dma(self, q, out, in_, **kw):
        reads, writes = [in_], [out]
        self._deps(q, reads, writes)
        slot = self.dslot
        self.dslot = (slot + 1) % NDSLOT
        key = ("d", slot)
        prev = self.dval[slot]
        w = self.waited[q]
        if prev > 0 and w.get(key, 0) < prev:
            w[key] = prev
            self.engs[q].wait_ge(self.dsem[slot], prev)
        ins = self.engs[q].dma_start(out=_ap(out), in_=_ap(in_), **kw)
        ins.then_inc(self.dsem[slot], 16)
        self.dval[slot] += 16
        tok = (key, self.dval[slot])
        self._mark(tok, reads, writes)
        return tok

    def phase_begin(self):
        self.pes = ExitStack()
        self._consts = {}

    def barrier(self):
        toks = [(k, v) for k, v in self.cnt.items() if v > 0]
        toks += [(("d", i), v) for i, v in enumerate(self.dval) if v > 0]
        for e in self.engs:
            for t in toks:
                if t[0] != e:
                    self.wait_tok(e, t)

    def phase_end(self):
        self.barrier()
        self.pes.close()
        self.pes = None
        self._consts = {}

    def allgather(self, src_h, dst_h, groups):
        if not self.csems:
            self.csems.append(self.es.enter_context(self.nc.semaphore("ccsem")))
            self.ccount = 0
        ins = self.nc.gpsimd.collective_compute("AllGather", ALU.bypass, replica_groups=groups,
                                                ins=[src_h.ap().opt()], outs=[dst_h.ap().opt()])
        ins.then_inc(self.csems[0])
        self.ccount += 1
        for e in self.engs:
            self.engs[e].wait_ge(self.csems[0], self.ccount)

    def wait_tok(self, e, tok):
        k, v = tok
        w = self.waited[e]
        if w.get(k, 0) >= v:
            return
        w[k] = v
        sem, sv = self._sem(k, v)
        self.engs[e].wait_ge(sem, sv)

    def mm(self, out, lhsT, rhs, start=True, stop=True):
        return self.emit("pe", lambda E: E.matmul(_ap(out), _ap(lhsT), _ap(rhs), start=start, stop=stop),
                         [lhsT, rhs], [out], pe_acc=True)

    def transpose(self, out, in_, ident):
        return self.emit("pe", lambda E: E.transpose(_ap(out), _ap(in_), _ap(ident)),
                         [in_, ident], [out], pe_acc=True)

    def act(self, out, in_, func, bias=None, scale=1.0, accum_out=None, e="act"):
        reads = [in_]
        kw = {}
        if bias is not None:
            kw["bias"] = _ap(bias)
            reads.append(bias)
        if not isinstance(scale, (int, float)):
            reads.append(scale)
        kw["scale"] = _ap(scale)
        writes = [out]
        if accum_out is not None:
            kw["accum_out"] = _ap(accum_out)
            writes.append(accum_out)
        return self.emit(e, lambda E: E.activation(_ap(out), _ap(in_), func, **kw), reads, writes)

    def tt(self, out, in0, in1, op, e="dve"):
        return self.emit(e, lambda E: E.tensor_tensor(_ap(out), _ap(in0), _ap(in1), op), [in0, in1], [out])

    def ts(self, out, in0, s1, op0, s2=None, op1=None, e="dve", accum_out=None):
        reads = [in0, s1, s2]
        kw = {}
        writes = [out]
        if op1 is not None:
            kw["op1"] = op1
        if accum_out is not None:
            kw["accum_out"] = _ap(accum_out)
            writes.append(accum_out)
        return self.emit(e, lambda E: E.tensor_scalar(_ap(out), _ap(in0), _ap(s1), _ap(s2), op0, **kw),
                         reads, writes)

    def stt(self, out, in0, scalar, in1, op0, op1, e="dve", accum_out=None):
        kw = {}
        writes = [out]
        if accum_out is not None:
            kw["accum_out"] = _ap(accum_out)
            writes.append(accum_out)
        return self.emit(e, lambda E: E.scalar_tensor_tensor(_ap(out), _ap(in0), _ap(scalar), _ap(in1), op0, op1, **kw),
                         [in0, scalar, in1], writes)

    def copy(self, out, in_, e="dve"):
        if e == "act":
            return self.emit(e, lambda E: E.copy(_ap(out), _ap(in_)), [in_], [out])
        return self.emit(e, lambda E: E.tensor_copy(_ap(out), _ap(in_)), [in_], [out])

    def memset(self, out, val, e="dve"):
        return self.emit(e, lambda E: E.memset(_ap(out), val), [], [out])

    def reduce(self, out, in_, op, axis=AX.X, e="dve"):
        return self.emit(e, lambda E: E.tensor_reduce(_ap(out), _ap(in_), axis, op), [in_], [out])

    def const(self, val):
        if not hasattr(self, "_consts"):
            self._consts = {}
        if val not in self._consts:
            t = self.sb([128, 1], F32)
            self.memset(t[:], float(val), e="pool")
            self._consts[val] = t
        return self._consts[val]

    def recip(self, out, in_):
        return self.emit("dve", lambda E: E.reciprocal(_ap(out), _ap(in_)), [in_], [out])

    def bn_stats(self, out, in_):
        return self.emit("dve", lambda E: E.bn_stats(_ap(out), _ap(in_)), [in_], [out])

    def bn_aggr(self, out, in_):
        return self.emit("dve", lambda E: E.bn_aggr(_ap(out), _ap(in_)), [in_], [out])

    def finish(self, toks):
        for t in toks:
            self.wait_tok("sp", t)
        self.es.close()
        return self.nc


ALPHA = 8.0 ** 0.25
D = 1024
DFF = 2816
NFC = 22


def load_cast(P, dst_views, src_views, stg, i0=0):
    for i, (d, s) in enumerate(zip(dst_views, src_views)):
        st = stg[(i0 + i) % len(stg)]
        n = _ap(s).shape[-1]
        P.dma("sp", st[:, 0:n], s)
        P.copy(d, st[:, 0:n], e=("dve" if (i0 + i) % 2 else "pool"))
    return i0 + len(dst_views)


def ln_inplace(P, z, TT, onesm, g, b, pl, sq, stat, eps=1e-5, out_bf=None):
    for c in range(8):
        P.mm(pl[0][:, 0:TT], onesm[:], z[:, c, :], start=(c == 0), stop=(c == 7))
        s = sq[c % 2]
        P.act(s[:, 0:TT], z[:, c, :], AF.Square)
        P.mm(pl[1][:, 0:TT], onesm[:], s[:, 0:TT], start=(c == 0), stop=(c == 7))
    mean, msq, rstd = stat
    P.copy(mean[:, 0:TT], pl[0][:, 0:TT], e="act")
    P.tt(msq[:, 0:TT], mean[:, 0:TT], mean[:, 0:TT], ALU.mult, e="pool")
    P.tt(rstd[:, 0:TT], pl[1][:, 0:TT], msq[:, 0:TT], ALU.subtract)
    P.act(rstd[:, 0:TT], rstd[:, 0:TT], AF.Sqrt, bias=P.const(eps)[:, 0:1])
    P.recip(rstd[:, 0:TT], rstd[:, 0:TT])
    for c in range(8):
        P.tt(z[:, c, :], z[:, c, :], mean[:, 0:TT], ALU.subtract)
        P.tt(z[:, c, :], z[:, c, :], rstd[:, 0:TT], ALU.mult, e="pool")
        P.act(z[:, c, :], z[:, c, :], AF.Identity, bias=b[:, c:c + 1], scale=g[:, c:c + 1])


T_SEQ = 16384
NTOK = 4096
GROUPS = [[0, 1, 2, 3], [4, 5, 6, 7]]


def emit_cast_x(P, x_ap, xg_ap):
    P.phase_begin()
    TT = 512
    xf = [P.sb([128, 8, TT], F32) for _ in range(2)]
    xb = [P.sb([128, 8, TT], BF16) for _ in range(2)]
    for i in range(NTOK // TT):
        P.dma("sp", xf[i % 2][:], x_ap[:, :, i * TT:(i + 1) * TT])
        P.copy(xb[i % 2][:], xf[i % 2][:], e=("dve" if i % 2 else "pool"))
        P.dma("sp", xg_ap[i][:, :, :], xb[i % 2][:])
    P.phase_end()


def emit_abin_h(P, xg_all, Wd, S, TT=256):
    T = T_SEQ
    P.phase_begin()
    pp = P.sb([128, 10]); w2 = P.sb([128, 128]); a2 = P.sb([128, 128]); g2 = P.sb([128, 128])
    hblk = P.sb([128, 128]); rotm = P.sb([128, 128]); ident = P.sb([128, 128])
    for t_, n in ((pp, "pp"), (w2, "w2p"), (a2, "a2p"), (g2, "g2"), (hblk, "hblk"), (rotm, "rotm"), (ident, "ident")):
        P.dma("sp", t_[:], Wd[n])
    MU, W0, A0, KK_, KA, RK = 0, 5, 6, 7, 8, 9
    win = [P.sb([128, 1024], BF16) for _ in range(8)]
    stg = [P.sb([128, 1024], F32) for _ in range(2)]
    load_cast(P, [w[:, :] for w in win], [Wd["w_in"][:, kc, :] for kc in range(8)], stg)

    xb = [P.sb([128, 8, TT + 1], BF16) for _ in range(2)]
    prs = [P.sb([128, TT + 1], F32) for _ in range(2)]
    dd = [P.sb([128, TT], F32) for _ in range(2)]
    XS = P.sb([128, 2, TT], F32)
    KR = P.sb([128, TT], F32)
    FM4 = [P.sb([128, 4, TT], F32) for _ in range(2)]
    LW = [P.sb([128, TT], F32) for _ in range(2)]
    OV = [P.sb([128, TT], F32) for _ in range(2)]
    OG = [P.sb([128, TT], F32) for _ in range(2)]
    OB = [P.sb([128, TT], F32) for _ in range(2)]
    DQ = [P.sb([128, TT], F32) for _ in range(2)]
    DK = [P.sb([128, TT], F32) for _ in range(2)]
    DVt = P.sb([128, TT], F32)
    TM4 = [P.sb([128, 4, 128], F32) for _ in range(2)]
    VT = [P.sb([128, 128], F32) for _ in range(2)]
    TW = P.sb([128, TT], F32); SG = P.sb([128, TT], F32)
    tmp = [P.sb([128, TT], F32) for _ in range(4)]
    tabs = [P.sb([128, 4, TT], F32) for _ in range(2)]
    ps = [P.ps([128, 512]) for _ in range(8)]
    pi = [0]

    def nps():
        pi[0] += 1
        return ps[pi[0] % 8]

    NTL = T // TT
    TPR = NTOK // TT

    def load_tile(i):
        r, lo = divmod(i, TPR)
        lo *= TT
        X = xb[i % 2]
        P.dma("sp", X[:, :, 1:TT + 1], xg_all[lo // 512][r, :, :, lo % 512:lo % 512 + TT])
        if i == 0:
            P.memset(X[:, :, 0:1], 0.0, e="pool")
        else:
            P.copy(X[:, :, 0:1], xb[(i - 1) % 2][:, :, TT:TT + 1], e="pool")
        for j, n in enumerate(("cosq", "sinq", "cosk", "sink")):
            P.dma("sp", tabs[i % 2][:, j, :], Wd[n][:, i * TT:(i + 1) * TT])

    load_tile(0)
    for i in range(NTL):
        XB = xb[i % 2]; TB = tabs[i % 2]
        if i + 1 < NTL:
            load_tile(i + 1)
        F4 = FM4[i % 2]; lw_ = LW[i % 2]; ov = OV[i % 2]; og = OG[i % 2]; obn = OB[i % 2]; dq = DQ[i % 2]; dk = DK[i % 2]
        for c in range(5):
            p_ = nps()
            for kc in range(8):
                P.mm(p_[:, 0:TT + 1], win[kc][:, c * 128:(c + 1) * 128], XB[:, kc, :], start=(kc == 0), stop=(kc == 7))
            s_ = prs[c % 2]; d_ = dd[c % 2]
            P.copy(s_[:], p_[:, 0:TT + 1], e="act")
            P.tt(d_[:], s_[:, 0:TT], s_[:, 1:TT + 1], ALU.subtract)
            dst_ = (F4[:, 0, :], KR[:], ov[:], XS[:, 0, :], XS[:, 1, :])[c]
            P.stt(dst_, d_[:], pp[:, MU + c:MU + c + 1], s_[:, 1:TT + 1], ALU.mult, ALU.add)
        P.act(TW[:], XS[:, 0, :], AF.Tanh)
        P.act(SG[:], XS[:, 1, :], AF.Sigmoid)
        p_ = nps()
        P.mm(p_[:, 0:TT], w2[:], TW[:])
        t_ = tmp[0]
        P.act(t_[:], p_[:, 0:TT], AF.Sigmoid, bias=pp[:, W0:W0 + 1])
        P.ts(lw_[:], t_[:], -0.6065306597126334, ALU.mult, e="pool")
        p_ = nps()
        P.mm(p_[:, 0:TT], a2[:], XS[:, 0, :])
        A_ = tmp[1]
        P.act(A_[:], p_[:, 0:TT], AF.Sigmoid, bias=pp[:, A0:A0 + 1])
        p_ = nps()
        P.mm(p_[:, 0:TT], g2[:], SG[:])
        P.copy(og[:], p_[:, 0:TT], e="act")
        KKt = tmp[2]
        P.ts(KKt[:], KR[:], pp[:, KK_:KK_ + 1], ALU.mult)
        s2 = tmp[3]
        P.tt(s2[:], KKt[:], KKt[:], ALU.mult, e="pool")
        p_ = nps()
        P.mm(p_[:, 0:TT], hblk[:], s2[:])
        P.act(s2[:], p_[:, 0:TT], AF.Sqrt)
        P.ts(s2[:], s2[:], 1e-12, ALU.max)
        P.recip(s2[:], s2[:])
        P.tt(KKt[:], KKt[:], s2[:], ALU.mult)
        P.ts(F4[:, 2, :], KKt[:], -1.0, ALU.mult, e="pool")
        P.tt(F4[:, 3, :], KKt[:], A_[:], ALU.mult)
        P.ts(A_[:], A_[:], -1.0, ALU.add, pp[:, KA:KA + 1], ALU.mult)
        P.stt(F4[:, 1, :], A_[:], 1.0, KR[:], ALU.add, ALU.mult)
        P.stt(s2[:], F4[:, 0, :], pp[:, RK:RK + 1], F4[:, 1, :], ALU.mult, ALU.mult)
        p_ = nps()
        P.mm(p_[:, 0:TT], hblk[:], s2[:])
        P.tt(obn[:], p_[:, 0:TT], ov[:], ALU.mult)
        for c in range(3):
            p_ = nps()
            col = 640 + c * 128
            for kc in range(8):
                P.mm(p_[:, 0:TT], win[kc][:, col:col + 128], XB[:, kc, 1:TT + 1], start=(kc == 0), stop=(kc == 7))
            if c == 2:
                P.copy(DVt[:], p_[:, 0:TT], e="act")
                continue
            isq = c == 0
            qs = tmp[c % 2]
            P.copy(qs[:], p_[:, 0:TT], e="act")
            p2 = nps()
            P.mm(p2[:, 0:TT], rotm[:], qs[:])
            t2 = tmp[2 + c % 2]
            P.tt(t2[:], p2[:, 0:TT], TB[:, 1 if isq else 3, :], ALU.mult)
            P.tt(qs[:], qs[:], TB[:, 0 if isq else 2, :], ALU.mult, e="pool")
            P.tt((dq if isq else dk)[:], qs[:], t2[:], ALU.add)
        c0 = i * (TT // 64)
        for h in range(2):
            for a in range(4):
                P.dma("sp", S["fmp"][h, c0:c0 + TT // 64, :, a * 64:(a + 1) * 64].rearrange("c k t -> k c t"),
                      F4[h * 64:(h + 1) * 64, a, :].rearrange("k (c t) -> k c t", t=64))
        for half in range(TT // 128):
            sl = slice(half * 128, (half + 1) * 128)
            pt = nps()
            for ai, src in enumerate((F4[:, 3, sl], F4[:, 1, sl], lw_[:, sl], ov[:, sl])):
                P.transpose(pt[:, ai * 128:(ai + 1) * 128], src, ident[:])
            tm = TM4[half % 2]
            P.copy(tm[:].rearrange("p a k -> p (a k)"), pt[:, 0:512], e=("act" if half % 2 else "dve"))
            for h in range(2):
                for cc in range(2):
                    P.dma("sp", S["tmp"][h, c0 + half * 2 + cc].rearrange("t (a k) -> t a k", a=4),
                          tm[cc * 64:(cc + 1) * 64, :, h * 64:(h + 1) * 64])
            pv_ = nps()
            P.transpose(pv_[:, 0:128], DVt[:, sl], ident[:])
            vt = VT[half % 2]
            P.copy(vt[:], pv_[:, 0:128], e="act")
            P.dma("sp", S["vtm"][:, i * (TT // 128) + half, :], vt[:])
        tsl = slice(i * TT, (i + 1) * TT)
        for cmp_ in range(2):
            P.dma("sp", S["qT"][:, cmp_, tsl], dq[cmp_ * 64:(cmp_ + 1) * 64, :])
            P.dma("sp", S["kT"][:, cmp_, tsl], dk[cmp_ * 64:(cmp_ + 1) * 64, :])
        P.dma("sp", S["g"][:, tsl], og[:])
        P.dma("sp", S["bonus"][:, tsl], obn[:])
    P.phase_end()


def emit_scan(P, S, Cd, NH=2, NCH=256):
    L = 64
    P.phase_begin()
    fmp_d, tmp_d, y_d = S["fmp"], S["tmp"], S["y"]
    C = {}
    for n in ("TRI2", "TGT", "MASK2", "MASKT", "IDENT"):
        C[n] = P.sb(list(Cd[n].shape), F32)
        P.dma("sp", C[n][:], Cd[n])
    NS = 2

    def mk():
        return dict(FM=P.sb([64, 256]), TM=P.sb([64, 256]), E12=P.sb([64, 128]), E3=P.sb([64, 64]), E4=P.sb([64, 64]),
                    AR=P.sb([64, 128]), BK=P.sb([64, 128]), BKT=P.sb([64, 128]), AB=P.sb([64, 128]), AK=P.sb([64, 128]),
                    X=P.sb([64, 64]), XT=P.sb([64, 64]), Tm=P.sb([64, 64]), Psb=P.sb([64, 64]), U=P.sb([64, 64]))
    sets = [[mk() for _ in range(NS)] for _ in range(NH)]
    ST = [P.sb([64, 64]) for _ in range(NH)]
    for h in range(NH):
        P.memset(ST[h][:], 0.0)
    YB = [[P.sb([64, 512]) for _ in range(2)] for _ in range(NH)]
    banks = [P.ps([128, 512]) for _ in range(8)]
    pi = [0]

    def nps():
        pi[0] += 1
        return banks[pi[0] % 8]

    def load(h, c):
        s = sets[h][c % NS]
        P.dma("sp", s["FM"][:], fmp_d[h, c])
        P.dma("sp", s["TM"][:], tmp_d[h, c])

    def prep(h, c):
        s = sets[h][c % NS]
        FM, TM = s["FM"], s["TM"]
        lw = TM[:, 128:192]
        cps = nps(); sfx = nps()
        P.mm(cps[0:64, 0:128], lw, C["TRI2"][:])
        yield
        P.mm(sfx[0:64, 0:64], C["TGT"][:], lw)
        yield
        P.act(s["E12"][:], cps[0:64, 0:128], AF.Exp)
        yield
        P.act(s["E3"][:], cps[0:64, 0:64], AF.Exp, scale=-1.0)
        yield
        P.act(s["E4"][:], sfx[0:64, 0:64], AF.Exp)
        yield
        P.tt(s["AR"][:, 0:64], FM[:, 128:192], s["E12"][:, 64:128], ALU.mult, e="pool")
        yield
        P.tt(s["AR"][:, 64:128], FM[:, 0:64], s["E12"][:, 0:64], ALU.mult, e="pool")
        yield
        P.tt(s["BK"][:, 0:64], FM[:, 192:256], s["E3"][:], ALU.mult, e="pool")
        yield
        P.tt(s["BK"][:, 64:128], FM[:, 64:128], s["E3"][:], ALU.mult, e="pool")
        yield
        P.tt(s["BKT"][:, 0:64], TM[:, 0:64], s["E4"][:], ALU.mult, e="pool")
        yield
        P.tt(s["BKT"][:, 64:128], TM[:, 64:128], s["E4"][:], ALU.mult, e="pool")
        yield
        pb = nps(); pk = nps(); pn = nps()
        P.mm(pb[0:64, 0:128], s["BK"][:, 0:64], s["AR"][:])
        yield
        P.mm(pk[0:64, 0:128], s["BK"][:, 64:128], s["AR"][:])
        yield
        P.mm(pn[0:64, 0:64], s["AR"][:, 0:64], s["BK"][:, 0:64])
        yield
        P.tt(s["AB"][:], pb[0:64, 0:128], C["MASK2"][:], ALU.mult)
        yield
        P.tt(s["AK"][:], pk[0:64, 0:128], C["MASK2"][:], ALU.mult)
        yield
        P.tt(s["XT"][:], pn[0:64, 0:64], C["MASKT"][:], ALU.mult)
        yield
        P.copy(s["X"][:], s["AB"][:, 0:64], e="pool")
        yield
        P.tt(s["Tm"][:], s["AB"][:, 0:64], C["IDENT"][:], ALU.add, e="pool")
        yield
        for j in range(1, 6):
            pxt = nps()
            P.mm(pxt[0:64, 0:64], s["X"][:], s["XT"][:])
            yield
            if j < 5:
                px = nps()
                P.mm(px[0:64, 0:64], s["XT"][:], s["X"][:])
                yield
                P.copy(s["X"][:], px[0:64, 0:64], e="act")
                yield
            P.copy(s["XT"][:], pxt[0:64, 0:64])
            yield
            pt = nps()
            P.mm(pt[0:64, 0:64], s["XT"][:], s["Tm"][:])
            yield
            P.tt(s["Tm"][:], pt[0:64, 0:64], s["Tm"][:], ALU.add)
            yield

    def rec(h, c):
        s = sets[h][c % NS]
        V_ = s["TM"][:, 192:256]
        pp = nps()
        P.mm(pp[0:64, 0:64], s["AR"][:, 0:64], ST[h][:], start=True, stop=False)
        P.mm(pp[0:64, 0:64], s["AK"][:, 0:64], V_, start=False, stop=True)
        yield
        P.copy(s["Psb"][:], pp[0:64, 0:64], e="act")
        yield
        pu = nps()
        P.mm(pu[0:64, 0:64], s["Tm"][:], s["Psb"][:])
        yield
        P.copy(s["U"][:], pu[0:64, 0:64], e="act")
        yield
        py = nps()
        P.mm(py[0:64, 0:64], ST[h][:], s["AR"][:, 64:128], start=True, stop=False)
        P.mm(py[0:64, 0:64], s["U"][:], s["AB"][:, 64:128], start=False, stop=False)
        P.mm(py[0:64, 0:64], V_, s["AK"][:, 64:128], start=False, stop=True)
        yield
        pd_ = nps()
        P.mm(pd_[0:64, 0:64], s["BKT"][:, 0:64], s["U"][:], start=True, stop=False)
        P.mm(pd_[0:64, 0:64], s["BKT"][:, 64:128], V_, start=False, stop=True)
        yield
        P.stt(ST[h][:], ST[h][:], s["E12"][:, 63:64], pd_[0:64, 0:64], ALU.mult, ALU.add)
        yield
        yb = YB[h][(c // 8) % 2]
        P.copy(yb[:, (c % 8) * 64:(c % 8 + 1) * 64], py[0:64, 0:64], e="act")
        yield
        if c % 8 == 7 or c == NCH - 1:
            c0 = (c // 8) * 8
            P.dma("sp", y_d[h, :, c0 * L:(c + 1) * L], yb[:, 0:(c - c0 + 1) * L])
            yield

    def run_il(gens):
        gens = list(gens)
        while gens:
            for g_ in list(gens):
                try:
                    next(g_)
                except StopIteration:
                    gens.remove(g_)

    for h in range(NH):
        load(h, 0)
    run_il([prep(h, 0) for h in range(NH)])
    for c in range(NCH):
        gl = []
        if c + 1 < NCH:
            for h in range(NH):
                load(h, c + 1)
            gl += [prep(h, c + 1) for h in range(NH)]
        gl += [rec(h, c) for h in range(NH)]
        run_il(gl)
    P.phase_end()


def emit_attn(P, S, Wd, mix_src):
    T = T_SEQ
    NB = T // 128
    NG = T // 512
    P.phase_begin()
    q_d, k_d, v_d = S["qT"], S["kT"], S["vtm"]
    lamv = P.sb([128, 256]); li = P.sb([128, 1]); gsc = P.sb([128, 128]); trif = P.sb([128, 128]); sel = P.sb([64, 65])
    ident = P.sb([128, 128])
    P.dma("sp", lamv[:], Wd["lamv"][0:1, :].to_broadcast([128, 256]))
    P.dma("sp", li[:], Wd["lam_init"][0:1, :].to_broadcast([128, 1]))
    P.dma("sp", gsc[:], Wd["subln_g"][0:1, :].to_broadcast([128, 128]))
    P.dma("sp", trif[:], Wd["tri"])
    P.dma("sp", sel[:], Wd["sel"])
    P.dma("sp", ident[:], Wd["ident"])
    tri = P.sb([128, 128], BF16)
    P.copy(tri[:], trif[:])
    sc = P.sb([128, 8]); junk = P.sb([128, 128])
    P.stt(junk[:, 0:64], lamv[:, 0:64], 1.0, lamv[:, 64:128], ALU.mult, ALU.mult, accum_out=sc[:, 0:1])
    P.stt(junk[:, 0:64], lamv[:, 128:192], 1.0, lamv[:, 192:256], ALU.mult, ALU.mult, accum_out=sc[:, 1:2])
    P.act(sc[:, 2:4], sc[:, 0:2], AF.Exp)
    P.tt(sc[:, 4:5], sc[:, 2:3], sc[:, 3:4], ALU.subtract)
    P.tt(sc[:, 4:5], sc[:, 4:5], li[:], ALU.add)
    P.ts(sc[:, 5:6], sc[:, 4:5], -1.0, ALU.mult)
    P.ts(sc[:, 6:7], li[:], -1.0, ALU.mult, 1.0, ALU.add)
    P.ts(gsc[:], gsc[:], sc[:, 6:7], ALU.mult)
    nlam = sc[:, 5:6]
    Ka = [P.sb([65, T], BF16) for _ in range(2)]
    Va = P.sb([128, NB, 129], BF16)
    for c in range(2):
        P.memset(Ka[c][64:65, :], 1.0, e="pool")
    P.memset(Va[:, :, 128:129], 1.0, e="pool")
    kst = P.sb([65, 8])
    P.memset(kst[64:65, :], 0.0)
    stg = [P.sb([128, 2048], F32) for _ in range(2)]
    sqb = [P.sb([64, 1024], F32) for _ in range(2)]
    pss = [P.ps([128, 512]) for _ in range(3)]
    pso = [P.ps([128, 512]) for _ in range(4)]
    pmisc = P.ps([128, 512])
    for i in range(NG):
        st = stg[i % 2]
        kf = st[0:64, 0:1024].rearrange("p (c t) -> p c t", c=2)
        P.dma("sp", kf, k_d[:, :, i * 512:(i + 1) * 512])
        s2 = sqb[i % 2]
        P.tt(s2[:], st[0:64, 0:1024], st[0:64, 0:1024], ALU.mult, e="pool")
        for c in range(2):
            P.copy(Ka[c][0:64, i * 512:(i + 1) * 512], st[0:64, c * 512:(c + 1) * 512], e="act")
            P.mm(pmisc[0:65, 0:512], sel[:], s2[:, c * 512:(c + 1) * 512])
            P.reduce(kst[64:65, 2 + c:3 + c], pmisc[64:65, 0:512], ALU.max)
            P.tt(kst[64:65, c:c + 1], kst[64:65, c:c + 1], kst[64:65, 2 + c:3 + c], ALU.max)
    VP = 16
    for i in range(NB // VP):
        st = stg[i % 2]
        P.dma("sp", st[:, 0:VP * 128].rearrange("p (n d) -> p n d", d=128), v_d[:, i * VP:(i + 1) * VP, :])
        P.copy(Va[:, i * VP:(i + 1) * VP, 0:128], st[:, 0:VP * 128].rearrange("p (n d) -> p n d", d=128), e=("dve" if i % 2 else "pool"))
    P.act(kst[64:65, 4:6], kst[64:65, 0:2], AF.Sqrt)
    P.ts(kst[64:65, 6:8], kst[64:65, 4:6], -1.0, ALU.mult)

    qf = [P.sb([64, 2, 512], F32) for _ in range(2)]
    qsq = P.sb([64, 2, 512], F32)
    nq = P.sb([65, 512], F32)
    Qa = [[P.sb([65, 512], BF16) for _ in range(2)] for _ in range(2)]
    PT = [P.sb([128, 512], BF16) for _ in range(3)]
    res = [P.sb([128, 128], F32) for _ in range(4)]
    ob = [P.sb([128, 128], F32) for _ in range(2)]
    obT = [P.sb([128, 128], BF16) for _ in range(2)]
    st2 = P.sb([128, 8], F32)
    P.dma("sp", qf[0][:], q_d[:, :, 0:512])
    cnt = 0
    for qg in range(NG):
        Qf = qf[qg % 2]
        if qg + 1 < NG:
            P.dma("sp", qf[(qg + 1) % 2][:], q_d[:, :, (qg + 1) * 512:(qg + 2) * 512])
        P.tt(qsq[:], Qf[:], Qf[:], ALU.mult, e="pool")
        Q = Qa[qg % 2]
        for c in range(2):
            P.copy(Q[c][0:64, :], Qf[:, c, :], e="pool")
            P.mm(pmisc[0:65, 0:512], sel[:], qsq[:, c, :])
            P.act(nq[64:65, :], pmisc[64:65, 0:512], AF.Sqrt)
            P.ts(Q[c][64:65, :], nq[64:65, :], kst[64:65, 6 + c:7 + c], ALU.mult)
        for c in range(2):
            nkb = (qg + 1) * 4
            for kb in range(nkb):
                j = kb - qg * 4
                q0 = max(j, 0) * 128
                ps_ = pss[cnt % 3]; pt = PT[cnt % 3]; cnt += 1
                P.mm(ps_[:, q0:512], Ka[c][:, kb * 128:(kb + 1) * 128], Q[c][:, q0:512])
                P.act(pt[:, q0:512], ps_[:, q0:512], AF.Exp)
                if j >= 0:
                    P.tt(pt[:, q0:q0 + 128], pt[:, q0:q0 + 128], tri[:], ALU.mult, e="dve")
                for qb in range(max(j, 0), 4):
                    P.mm(pso[qb][:, 0:129], pt[:, qb * 128:(qb + 1) * 128], Va[:, kb, :],
                         start=(kb == 0), stop=(kb == qg * 4 + qb))
            for qb in range(4):
                oo = pso[qb]
                if c == 0:
                    P.recip(st2[:, qb:qb + 1], oo[:, 128:129])
                    P.ts(res[qb][:], oo[:, 0:128], st2[:, qb:qb + 1], ALU.mult)
                else:
                    a_ = ob[qb % 2]
                    P.recip(st2[:, 4:5], oo[:, 128:129])
                    P.tt(st2[:, 5:6], st2[:, 4:5], nlam, ALU.mult)
                    P.stt(a_[:], oo[:, 0:128], st2[:, 5:6], res[qb][:], ALU.mult, ALU.add)
                    P.stt(junk[:], a_[:], 1.0, a_[:], ALU.mult, ALU.mult, accum_out=st2[:, 6:7])
                    P.act(st2[:, 7:8], st2[:, 6:7], AF.Sqrt, bias=P.const(1e-5)[:, 0:1], scale=1.0 / 128.0)
                    P.recip(st2[:, 7:8], st2[:, 7:8])
                    P.stt(a_[:], a_[:], st2[:, 7:8], gsc[:], ALU.mult, ALU.mult)
                    P.transpose(pmisc[:, 0:128], a_[:], ident[:])
                    aT = obT[qb % 2]
                    P.copy(aT[:], pmisc[:, 0:128])
                    blk = qg * 4 + qb
                    P.dma("sp", mix_src[blk // 16][128:256, (blk % 16) * 128:(blk % 16 + 1) * 128], aT[:])
    P.phase_end()


def emit_post(P, S, Wd, mix_src, TT=512):
    T = T_SEQ
    P.phase_begin()
    pp = P.sb([128, 2]); hb = P.sb([128, 128])
    P.dma("sp", pp[:], Wd["ppx"])
    P.dma("sp", hb[:], Wd["hblk64"])
    yf = [P.sb([128, TT], F32) for _ in range(2)]
    bf = [P.sb([128, TT], F32) for _ in range(2)]
    gf = [P.sb([128, TT], F32) for _ in range(2)]
    d_ = [P.sb([128, TT], F32) for _ in range(2)]
    q_ = [P.sb([128, TT], F32) for _ in range(2)]
    r_ = [P.sb([128, TT], F32) for _ in range(2)]
    mo = [P.sb([128, TT], BF16) for _ in range(2)]
    pm = [P.ps([128, 512]) for _ in range(2)]
    pv = [P.ps([128, 512]) for _ in range(2)]
    y2 = S["y"].rearrange("h k t -> (h k) t")

    def load(i):
        sl = slice(i * TT, (i + 1) * TT)
        P.dma("sp", yf[i % 2][:], y2[:, sl])
        P.dma("sp", bf[i % 2][:], S["bonus"][:, sl])
        P.dma("sp", gf[i % 2][:], S["g"][:, sl])

    load(0)
    for i in range(T // TT):
        if i + 1 < T // TT:
            load(i + 1)
        Y, B, G = yf[i % 2], bf[i % 2], gf[i % 2]
        d = d_[i % 2]; q = q_[i % 2]; r = r_[i % 2]
        P.mm(pm[i % 2][:, 0:TT], hb[:], Y[:])
        P.tt(d[:], Y[:], pm[i % 2][:, 0:TT], ALU.subtract)
        P.tt(q[:], d[:], d[:], ALU.mult, e="pool")
        P.mm(pv[i % 2][:, 0:TT], hb[:], q[:])
        P.act(r[:], pv[i % 2][:, 0:TT], AF.Sqrt, bias=P.const(64e-5)[:, 0:1])
        P.recip(r[:], r[:])
        P.tt(d[:], d[:], r[:], ALU.mult)
        P.act(d[:], d[:], AF.Identity, bias=pp[:, 1:2], scale=pp[:, 0:1])
        P.tt(d[:], d[:], B[:], ALU.add, e="pool")
        P.tt(mo[i % 2][:], d[:], G[:], ALU.mult)
        P.dma("sp", mix_src[(i * TT) // 2048][0:128, (i * TT) % 2048:(i * TT) % 2048 + TT], mo[i % 2][:])
    P.phase_end()


def emit_outproj(P, x_ap, mix_all, Wd, x1_ap, halo_src, qoff, TT=256):
    P.phase_begin()
    onesm = P.sb([128, 128], F32)
    P.memset(onesm[:], 1.0 / 1024.0)
    lg = P.sb([128, 8]); lb = P.sb([128, 8])
    P.dma("sp", lg[:], Wd["ln_g"]); P.dma("sp", lb[:], Wd["ln_b"])
    wout = [P.sb([128, 1024], BF16) for _ in range(8)]
    stg = [P.sb([128, 1024], F32) for _ in range(2)]
    load_cast(P, [w[:, :] for w in wout], [Wd["w_out"][:, c, :] for c in range(8)], stg)
    xf = [P.sb([128, 8, TT], F32) for _ in range(2)]
    MIX = [P.sb([128, 8, TT], BF16) for _ in range(2)]
    sq = [P.sb([128, TT], F32) for _ in range(2)]
    stat = [P.sb([128, TT], F32) for _ in range(3)]
    po = [P.ps([128, 512]) for _ in range(2)]
    pl = [P.ps([128, 512]) for _ in range(2)]
    NTL = NTOK // TT

    selq = P.sb([128, 4])
    P.dma("sp", selq[:], qoff)
    CAND = [[P.sb([128, 8, TT], BF16) for _ in range(4)] for _ in range(2)]

    def load(i):
        P.dma("sp", xf[i % 2][:], x_ap[:, :, i * TT:(i + 1) * TT])
        for qq in range(4):
            for hf in range(2):
                g_ = qq * NTOK + i * TT
                P.dma("sp", CAND[i % 2][qq][:, hf * 4:(hf + 1) * 4, :],
                      mix_all[g_ // 2048][:, hf, :, g_ % 2048: g_ % 2048 + TT].rearrange("r p t -> p r t"))
        M_ = MIX[i % 2]
        P.ts(M_[:], CAND[i % 2][0][:], selq[:, 0:1], ALU.mult)
        for qq in range(1, 4):
            P.stt(M_[:], CAND[i % 2][qq][:], selq[:, qq:qq + 1], M_[:], ALU.mult, ALU.add)

    load(0)
    for i in range(NTL):
        X, M = xf[i % 2], MIX[i % 2]
        if i + 1 < NTL:
            load(i + 1)
        for oc in range(8):
            p_ = po[oc % 2]
            for c in range(8):
                P.mm(p_[:, 0:TT], wout[c][:, oc * 128:(oc + 1) * 128], M[:, c, :], start=(c == 0), stop=(c == 7))
            P.stt(X[:, oc, :], X[:, oc, :], ALPHA, p_[:, 0:TT], ALU.mult, ALU.add)
        ln_inplace(P, X, TT, onesm, lg, lb, pl, sq, stat)
        P.dma("sp", x1_ap[:, :, i * TT:(i + 1) * TT], X[:])
        if i == NTL - 1:
            P.dma("sp", halo_src.rearrange("p (c t) -> p c t", t=2), X[:, :, TT - 2:TT])
    P.phase_end()


def emit_gmlp(P, x_ap, Wd, x1_ap, halo_src):
    TT = 128
    P.phase_begin()
    onesm = P.sb([128, 128], F32)
    P.memset(onesm[:], 1.0 / 1024.0)
    bu = P.sb([128, 8]); g = P.sb([128, 8]); b = P.sb([128, 8])
    bv = P.sb([128, 1024]); lg = P.sb([128, 1024]); lb = P.sb([128, 1024]); bs = P.sb([128, 1024])
    for t_, n in ((bu, "b_u"), (g, "ln_g"), (b, "ln_b")):
        P.dma("sp", t_[:], Wd[n])
    for t_, n in ((bv, "b_v"), (lg, "cln_g"), (lb, "cln_b"), (bs, "b_s")):
        P.dma("sp", t_[:], Wd[n][0:1, :].to_broadcast([128, 1024]))
    mask = P.sb([128, 128], F32)
    P.dma("sp", mask[:], Wd["tri"])
    wsf = P.sb([128, 8, 128], F32)
    P.dma("sp", wsf[:], Wd["wsT"])
    wsb = P.sb([128, 8, 128], BF16)
    for gi in range(8):
        P.tt(wsb[:, gi, :], wsf[:, gi, :], mask[:], ALU.mult)
    win = [P.sb([128, 2048], BF16) for _ in range(8)]
    wout = [P.sb([128, 1024], BF16) for _ in range(8)]
    stg = [P.sb([128, 2048], F32) for _ in range(2)]
    load_cast(P, [w[:, :] for w in win] + [w[:, :] for w in wout],
              [Wd["w_in"][:, kc, :] for kc in range(8)] + [Wd["w_out"][:, kc, :] for kc in range(8)], stg)
    xf = [P.sb([128, 8, TT], F32) for _ in range(2)]
    xb = [P.sb([128, 8, TT], BF16) for _ in range(2)]
    U = P.sb([128, 8, TT], F32)
    V1 = P.sb([128, 1024], F32)
    junk = P.sb([128, 1024], F32)
    VN = P.sb([128, 1024], BF16)
    GU = P.sb([128, 8, TT], BF16)
    mx = [P.sb([128, TT], F32) for _ in range(2)]
    st = P.sb([128, 4], F32)
    sq = [P.sb([128, TT], F32) for _ in range(2)]
    stat = [P.sb([128, TT], F32) for _ in range(3)]
    pu = [P.ps([128, 512]) for _ in range(2)]
    pvv = [P.ps([128, 512]) for _ in range(2)]
    pm = [P.ps([128, 512]) for _ in range(2)]
    pl = [P.ps([128, 512]) for _ in range(2)]
    NTL = NTOK // TT

    def load_tile(i):
        P.dma("sp", xf[i % 2][:], x_ap[:, :, i * TT:(i + 1) * TT])
        P.copy(xb[i % 2][:], xf[i % 2][:], e="pool")

    load_tile(0)
    for i in range(NTL):
        X, XB = xf[i % 2], xb[i % 2]
        if i + 1 < NTL:
            load_tile(i + 1)
        for oc in range(8):
            pp_ = pu[oc % 2]
            for kc in range(8):
                P.mm(pp_[:, 0:TT], win[kc][:, oc * 128:(oc + 1) * 128], XB[:, kc, :], start=(kc == 0), stop=(kc == 7))
            P.act(U[:, oc, :], pp_[:, 0:TT], AF.Gelu, bias=bu[:, oc:oc + 1])
        for hf in range(2):
            pp_ = pvv[hf]
            for kc in range(8):
                P.mm(pp_[:, :], XB[:, kc, :], win[kc][:, 1024 + hf * 512: 1024 + (hf + 1) * 512], start=(kc == 0), stop=(kc == 7))
            P.tt(V1[:, hf * 512:(hf + 1) * 512], pp_[:, :], bv[:, hf * 512:(hf + 1) * 512], ALU.add)
        P.act(V1[:], V1[:], AF.Gelu)
        P.reduce(st[:, 0:1], V1[:], ALU.add)
        P.ts(st[:, 1:2], st[:, 0:1], -1.0 / 1024.0, ALU.mult)
        P.ts(V1[:], V1[:], st[:, 1:2], ALU.add)
        P.stt(junk[:], V1[:], 1.0, V1[:], ALU.mult, ALU.mult, accum_out=st[:, 2:3])
        P.act(st[:, 3:4], st[:, 2:3], AF.Sqrt, bias=P.const(1e-5)[:, 0:1], scale=1.0 / 1024.0)
        P.recip(st[:, 3:4], st[:, 3:4])
        P.stt(V1[:], V1[:], st[:, 3:4], lg[:], ALU.mult, ALU.mult)
        P.tt(VN[:], V1[:], lb[:], ALU.add, e="pool")
        for gi in range(8):
            pp_ = pm[gi % 2]
            P.mm(pp_[:, 0:TT], VN[:, gi * 128:(gi + 1) * 128], wsb[:, gi, :])
            m = mx[gi % 2]
            P.tt(m[:], pp_[:, 0:TT], bs[:, gi * 128:(gi + 1) * 128], ALU.add)
            P.tt(GU[:, gi, :], m[:], U[:, gi, :], ALU.mult, e="pool")
        for oc in range(8):
            pp_ = pu[oc % 2]
            for c in range(8):
                P.mm(pp_[:, 0:TT], wout[c][:, oc * 128:(oc + 1) * 128], GU[:, c, :], start=(c == 0), stop=(c == 7))
            P.stt(X[:, oc, :], X[:, oc, :], ALPHA, pp_[:, 0:TT], ALU.mult, ALU.add)
        ln_inplace(P, X, TT, onesm, g, b, pl, sq, stat)
        P.dma("sp", x1_ap[:, :, i * TT:(i + 1) * TT], X[:])
        if i == NTL - 1:
            P.dma("sp", halo_src.rearrange("p (c t) -> p c t", t=2), X[:, :, TT - 2:TT])
    P.phase_end()


def emit_ffn(P, x1_ap, halo_all, selp_ap, Wd, out_ap, xg_ap=None, TT=256):
    P.phase_begin()
    onesm = P.sb([128, 128], F32)
    P.memset(onesm[:], 1.0 / 1024.0)
    cw = P.sb([128, 3, NFC]); cb = P.sb([128, NFC]); g = P.sb([128, 8]); b = P.sb([128, 8]); selp = P.sb([128, 4])
    for t_, n in ((cw, "conv_w"), (cb, "conv_b"), (g, "ln_g"), (b, "ln_b")):
        P.dma("sp", t_[:], Wd[n])
    P.dma("sp", selp[:], selp_ap)
    wup = [P.sb([128, 2 * DFF], BF16) for _ in range(8)]
    wdn = [P.sb([128, D], BF16) for _ in range(NFC)]
    stg = [P.sb([128, 1408], F32) for _ in range(2)]
    dst, src = [], []
    for kc in range(8):
        for j in range(4):
            dst.append(wup[kc][:, j * 1408:(j + 1) * 1408]); src.append(Wd["w_up"][:, kc, j * 1408:(j + 1) * 1408])
    for c in range(NFC):
        dst.append(wdn[c][:, :]); src.append(Wd["w_down"][:, c, :])
    load_cast(P, dst, src, stg)
    xf = [P.sb([128, 8, TT], F32) for _ in range(2)]
    xb = [P.sb([128, 8, TT], BF16) for _ in range(2)]
    H = P.sb([128, NFC, TT], BF16)
    G = [P.sb([128, TT + 2], F32) for _ in range(2)]
    Tt = [P.sb([128, TT], F32) for _ in range(2)]
    sq = [P.sb([128, TT], F32) for _ in range(2)]
    stat = [P.sb([128, TT], F32) for _ in range(3)]
    carry = P.sb([128, NFC, 2], F32)
    HA = P.sb([128, 4, 16], F32); xh = P.sb([128, 16], F32); xhb = P.sb([128, 8, 2], BF16)
    pg = [P.ps([128, 512]) for _ in range(2)]
    pv = [P.ps([128, 512]) for _ in range(2)]
    pd = [P.ps([128, 512]) for _ in range(2)]
    pl = [P.ps([128, 512]) for _ in range(2)]
    NTL = NTOK // TT

    def load_tile(i):
        P.dma("sp", xf[i % 2][:], x1_ap[:, :, i * TT:(i + 1) * TT])
        P.copy(xb[i % 2][:], xf[i % 2][:], e="pool")

    load_tile(0)
    P.dma("sp", HA[:], halo_all.rearrange("r p f -> p r f"))
    P.ts(xh[:], HA[:, 0, :], selp[:, 0:1], ALU.mult)
    for r in range(1, 4):
        P.stt(xh[:], HA[:, r, :], selp[:, r:r + 1], xh[:], ALU.mult, ALU.add)
    P.copy(xhb[:].rearrange("p c t -> p (c t)"), xh[:])
    for c in range(NFC):
        pp_ = pg[c % 2]
        for kc in range(8):
            P.mm(pp_[:, 0:2], wup[kc][:, c * 128:(c + 1) * 128], xhb[:, kc, :], start=(kc == 0), stop=(kc == 7))
        P.copy(carry[:, c, :], pp_[:, 0:2])
    for i in range(NTL):
        X, XB = xf[i % 2], xb[i % 2]
        if i + 1 < NTL:
            load_tile(i + 1)
        for c in range(NFC):
            pgc, pvc = pg[c % 2], pv[c % 2]
            for kc in range(8):
                P.mm(pgc[:, 0:TT], wup[kc][:, c * 128:(c + 1) * 128], XB[:, kc, :], start=(kc == 0), stop=(kc == 7))
            for kc in range(8):
                P.mm(pvc[:, 0:TT], wup[kc][:, DFF + c * 128: DFF + (c + 1) * 128], XB[:, kc, :], start=(kc == 0), stop=(kc == 7))
            Gc, Tc = G[c % 2], Tt[c % 2]
            P.copy(Gc[:, 2:TT + 2], pgc[:, 0:TT], e="act")
            P.copy(Gc[:, 0:2], carry[:, c, :], e="pool")
            P.ts(Tc[:], Gc[:, 0:TT], cw[:, 0, c:c + 1], ALU.mult, cb[:, c:c + 1], ALU.add)
            P.stt(Tc[:], Gc[:, 1:TT + 1], cw[:, 1, c:c + 1], Tc[:], ALU.mult, ALU.add)
            P.stt(Tc[:], Gc[:, 2:TT + 2], cw[:, 2, c:c + 1], Tc[:], ALU.mult, ALU.add)
            P.copy(carry[:, c, :], Gc[:, TT:TT + 2], e="pool")
            P.act(Tc[:], Tc[:], AF.Silu)
            P.tt(H[:, c, :], Tc[:], pvc[:, 0:TT], ALU.mult)
        for oc in range(8):
            pp_ = pd[oc % 2]
            for c in range(NFC):
                P.mm(pp_[:, 0:TT], wdn[c][:, oc * 128:(oc + 1) * 128], H[:, c, :], start=(c == 0), stop=(c == NFC - 1))
            P.stt(X[:, oc, :], X[:, oc, :], ALPHA, pp_[:, 0:TT], ALU.mult, ALU.add)
        ln_inplace(P, X, TT, onesm, g, b, pl, sq, stat)
        P.dma("sp", out_ap[:, :, i * TT:(i + 1) * TT], X[:])
        if xg_ap is not None:
            P.copy(XB[:], X[:], e="pool")
            P.dma("sp", xg_ap[(i * TT) // 512][:, :, (i * TT) % 512:(i * TT) % 512 + TT], XB[:])
    P.phase_end()


def build_fused():
    P = Prog()
    nc = P.nc
    ext_shapes = {}

    def ext(name, shape):
        ext_shapes[name] = tuple(shape)
        return nc.dram_tensor(name, list(shape), F32, kind="ExternalInput")[:]

    x0 = ext("x0", [128, 8, NTOK])
    selp = ext("selp", [128, 4])
    G_ = {n: ext(n, s) for n, s in (("ident", [128, 128]), ("hblk", [128, 128]), ("hblk64", [128, 128]), ("rotm", [128, 128]),
                                    ("tri", [128, 128]), ("sel", [64, 65]), ("TRI2", [64, 128]), ("TGT", [64, 64]),
                                    ("MASK2", [64, 128]), ("MASKT", [64, 64]), ("IDENT", [64, 64]),
                                    ("cosq", [128, T_SEQ]), ("sinq", [128, T_SEQ]), ("cosk", [128, T_SEQ]), ("sink", [128, T_SEQ]))}
    Wa, Wc, Wf = [], [], []
    for k in range(2):
        d = dict(G_)
        for n, s in (("w_in", [128, 8, 1024]), ("pp", [128, 10]), ("w2p", [128, 128]), ("a2p", [128, 128]), ("g2", [128, 128]),
                     ("lamv", [1, 256]), ("lam_init", [1, 1]), ("subln_g", [1, 128]), ("ppx", [128, 2]),
                     ("w_out", [128, 8, 1024]), ("ln_g", [128, 8]), ("ln_b", [128, 8])):
            d[n] = ext(f"a{k}_{n}", s)
        Wa.append(d)
        d = dict(G_)
        for n, s in (("w_in", [128, 8, 2048]), ("w_out", [128, 8, 1024]), ("wsT", [128, 8, 128]), ("b_u", [128, 8]),
                     ("b_v", [1, 1024]), ("cln_g", [1, 1024]), ("cln_b", [1, 1024]), ("b_s", [1, 1024]),
                     ("ln_g", [128, 8]), ("ln_b", [128, 8])):
            d[n] = ext(f"c{k}_{n}", s)
        Wc.append(d)
    for i in range(4):
        d = {}
        for n, s in (("w_up", [128, 8, 2 * DFF]), ("w_down", [128, NFC, D]), ("conv_w", [128, 3, NFC]), ("conv_b", [128, NFC]),
                     ("ln_g", [128, 8]), ("ln_b", [128, 8])):
            d[n] = ext(f"f{i}_{n}", s)
        Wf.append(d)
    out_d = nc.dram_tensor("x_out", [128, 8, NTOK], F32, kind="ExternalOutput")[:]

    T = T_SEQ
    S = {"fmp": nc.dram_tensor("s_fmp", [2, T // 64, 64, 256], F32), "tmp": nc.dram_tensor("s_tmp", [2, T // 64, 64, 256], F32),
         "qT": nc.dram_tensor("s_qT", [64, 2, T], F32), "kT": nc.dram_tensor("s_kT", [64, 2, T], F32),
         "vtm": nc.dram_tensor("s_vtm", [128, T // 128, 128], F32), "g": nc.dram_tensor("s_g", [128, T], F32),
         "bonus": nc.dram_tensor("s_bonus", [128, T], F32), "y": nc.dram_tensor("s_y", [2, 64, T], F32)}
    S = {k: v[:] for k, v in S.items()}
    xs = [nc.dram_tensor(f"s_x{k}", [128, 8, NTOK], F32)[:] for k in range(2)]
    x1s = nc.dram_tensor("s_xone", [128, 8, NTOK], F32)[:]
    xgs = [[nc.dram_tensor(f"s_xgs{k}_{j}", [1024, 512], BF16) for j in range(8)] for k in range(2)]
    xga = [[nc.dram_tensor(f"s_xga{k}_{j}", [4096, 512], BF16) for j in range(8)] for k in range(2)]
    mxs = [[nc.dram_tensor(f"s_mxs{k}_{j}", [256, 2048], BF16) for j in range(8)] for k in range(2)]
    mxa = [[nc.dram_tensor(f"s_mxa{k}_{j}", [1024, 2048], BF16) for j in range(8)] for k in range(2)]
    hls = [nc.dram_tensor(f"s_hls{k}", [128, 16], F32) for k in range(4)]
    hla = [nc.dram_tensor(f"s_hla{k}", [512, 16], F32) for k in range(4)]
    qoff = ext("selq", [128, 4])

    def xg_view(hs):
        return [h[:].rearrange("(p c) t -> p c t", c=8) for h in hs]

    emit_cast_x(P, x0, xg_view(xgs[0]))
    x_cur = x0
    for i in range(4):
        k = i // 2
        if i % 2 == 0:
            for j in range(8):
                P.allgather(xgs[k][j], xga[k][j], GROUPS)
            emit_abin_h(P, [h[:].rearrange("(r p c) t -> r p c t", r=4, c=8) for h in xga[k]], Wa[k], S)
            emit_scan(P, S, G_)
            emit_attn(P, S, Wa[k], [h[:] for h in mxs[k]])
            emit_post(P, S, Wa[k], [h[:] for h in mxs[k]])
            for j in range(8):
                P.allgather(mxs[k][j], mxa[k][j], GROUPS)
            emit_outproj(P, x_cur, [h[:].rearrange("(r f p) t -> r f p t", r=4, f=2) for h in mxa[k]], Wa[k], x1s, hls[i][:], qoff)
        else:
            emit_gmlp(P, x_cur, Wc[k], x1s, hls[i][:])
        P.allgather(hls[i], hla[i], GROUPS)
        out = out_d if i == 3 else xs[i % 2]
        emit_ffn(P, x1s, hla[i][:].rearrange("(r p) f -> r p f", r=4), selp, Wf[i], out,
                 xg_ap=(xg_view(xgs[1]) if i == 1 else None))
        x_cur = out
    return P.finish([]), ext_shapes


def _fm(a):
    T, C = a.shape
    return np.ascontiguousarray(a.T.reshape(C // 128, 128, T).transpose(1, 0, 2))


def _col(v, n):
    return np.ascontiguousarray(np.asarray(v, np.float32).reshape(n, 128).T)


def _wl(w):
    K, N = w.shape
    return np.ascontiguousarray(w.reshape(K // 128, 128, N).transpose(1, 0, 2))


def _rope_tabs(pos):
    inv = (500000.0 ** (-np.arange(0, 16, 2, dtype=np.float32) / 16)).astype(np.float32)
    ang = pos.astype(np.float32)[:, None] * inv[None, :]
    c, s = np.cos(ang).astype(np.float32), np.sin(ang).astype(np.float32)
    C = np.ones((128, len(pos)), np.float32)
    S = np.zeros((128, len(pos)), np.float32)
    for comp in range(2):
        for half in range(2):
            lo = comp * 64 + half * 8
            C[lo:lo + 8] = c.T
            S[lo:lo + 8] = s.T
    return C * np.float32(0.125), S * np.float32(0.125), C, S


def _rotm():
    R = np.zeros((128, 128), np.float32)
    for comp in range(2):
        for p in range(8):
            R[comp * 64 + p + 8, comp * 64 + p] = -1.0
            R[comp * 64 + p, comp * 64 + p + 8] = 1.0
    return R


def _consts():
    f32 = np.float32
    L = 64
    i = np.arange(L)[:, None]; t = np.arange(L)[None, :]
    incl = (i <= t).astype(f32); strict = (i < t).astype(f32); gt = (i > t).astype(f32)
    hb = np.zeros((128, 128), f32); hb[:64, :64] = 1; hb[64:, 64:] = 1
    kk_ = np.arange(128)[:, None]; qq_ = np.arange(128)[None, :]
    sel = np.zeros((64, 65), f32); sel[:, 64] = 1
    cq, sq, ck, sk = _rope_tabs(np.arange(T_SEQ))
    return {"ident": np.eye(128, dtype=f32), "hblk": hb, "hblk64": hb / f32(64.0), "rotm": _rotm(),
            "tri": (kk_ <= qq_).astype(f32), "sel": sel,
            "TRI2": np.concatenate([incl, strict], 1), "TGT": gt, "MASK2": np.concatenate([strict, incl], 1),
            "MASKT": np.ascontiguousarray(strict.T), "IDENT": np.eye(L, dtype=f32),
            "cosq": cq, "sinq": sq, "ck_": None, "cosk": ck, "sink": sk}


_NC = []


def kernel(**inp):
    f32 = np.float32
    inp = {k: np.asarray(v, f32) for k, v in inp.items()}
    if not _NC:
        _NC.append(build_fused())
    nc, shapes = _NC[0]
    cst = _consts()
    cst.pop("ck_")
    shared = dict(cst)
    for k in range(2):
        i = 2 * k
        lam_init = 0.8 - 0.6 * math.exp(-0.3 * i)
        shared[f"a{k}_lamv"] = np.concatenate([inp["ab_lam_q1"][k], inp["ab_lam_k1"][k], inp["ab_lam_q2"][k], inp["ab_lam_k2"][k]]).reshape(1, 256).astype(f32)
        shared[f"a{k}_lam_init"] = np.full((1, 1), lam_init, f32)
        shared[f"a{k}_subln_g"] = np.ascontiguousarray(inp["ab_subln_g"][k].reshape(1, 128))
        shared[f"a{k}_w_out"] = _wl(inp["ab_w_out"][k])
        shared[f"a{k}_ln_g"] = _col(inp["ln1_g"][i], 8); shared[f"a{k}_ln_b"] = _col(inp["ln1_b"][i], 8)
        i = 2 * k + 1
        b_in = inp["c_b_in"][k]
        shared[f"c{k}_w_in"] = _wl(inp["c_w_in"][k]); shared[f"c{k}_w_out"] = _wl(inp["c_w_out"][k])
        shared[f"c{k}_wsT"] = np.ascontiguousarray(inp["c_w_s"][k].transpose(2, 0, 1))
        shared[f"c{k}_b_u"] = _col(b_in[:1024], 8); shared[f"c{k}_b_v"] = np.ascontiguousarray(b_in[1024:].reshape(1, 1024))
        shared[f"c{k}_cln_g"] = np.ascontiguousarray(inp["c_ln_g"][k].reshape(1, 1024))
        shared[f"c{k}_cln_b"] = np.ascontiguousarray(inp["c_ln_b"][k].reshape(1, 1024))
        shared[f"c{k}_b_s"] = np.ascontiguousarray(inp["c_b_s"][k].reshape(1, 1024))
        shared[f"c{k}_ln_g"] = _col(inp["ln1_g"][i], 8); shared[f"c{k}_ln_b"] = _col(inp["ln1_b"][i], 8)
    for i in range(4):
        shared[f"f{i}_w_up"] = _wl(inp["ffn_w_up"][i]); shared[f"f{i}_w_down"] = _wl(inp["ffn_w_down"][i])
        shared[f"f{i}_conv_w"] = np.ascontiguousarray(inp["ffn_conv_w"][i].reshape(3, 22, 128).transpose(2, 0, 1))
        shared[f"f{i}_conv_b"] = _col(inp["ffn_conv_b"][i], 22)
        shared[f"f{i}_ln_g"] = _col(inp["ln2_g"][i], 8); shared[f"f{i}_ln_b"] = _col(inp["ln2_b"][i], 8)
    in_maps = []
    x = inp["x"]
    for c in range(8):
        b, q = divmod(c, 4)
        hp = q
        m = dict(shared)
        m["x0"] = _fm(x[b, q * NTOK:(q + 1) * NTOK])
        sp = np.zeros((128, 4), f32)
        if q > 0:
            sp[:, q - 1] = 1.0
        m["selp"] = sp
        sq_ = np.zeros((128, 4), f32); sq_[:, q] = 1.0
        m["selq"] = sq_
        hs = slice(hp * 128, (hp + 1) * 128)
        cols = np.concatenate([np.arange(hp * 128, hp * 128 + 128), 512 + np.arange(hp * 128, hp * 128 + 128),
                               1024 + np.arange(hp * 128, hp * 128 + 128), np.arange(1536, 1792),
                               1792 + np.arange(hp * 128, hp * 128 + 128), 2304 + np.arange(hp * 128, hp * 128 + 128),
                               2816 + np.arange(hp * 128, hp * 128 + 128)])
        for k in range(2):
            m[f"a{k}_w_in"] = _wl(np.ascontiguousarray(inp["ab_w_in"][k][:, cols]))
            mu = inp["ab_shift_mu"][k][cols[:640]]
            m[f"a{k}_pp"] = np.ascontiguousarray(np.concatenate(
                [_col(mu, 5), _col(inp["ab_w0"][k][hs], 1), _col(inp["ab_a0"][k][hs], 1), _col(inp["ab_k_k"][k][hs], 1),
                 _col(inp["ab_k_a"][k][hs], 1), _col(inp["ab_r_k"][k].reshape(-1)[hs], 1)], axis=1))
            w2p = np.zeros((128, 128), f32); w2p[:64] = inp["ab_w2"][k][:, hs]
            a2p = np.zeros((128, 128), f32); a2p[64:] = inp["ab_a2"][k][:, hs]
            m[f"a{k}_w2p"] = w2p; m[f"a{k}_a2p"] = a2p
            m[f"a{k}_g2"] = np.ascontiguousarray(inp["ab_g2"][k][:, hs])
            m[f"a{k}_ppx"] = np.ascontiguousarray(np.concatenate([_col(inp["ab_lnx_g"][k][hs], 1), _col(inp["ab_lnx_b"][k][hs], 1)], axis=1))
        for n, s_ in shapes.items():
            assert m[n].shape == s_, (n, m[n].shape, s_)
        in_maps.append({n: m[n] for n in shapes})
    res = run_bass_kernel_spmd(nc, in_maps, core_ids=list(range(8))).results
    out = np.empty((2, T_SEQ, 1024), f32)
    for c in range(8):
        b, q = divmod(c, 4)
        out[b, q * NTOK:(q + 1) * NTOK] = res[c]["x_out"].transpose(1, 0, 2).reshape(1024, NTOK).T
    return out
```

```python
import math
import numpy as np
from contextlib import ExitStack
import concourse.bass as bass
import concourse.mybir as mybir
from concourse.bass_utils import run_bass_kernel_spmd

F32 = mybir.dt.float32
BF16 = mybir.dt.bfloat16
AF = mybir.ActivationFunctionType
ALU = mybir.AluOpType
AX = mybir.AxisListType

EPOCH = 12000
NDSLOT = 24


class Tk:
    def __init__(self, h, name):
        self.h = h
        self.name = name
        self.lw = None
        self.rd = {}

    def __getitem__(self, idx):
        return V(self, self.h[idx])

    def ap(self):
        return V(self, self.h[:])


class V:
    def __init__(self, tk, ap):
        self.tk = tk
        self.ap = ap

    def __getitem__(self, idx):
        return V(self.tk, self.ap[idx])

    def rearrange(self, *a, **k):
        return V(self.tk, self.ap.rearrange(*a, **k))

    def bitcast(self, dt):
        return V(self.tk, self.ap.bitcast(dt))

    def to_broadcast(self, shape):
        return V(self.tk, self.ap.to_broadcast(shape))


def _ap(x):
    return x.ap if isinstance(x, V) else x


class Prog:
    def __init__(self, name="k"):
        self.nc = bass.Bass("TRN2", target_bir_lowering=False)
        self.es = ExitStack()
        nc = self.nc
        self.engs = {"pe": nc.tensor, "act": nc.scalar, "dve": nc.vector,
                     "pool": nc.gpsimd, "sp": nc.sync}
        self.cnt = {k: 0 for k in self.engs}
        self.sems = {k: [] for k in self.engs}
        self.waited = {k: {} for k in self.engs}
        self.dsem = [self.es.enter_context(nc.semaphore(f"d{i}")) for i in range(NDSLOT)]
        self.dval = [0] * NDSLOT
        self.dslot = 0
        self.nt = 0
        self.out_tokens = []
        self.pes = None
        self.csems = []

    def sb(self, shape, dt=F32, name=None):
        self.nt += 1
        name = name or f"t{self.nt}"
        h = (self.pes or self.es).enter_context(self.nc.sbuf_tensor(name, list(shape), dt))
        return Tk(h, name)

    def ps(self, shape, dt=F32, name=None):
        self.nt += 1
        name = name or f"p{self.nt}"
        h = (self.pes or self.es).enter_context(self.nc.psum_tensor(name, list(shape), dt))
        return Tk(h, name)

    def dram(self, name, shape, dt=F32, kind="Internal"):
        h = self.nc.dram_tensor(name, list(shape), dt, kind=kind)
        return Tk(h, name)

    def _sem(self, key, val):
        if isinstance(key, tuple):
            if key[0] == "c":
                return self.csems[key[1]], val
            return self.dsem[key[1]], val
        ep = (val - 1) // EPOCH
        lst = self.sems[key]
        while len(lst) <= ep:
            lst.append(self.es.enter_context(self.nc.semaphore(f"s_{key}_{len(lst)}")))
        return lst[ep], (val - 1) % EPOCH + 1

    def _deps(self, e, reads, writes, pe_acc=False):
        deps = {}

        def add(tok):
            if tok is None:
                return
            k, v = tok
            if deps.get(k, 0) < v:
                deps[k] = v

        for x in reads:
            if isinstance(x, V):
                add(x.tk.lw)
        for x in writes:
            if isinstance(x, V):
                lw = x.tk.lw
                if not (pe_acc and lw is not None and lw[0] == "pe"):
                    add(lw)
                for k, v in x.tk.rd.items():
                    add((k, v))
        w = self.waited[e]
        eng = self.engs[e]
        for k, v in deps.items():
            if w.get(k, 0) >= v:
                continue
            w[k] = v
            sem, sv = self._sem(k, v)
            eng.wait_ge(sem, sv)

    def _mark(self, tok, reads, writes):
        k, v = tok
        for x in reads:
            if isinstance(x, V):
                if x.tk.rd.get(k, 0) < v:
                    x.tk.rd[k] = v
        for x in writes:
            if isinstance(x, V):
                x.tk.lw = tok
                x.tk.rd = {}

    def emit(self, e, fn, reads, writes, pe_acc=False):
        self._deps(e, reads, writes, pe_acc)
        ins = fn(self.engs[e])
        self.cnt[e] += 1
        tok = (e, self.cnt[e])
        sem, sv = self._sem(e, self.cnt[e])
        ins.then_inc(sem, 1)
        self._mark(tok, reads, writes)
        return tok

    def dma(self, q, out, in_, **kw):
        reads, writes = [in_], [out]
        self._deps(q, reads, writes)
        slot = self.dslot
        self.dslot = (slot + 1) % NDSLOT
        key = ("d", slot)
        prev = self.dval[slot]
        w = self.waited[q]
        if prev > 0 and w.get(key, 0) < prev:
            w[key] = prev
            self.engs[q].wait_ge(self.dsem[slot], prev)
        ins = self.engs[q].dma_start(out=_ap(out), in_=_ap(in_), **kw)
        ins.then_inc(self.dsem[slot], 16)
        self.dval[slot] += 16
        tok = (key, self.dval[slot])
        self._mark(tok, reads, writes)
        return tok

    def phase_begin(self):
        self.pes = ExitStack()
        self._consts = {}

    def barrier(self):
        toks = [(k, v) for k, v in self.cnt.items() if v > 0]
        toks += [(("d", i), v) for i, v in enumerate(self.dval) if v > 0]
        for e in self.engs:
            for t in toks:
                if t[0] != e:
                    self.wait_tok(e, t)

    def phase_end(self):
        self.barrier()
        self.pes.close()
        self.pes = None
        self._consts = {}

    def allgather(self, src_h, dst_h, groups):
        if not self.csems:
            self.csems.append(self.es.enter_context(self.nc.semaphore("ccsem")))
            self.ccount = 0
        ins = self.nc.gpsimd.collective_compute("AllGather", ALU.bypass, replica_groups=groups,
                                                ins=[src_h.ap().opt()], outs=[dst_h.ap().opt()])
        ins.then_inc(self.csems[0])
        self.ccount += 1
        for e in self.engs:
            self.engs[e].wait_ge(self.csems[0], self.ccount)

    def wait_tok(self, e, tok):
        k, v = tok
        w = self.waited[e]
        if w.get(k, 0) >= v:
            return
        w[k] = v
        sem, sv = self._sem(k, v)
        self.engs[e].wait_ge(sem, sv)

    def mm(self, out, lhsT, rhs, start=True, stop=True):
        return self.emit("pe", lambda E: E.matmul(_ap(out), _ap(lhsT), _ap(rhs), start=start, stop=stop),
                         [lhsT, rhs], [out], pe_acc=True)

    def transpose(self, out, in_, ident):
        return self.emit("pe", lambda E: E.transpose(_ap(out), _ap(in_), _ap(ident)),
                         [in_, ident], [out], pe_acc=True)

    def act(self, out, in_, func, bias=None, scale=1.0, accum_out=None, e="act"):
        reads = [in_]
        kw = {}
        if bias is not None:
            kw["bias"] = _ap(bias)
            reads.append(bias)
        if not isinstance(scale, (int, float)):
            reads.append(scale)
        kw["scale"] = _ap(scale)
        writes = [out]
        if accum_out is not None:
            kw["accum_out"] = _ap(accum_out)
            writes.append(accum_out)
        return self.emit(e, lambda E: E.activation(_ap(out), _ap(in_), func, **kw), reads, writes)

    def tt(self, out, in0, in1, op, e="dve"):
        return self.emit(e, lambda E: E.tensor_tensor(_ap(out), _ap(in0), _ap(in1), op), [in0, in1], [out])

    def ts(self, out, in0, s1, op0, s2=None, op1=None, e="dve", accum_out=None):
        reads = [in0, s1, s2]
        kw = {}
        writes = [out]
        if op1 is not None:
            kw["op1"] = op1
        if accum_out is not None:
            kw["accum_out"] = _ap(accum_out)
            writes.append(accum_out)
        return self.emit(e, lambda E: E.tensor_scalar(_ap(out), _ap(in0), _ap(s1), _ap(s2), op0, **kw),
                         reads, writes)

    def stt(self, out, in0, scalar, in1, op0, op1, e="dve", accum_out=None):
        kw = {}
        writes = [out]
        if accum_out is not None:
            kw["accum_out"] = _ap(accum_out)
            writes.append(accum_out)
        return self.emit(e, lambda E: E.scalar_tensor_tensor(_ap(out), _ap(in0), _ap(scalar), _ap(in1), op0, op1, **kw),
                         [in0, scalar, in1], writes)

    def copy(self, out, in_, e="dve"):
        if e == "act":
            return self.emit(e, lambda E: E.copy(_ap(out), _ap(in_)), [in_], [out])
        return self.emit(e, lambda E: E.tensor_copy(_ap(out), _ap(in_)), [in_], [out])

    def memset(self, out, val, e="dve"):
        return self.emit(e, lambda E: E.memset(_ap(out), val), [], [out])

    def reduce(self, out, in_, op, axis=AX.X, e="dve"):
        return self.emit(e, lambda E: E.tensor_reduce(_ap(out), _ap(in_), axis, op), [in_], [out])

    def const(self, val):
        if not hasattr(self, "_consts"):
            self._consts = {}
        if val not in self._consts:
            t = self.sb([128, 1], F32)
            self.memset(t[:], float(val), e="pool")
            self._consts[val] = t
        return self._consts[val]

    def recip(self, out, in_):
        return self.emit("dve", lambda E: E.reciprocal(_ap(out), _ap(in_)), [in_], [out])

    def bn_stats(self, out, in_):
        return self.emit("dve", lambda E: E.bn_stats(_ap(out), _ap(in_)), [in_], [out])

    def bn_aggr(self, out, in_):
        return self.emit("dve", lambda E: E.bn_aggr(_ap(out), _ap(in_)), [in_], [out])

    def finish(self, toks):
        for t in toks:
            self.wait_tok("sp", t)
        self.es.close()
        return self.nc


ALPHA = 8.0 ** 0.25
D = 1024
DFF = 2816
NFC = 22


def load_cast(P, dst_views, src_views, stg, i0=0):
    for i, (d, s) in enumerate(zip(dst_views, src_views)):
        st = stg[(i0 + i) % len(stg)]
        n = _ap(s).shape[-1]
        P.dma("sp", st[:, 0:n], s)
        P.copy(d, st[:, 0:n], e=("dve" if (i0 + i) % 2 else "pool"))
    return i0 + len(dst_views)


def ln_inplace(P, z, TT, onesm, g, b, pl, sq, stat, eps=1e-5, out_bf=None):
    for c in range(8):
        P.mm(pl[0][:, 0:TT], onesm[:], z[:, c, :], start=(c == 0), stop=(c == 7))
        s = sq[c % 2]
        P.act(s[:, 0:TT], z[:, c, :], AF.Square)
        P.mm(pl[1][:, 0:TT], onesm[:], s[:, 0:TT], start=(c == 0), stop=(c == 7))
    mean, msq, rstd = stat
    P.copy(mean[:, 0:TT], pl[0][:, 0:TT], e="act")
    P.tt(msq[:, 0:TT], mean[:, 0:TT], mean[:, 0:TT], ALU.mult, e="pool")
    P.tt(rstd[:, 0:TT], pl[1][:, 0:TT], msq[:, 0:TT], ALU.subtract)
    P.act(rstd[:, 0:TT], rstd[:, 0:TT], AF.Sqrt, bias=P.const(eps)[:, 0:1])
    P.recip(rstd[:, 0:TT], rstd[:, 0:TT])
    for c in range(8):
        P.tt(z[:, c, :], z[:, c, :], mean[:, 0:TT], ALU.subtract)
        P.tt(z[:, c, :], z[:, c, :], rstd[:, 0:TT], ALU.mult, e="pool")
        P.act(z[:, c, :], z[:, c, :], AF.Identity, bias=b[:, c:c + 1], scale=g[:, c:c + 1])


T_SEQ = 16384
NTOK = 4096
GROUPS = [[0, 1, 2, 3], [4, 5, 6, 7]]


def emit_cast_x(P, x_ap, xg_ap):
    P.phase_begin()
    TT = 512
    xf = [P.sb([128, 8, TT], F32) for _ in range(2)]
    xb = [P.sb([128, 8, TT], BF16) for _ in range(2)]
    for i in range(NTOK // TT):
        P.dma("sp", xf[i % 2][:], x_ap[:, :, i * TT:(i + 1) * TT])
        P.copy(xb[i % 2][:], xf[i % 2][:], e=("dve" if i % 2 else "pool"))
        P.dma("sp", xg_ap[i][:, :, :], xb[i % 2][:])
    P.phase_end()


def emit_abin_h(P, xg_all, Wd, S, TT=256):
    T = T_SEQ
    P.phase_begin()
    pp = P.sb([128, 10]); w2 = P.sb([128, 128]); a2 = P.sb([128, 128]); g2 = P.sb([128, 128])
    hblk = P.sb([128, 128]); rotm = P.sb([128, 128]); ident = P.sb([128, 128])
    for t_, n in ((pp, "pp"), (w2, "w2p"), (a2, "a2p"), (g2, "g2"), (hblk, "hblk"), (rotm, "rotm"), (ident, "ident")):
        P.dma("sp", t_[:], Wd[n])
    MU, W0, A0, KK_, KA, RK = 0, 5, 6, 7, 8, 9
    win = [P.sb([128, 1024], BF16) for _ in range(8)]
    stg = [P.sb([128, 1024], F32) for _ in range(2)]
    load_cast(P, [w[:, :] for w in win], [Wd["w_in"][:, kc, :] for kc in range(8)], stg)

    xb = [P.sb([128, 8, TT + 1], BF16) for _ in range(2)]
    prs = [P.sb([128, TT + 1], F32) for _ in range(2)]
    dd = [P.sb([128, TT], F32) for _ in range(2)]
    XS = P.sb([128, 2, TT], F32)
    KR = P.sb([128, TT], F32)
    FM4 = [P.sb([128, 4, TT], F32) for _ in range(2)]
    LW = [P.sb([128, TT], F32) for _ in range(2)]
    OV = [P.sb([128, TT], F32) for _ in range(2)]
    OG = [P.sb([128, TT], F32) for _ in range(2)]
    OB = [P.sb([128, TT], F32) for _ in range(2)]
    DQ = [P.sb([128, TT], F32) for _ in range(2)]
    DK = [P.sb([128, TT], F32) for _ in range(2)]
    DVt = P.sb([128, TT], F32)
    TM4 = [P.sb([128, 4, 128], F32) for _ in range(2)]
    VT = [P.sb([128, 128], F32) for _ in range(2)]
    TW = P.sb([128, TT], F32); SG = P.sb([128, TT], F32)
    tmp = [P.sb([128, TT], F32) for _ in range(4)]
    tabs = [P.sb([128, 4, TT], F32) for _ in range(2)]
    ps = [P.ps([128, 512]) for _ in range(8)]
    pi = [0]

    def nps():
        pi[0] += 1
        return ps[pi[0] % 8]

    NTL = T // TT
    TPR = NTOK // TT

    def load_tile(i):
        r, lo = divmod(i, TPR)
        lo *= TT
        X = xb[i % 2]
        P.dma("sp", X[:, :, 1:TT + 1], xg_all[lo // 512][r, :, :, lo % 512:lo % 512 + TT])
        if i == 0:
            P.memset(X[:, :, 0:1], 0.0, e="pool")
        else:
            P.copy(X[:, :, 0:1], xb[(i - 1) % 2][:, :, TT:TT + 1], e="pool")
        for j, n in enumerate(("cosq", "sinq", "cosk", "sink")):
            P.dma("sp", tabs[i % 2][:, j, :], Wd[n][:, i * TT:(i + 1) * TT])

    load_tile(0)
    for i in range(NTL):
        XB = xb[i % 2]; TB = tabs[i % 2]
        if i + 1 < NTL:
            load_tile(i + 1)
        F4 = FM4[i % 2]; lw_ = LW[i % 2]; ov = OV[i % 2]; og = OG[i % 2]; obn = OB[i % 2]; dq = DQ[i % 2]; dk = DK[i % 2]
        for c in range(5):
            p_ = nps()
            for kc in range(8):
                P.mm(p_[:, 0:TT + 1], win[kc][:, c * 128:(c + 1) * 128], XB[:, kc, :], start=(kc == 0), stop=(kc == 7))
            s_ = prs[c % 2]; d_ = dd[c % 2]
            P.copy(s_[:], p_[:, 0:TT + 1], e="act")
            P.tt(d_[:], s_[:, 0:TT], s_[:, 1:TT + 1], ALU.subtract)
            dst_ = (F4[:, 0, :], KR[:], ov[:], XS[:, 0, :], XS[:, 1, :])[c]
            P.stt(dst_, d_[:], pp[:, MU + c:MU + c + 1], s_[:, 1:TT + 1], ALU.mult, ALU.add)
        P.act(TW[:], XS[:, 0, :], AF.Tanh)
        P.act(SG[:], XS[:, 1, :], AF.Sigmoid)
        p_ = nps()
        P.mm(p_[:, 0:TT], w2[:], TW[:])
        t_ = tmp[0]
        P.act(t_[:], p_[:, 0:TT], AF.Sigmoid, bias=pp[:, W0:W0 + 1])
        P.ts(lw_[:], t_[:], -0.6065306597126334, ALU.mult, e="pool")
        p_ = nps()
        P.mm(p_[:, 0:TT], a2[:], XS[:, 0, :])
        A_ = tmp[1]
        P.act(A_[:], p_[:, 0:TT], AF.Sigmoid, bias=pp[:, A0:A0 + 1])
        p_ = nps()
        P.mm(p_[:, 0:TT], g2[:], SG[:])
        P.copy(og[:], p_[:, 0:TT], e="act")
        KKt = tmp[2]
        P.ts(KKt[:], KR[:], pp[:, KK_:KK_ + 1], ALU.mult)
        s2 = tmp[3]
        P.tt(s2[:], KKt[:], KKt[:], ALU.mult, e="pool")
        p_ = nps()
        P.mm(p_[:, 0:TT], hblk[:], s2[:])
        P.act(s2[:], p_[:, 0:TT], AF.Sqrt)
        P.ts(s2[:], s2[:], 1e-12, ALU.max)
        P.recip(s2[:], s2[:])
        P.tt(KKt[:], KKt[:], s2[:], ALU.mult)
        P.ts(F4[:, 2, :], KKt[:], -1.0, ALU.mult, e="pool")
        P.tt(F4[:, 3, :], KKt[:], A_[:], ALU.mult)
        P.ts(A_[:], A_[:], -1.0, ALU.add, pp[:, KA:KA + 1], ALU.mult)
        P.stt(F4[:, 1, :], A_[:], 1.0, KR[:], ALU.add, ALU.mult)
        P.stt(s2[:], F4[:, 0, :], pp[:, RK:RK + 1], F4[:, 1, :], ALU.mult, ALU.mult)
        p_ = nps()
        P.mm(p_[:, 0:TT], hblk[:], s2[:])
        P.tt(obn[:], p_[:, 0:TT], ov[:], ALU.mult)
        for c in range(3):
            p_ = nps()
            col = 640 + c * 128
            for kc in range(8):
                P.mm(p_[:, 0:TT], win[kc][:, col:col + 128], XB[:, kc, 1:TT + 1], start=(kc == 0), stop=(kc == 7))
            if c == 2:
                P.copy(DVt[:], p_[:, 0:TT], e="act")
                continue
            isq = c == 0
            qs = tmp[c % 2]
            P.copy(qs[:], p_[:, 0:TT], e="act")
            p2 = nps()
            P.mm(p2[:, 0:TT], rotm[:], qs[:])
            t2 = tmp[2 + c % 2]
            P.tt(t2[:], p2[:, 0:TT], TB[:, 1 if isq else 3, :], ALU.mult)
            P.tt(qs[:], qs[:], TB[:, 0 if isq else 2, :], ALU.mult, e="pool")
            P.tt((dq if isq else dk)[:], qs[:], t2[:], ALU.add)
        c0 = i * (TT // 64)
        for h in range(2):
            for a in range(4):
                P.dma("sp", S["fmp"][h, c0:c0 + TT // 64, :, a * 64:(a + 1) * 64].rearrange("c k t -> k c t"),
                      F4[h * 64:(h + 1) * 64, a, :].rearrange("k (c t) -> k c t", t=64))
        for half in range(TT // 128):
            sl = slice(half * 128, (half + 1) * 128)
            pt = nps()
            for ai, src in enumerate((F4[:, 3, sl], F4[:, 1, sl], lw_[:, sl], ov[:, sl])):
                P.transpose(pt[:, ai * 128:(ai + 1) * 128], src, ident[:])
            tm = TM4[half % 2]
            P.copy(tm[:].rearrange("p a k -> p (a k)"), pt[:, 0:512], e=("act" if half % 2 else "dve"))
            for h in range(2):
                for cc in range(2):
                    P.dma("sp", S["tmp"][h, c0 + half * 2 + cc].rearrange("t (a k) -> t a k", a=4),
                          tm[cc * 64:(cc + 1) * 64, :, h * 64:(h + 1) * 64])
            pv_ = nps()
            P.transpose(pv_[:, 0:128], DVt[:, sl], ident[:])
            vt = VT[half % 2]
            P.copy(vt[:], pv_[:, 0:128], e="act")
            P.dma("sp", S["vtm"][:, i * (TT // 128) + half, :], vt[:])
        tsl = slice(i * TT, (i + 1) * TT)
        for cmp_ in range(2):
            P.dma("sp", S["qT"][:, cmp_, tsl], dq[cmp_ * 64:(cmp_ + 1) * 64, :])
            P.dma("sp", S["kT"][:, cmp_, tsl], dk[cmp_ * 64:(cmp_ + 1) * 64, :])
        P.dma("sp", S["g"][:, tsl], og[:])
        P.dma("sp", S["bonus"][:, tsl], obn[:])
    P.phase_end()


def emit_scan(P, S, Cd, NH=2, NCH=256):
    L = 64
    P.phase_begin()
    fmp_d, tmp_d, y_d = S["fmp"], S["tmp"], S["y"]
    C = {}
    for n in ("TRI2", "TGT", "MASK2", "MASKT", "IDENT"):
        C[n] = P.sb(list(Cd[n].shape), F32)
        P.dma("sp", C[n][:], Cd[n])
    NS = 3

    def mk():
        return dict(FM=P.sb([64, 256]), TM=P.sb([64, 256]), E12=P.sb([64, 128]), E3=P.sb([64, 64]), E4=P.sb([64, 64]),
                    AR=P.sb([64, 128]), BK=P.sb([64, 128]), BKT=P.sb([64, 128]), AB=P.sb([64, 128]), AK=P.sb([64, 128]),
                    X=P.sb([64, 64]), XT=P.sb([64, 64]), Tm=P.sb([64, 64]), Psb=P.sb([64, 64]), U=P.sb([64, 64]))
    sets = [[mk() for _ in range(NS)] for _ in range(NH)]
    ST = [P.sb([64, 64]) for _ in range(NH)]
    for h in range(NH):
        P.memset(ST[h][:], 0.0)
    YB = [[P.sb([64, 512]) for _ in range(2)] for _ in range(NH)]
    banks = [P.ps([128, 512]) for _ in range(8)]
    free = list(range(8))

    def acquire(n):
        if len(free) < n:
            return None
        got = [free.pop(0) for _ in range(n)]
        return got

    def release(ids):
        free.extend(ids)

    def load(h, c):
        s = sets[h][c % NS]
        P.dma("sp", s["FM"][:], fmp_d[h, c])
        P.dma("sp", s["TM"][:], tmp_d[h, c])

    def prep(h, c):
        s = sets[h][c % NS]
        FM, TM = s["FM"], s["TM"]
        lw = TM[:, 128:192]
        while True:
            ids = acquire(2)
            if ids is not None:
                break
            yield
        cps, sfx = banks[ids[0]], banks[ids[1]]
        P.mm(cps[0:64, 0:128], lw, C["TRI2"][:]); yield
        P.mm(sfx[0:64, 0:64], C["TGT"][:], lw); yield
        P.act(s["E12"][:], cps[0:64, 0:128], AF.Exp); yield
        P.act(s["E3"][:], cps[0:64, 0:64], AF.Exp, scale=-1.0); yield
        P.act(s["E4"][:], sfx[0:64, 0:64], AF.Exp); yield
        release(ids)
        P.tt(s["AR"][:, 0:64], FM[:, 128:192], s["E12"][:, 64:128], ALU.mult, e="pool"); yield
        P.tt(s["AR"][:, 64:128], FM[:, 0:64], s["E12"][:, 0:64], ALU.mult, e="pool"); yield
        P.tt(s["BK"][:, 0:64], FM[:, 192:256], s["E3"][:], ALU.mult, e="pool"); yield
        P.tt(s["BK"][:, 64:128], FM[:, 64:128], s["E3"][:], ALU.mult, e="pool"); yield
        P.tt(s["BKT"][:, 0:64], TM[:, 0:64], s["E4"][:], ALU.mult, e="pool"); yield
        P.tt(s["BKT"][:, 64:128], TM[:, 64:128], s["E4"][:], ALU.mult, e="pool"); yield
        while True:
            ids = acquire(3)
            if ids is not None:
                break
            yield
        pb, pk, pn = (banks[i_] for i_ in ids)
        P.mm(pb[0:64, 0:128], s["BK"][:, 0:64], s["AR"][:]); yield
        P.mm(pk[0:64, 0:128], s["BK"][:, 64:128], s["AR"][:]); yield
        P.mm(pn[0:64, 0:64], s["AR"][:, 0:64], s["BK"][:, 0:64]); yield
        P.tt(s["AB"][:], pb[0:64, 0:128], C["MASK2"][:], ALU.mult); yield
        P.tt(s["AK"][:], pk[0:64, 0:128], C["MASK2"][:], ALU.mult); yield
        P.tt(s["XT"][:], pn[0:64, 0:64], C["MASKT"][:], ALU.mult); yield
        release(ids)
        P.copy(s["X"][:], s["AB"][:, 0:64], e="pool"); yield
        P.tt(s["Tm"][:], s["AB"][:, 0:64], C["IDENT"][:], ALU.add, e="pool"); yield
        for j in range(1, 6):
            while True:
                ids = acquire(3)
                if ids is not None:
                    break
                yield
            pxt, px, pt = (banks[i_] for i_ in ids)
            P.mm(pxt[0:64, 0:64], s["X"][:], s["XT"][:]); yield
            if j < 5:
                P.mm(px[0:64, 0:64], s["XT"][:], s["X"][:]); yield
                P.copy(s["X"][:], px[0:64, 0:64], e="act"); yield
            P.copy(s["XT"][:], pxt[0:64, 0:64]); yield
            P.mm(pt[0:64, 0:64], s["XT"][:], s["Tm"][:]); yield
            P.tt(s["Tm"][:], pt[0:64, 0:64], s["Tm"][:], ALU.add); yield
            release(ids)

    def rec(h, c):
        s = sets[h][c % NS]
        V_ = s["TM"][:, 192:256]
        while True:
            ids = acquire(2)
            if ids is not None:
                break
            yield
        pp, pu = banks[ids[0]], banks[ids[1]]
        P.mm(pp[0:64, 0:64], s["AR"][:, 0:64], ST[h][:], start=True, stop=False)
        P.mm(pp[0:64, 0:64], s["AK"][:, 0:64], V_, start=False, stop=True); yield
        P.copy(s["Psb"][:], pp[0:64, 0:64], e="act"); yield
        P.mm(pu[0:64, 0:64], s["Tm"][:], s["Psb"][:]); yield
        P.copy(s["U"][:], pu[0:64, 0:64], e="act"); yield
        release(ids)
        while True:
            ids = acquire(2)
            if ids is not None:
                break
            yield
        py, pd_ = banks[ids[0]], banks[ids[1]]
        P.mm(py[0:64, 0:64], ST[h][:], s["AR"][:, 64:128], start=True, stop=False)
        P.mm(py[0:64, 0:64], s["U"][:], s["AB"][:, 64:128], start=False, stop=False)
        P.mm(py[0:64, 0:64], V_, s["AK"][:, 64:128], start=False, stop=True); yield
        P.mm(pd_[0:64, 0:64], s["BKT"][:, 0:64], s["U"][:], start=True, stop=False)
        P.mm(pd_[0:64, 0:64], s["BKT"][:, 64:128], V_, start=False, stop=True); yield
        P.stt(ST[h][:], ST[h][:], s["E12"][:, 63:64], pd_[0:64, 0:64], ALU.mult, ALU.add); yield
        yb = YB[h][(c // 8) % 2]
        P.copy(yb[:, (c % 8) * 64:(c % 8 + 1) * 64], py[0:64, 0:64], e="act"); yield
        release(ids)
        if c % 8 == 7 or c == NCH - 1:
            c0 = (c // 8) * 8
            P.dma("sp", y_d[h, :, c0 * L:(c + 1) * L], yb[:, 0:(c - c0 + 1) * L]); yield

    def run_il(gens):
        gens = list(gens)
        while gens:
            for g_ in list(gens):
                try:
                    next(g_)
                except StopIteration:
                    gens.remove(g_)

    for h in range(NH):
        load(h, 0)
        load(h, 1)
    run_il([prep(h, c_) for c_ in range(2) for h in range(NH)])
    for c in range(NCH):
        gl = [rec(h, c) for h in range(NH)]
        if c + 2 < NCH:
            for h in range(NH):
                load(h, c + 2)
            gl += [prep(h, c + 2) for h in range(NH)]
        run_il(gl)
    assert len(free) == 8
    P.phase_end()


def emit_attn(P, S, Wd, mix_src):
    T = T_SEQ
    NB = T // 128
    NG = T // 512
    P.phase_begin()
    q_d, k_d, v_d = S["qT"], S["kT"], S["vtm"]
    lamv = P.sb([128, 256]); li = P.sb([128, 1]); gsc = P.sb([128, 128]); trif = P.sb([128, 128]); sel = P.sb([64, 65])
    ident = P.sb([128, 128])
    P.dma("sp", lamv[:], Wd["lamv"][0:1, :].to_broadcast([128, 256]))
    P.dma("sp", li[:], Wd["lam_init"][0:1, :].to_broadcast([128, 1]))
    P.dma("sp", gsc[:], Wd["subln_g"][0:1, :].to_broadcast([128, 128]))
    P.dma("sp", trif[:], Wd["tri"])
    P.dma("sp", sel[:], Wd["sel"])
    P.dma("sp", ident[:], Wd["ident"])
    tri = P.sb([128, 128], BF16)
    P.copy(tri[:], trif[:])
    sc = P.sb([128, 8]); junk = P.sb([128, 128])
    P.stt(junk[:, 0:64], lamv[:, 0:64], 1.0, lamv[:, 64:128], ALU.mult, ALU.mult, accum_out=sc[:, 0:1])
    P.stt(junk[:, 0:64], lamv[:, 128:192], 1.0, lamv[:, 192:256], ALU.mult, ALU.mult, accum_out=sc[:, 1:2])
    P.act(sc[:, 2:4], sc[:, 0:2], AF.Exp)
    P.tt(sc[:, 4:5], sc[:, 2:3], sc[:, 3:4], ALU.subtract)
    P.tt(sc[:, 4:5], sc[:, 4:5], li[:], ALU.add)
    P.ts(sc[:, 5:6], sc[:, 4:5], -1.0, ALU.mult)
    P.ts(sc[:, 6:7], li[:], -1.0, ALU.mult, 1.0, ALU.add)
    P.ts(gsc[:], gsc[:], sc[:, 6:7], ALU.mult)
    nlam = sc[:, 5:6]
    Ka = [P.sb([65, T], BF16) for _ in range(2)]
    Va = P.sb([128, NB, 129], BF16)
    for c in range(2):
        P.memset(Ka[c][64:65, :], 1.0, e="pool")
    P.memset(Va[:, :, 128:129], 1.0, e="pool")
    kst = P.sb([65, 8])
    P.memset(kst[64:65, :], 0.0)
    stg = [P.sb([128, 2048], F32) for _ in range(2)]
    sqb = [P.sb([64, 1024], F32) for _ in range(2)]
    pss = [P.ps([128, 512]) for _ in range(3)]
    pso = [P.ps([128, 512]) for _ in range(4)]
    pmisc = P.ps([128, 512])
    for i in range(NG):
        st = stg[i % 2]
        kf = st[0:64, 0:1024].rearrange("p (c t) -> p c t", c=2)
        P.dma("sp", kf, k_d[:, :, i * 512:(i + 1) * 512])
        s2 = sqb[i % 2]
        P.tt(s2[:], st[0:64, 0:1024], st[0:64, 0:1024], ALU.mult, e="pool")
        for c in range(2):
            P.copy(Ka[c][0:64, i * 512:(i + 1) * 512], st[0:64, c * 512:(c + 1) * 512], e="act")
            P.mm(pmisc[0:65, 0:512], sel[:], s2[:, c * 512:(c + 1) * 512])
            P.reduce(kst[64:65, 2 + c:3 + c], pmisc[64:65, 0:512], ALU.max)
            P.tt(kst[64:65, c:c + 1], kst[64:65, c:c + 1], kst[64:65, 2 + c:3 + c], ALU.max)
    VP = 16
    for i in range(NB // VP):
        st = stg[i % 2]
        P.dma("sp", st[:, 0:VP * 128].rearrange("p (n d) -> p n d", d=128), v_d[:, i * VP:(i + 1) * VP, :])
        P.copy(Va[:, i * VP:(i + 1) * VP, 0:128], st[:, 0:VP * 128].rearrange("p (n d) -> p n d", d=128), e=("dve" if i % 2 else "pool"))
    P.act(kst[64:65, 4:6], kst[64:65, 0:2], AF.Sqrt)
    P.ts(kst[64:65, 6:8], kst[64:65, 4:6], -1.0, ALU.mult)

    qf = [P.sb([64, 2, 512], F32) for _ in range(2)]
    qsq = P.sb([64, 2, 512], F32)
    nq = P.sb([65, 512], F32)
    Qa = [[P.sb([65, 512], BF16) for _ in range(2)] for _ in range(2)]
    PT = [P.sb([128, 512], BF16) for _ in range(3)]
    res = [P.sb([128, 128], F32) for _ in range(4)]
    ob = [P.sb([128, 128], F32) for _ in range(2)]
    obT = [P.sb([128, 128], BF16) for _ in range(2)]
    st2 = P.sb([128, 8], F32)
    P.dma("sp", qf[0][:], q_d[:, :, 0:512])
    cnt = 0
    for qg in range(NG):
        Qf = qf[qg % 2]
        if qg + 1 < NG:
            P.dma("sp", qf[(qg + 1) % 2][:], q_d[:, :, (qg + 1) * 512:(qg + 2) * 512])
        P.tt(qsq[:], Qf[:], Qf[:], ALU.mult, e="pool")
        Q = Qa[qg % 2]
        for c in range(2):
            P.copy(Q[c][0:64, :], Qf[:, c, :], e="pool")
            P.mm(pmisc[0:65, 0:512], sel[:], qsq[:, c, :])
            P.act(nq[64:65, :], pmisc[64:65, 0:512], AF.Sqrt)
            P.ts(Q[c][64:65, :], nq[64:65, :], kst[64:65, 6 + c:7 + c], ALU.mult)
        for c in range(2):
            nkb = (qg + 1) * 4
            for kb in range(nkb):
                j = kb - qg * 4
                q0 = max(j, 0) * 128
                ps_ = pss[cnt % 3]; pt = PT[cnt % 3]; cnt += 1
                P.mm(ps_[:, q0:512], Ka[c][:, kb * 128:(kb + 1) * 128], Q[c][:, q0:512])
                P.act(pt[:, q0:512], ps_[:, q0:512], AF.Exp)
                if j >= 0:
                    P.tt(pt[:, q0:q0 + 128], pt[:, q0:q0 + 128], tri[:], ALU.mult, e="dve")
                for qb in range(max(j, 0), 4):
                    P.mm(pso[qb][:, 0:129], pt[:, qb * 128:(qb + 1) * 128], Va[:, kb, :],
                         start=(kb == 0), stop=(kb == qg * 4 + qb))
            for qb in range(4):
                oo = pso[qb]
                if c == 0:
                    P.recip(st2[:, qb:qb + 1], oo[:, 128:129])
                    P.ts(res[qb][:], oo[:, 0:128], st2[:, qb:qb + 1], ALU.mult)
                else:
                    a_ = ob[qb % 2]
                    P.recip(st2[:, 4:5], oo[:, 128:129])
                    P.tt(st2[:, 5:6], st2[:, 4:5], nlam, ALU.mult)
                    P.stt(a_[:], oo[:, 0:128], st2[:, 5:6], res[qb][:], ALU.mult, ALU.add)
                    P.stt(junk[:], a_[:], 1.0, a_[:], ALU.mult, ALU.mult, accum_out=st2[:, 6:7])
                    P.act(st2[:, 7:8], st2[:, 6:7], AF.Sqrt, bias=P.const(1e-5)[:, 0:1], scale=1.0 / 128.0)
                    P.recip(st2[:, 7:8], st2[:, 7:8])
                    P.stt(a_[:], a_[:], st2[:, 7:8], gsc[:], ALU.mult, ALU.mult)
                    P.transpose(pmisc[:, 0:128], a_[:], ident[:])
                    aT = obT[qb % 2]
                    P.copy(aT[:], pmisc[:, 0:128])
                    blk = qg * 4 + qb
                    P.dma("sp", mix_src[blk // 16][128:256, (blk % 16) * 128:(blk % 16 + 1) * 128], aT[:])
    P.phase_end()


def emit_post(P, S, Wd, mix_src, TT=512):
    T = T_SEQ
    P.phase_begin()
    pp = P.sb([128, 2]); hb = P.sb([128, 128])
    P.dma("sp", pp[:], Wd["ppx"])
    P.dma("sp", hb[:], Wd["hblk64"])
    yf = [P.sb([128, TT], F32) for _ in range(2)]
    bf = [P.sb([128, TT], F32) for _ in range(2)]
    gf = [P.sb([128, TT], F32) for _ in range(2)]
    d_ = [P.sb([128, TT], F32) for _ in range(2)]
    q_ = [P.sb([128, TT], F32) for _ in range(2)]
    r_ = [P.sb([128, TT], F32) for _ in range(2)]
    mo = [P.sb([128, TT], BF16) for _ in range(2)]
    pm = [P.ps([128, 512]) for _ in range(2)]
    pv = [P.ps([128, 512]) for _ in range(2)]
    y2 = S["y"].rearrange("h k t -> (h k) t")

    def load(i):
        sl = slice(i * TT, (i + 1) * TT)
        P.dma("sp", yf[i % 2][:], y2[:, sl])
        P.dma("sp", bf[i % 2][:], S["bonus"][:, sl])
        P.dma("sp", gf[i % 2][:], S["g"][:, sl])

    load(0)
    for i in range(T // TT):
        if i + 1 < T // TT:
            load(i + 1)
        Y, B, G = yf[i % 2], bf[i % 2], gf[i % 2]
        d = d_[i % 2]; q = q_[i % 2]; r = r_[i % 2]
        P.mm(pm[i % 2][:, 0:TT], hb[:], Y[:])
        P.tt(d[:], Y[:], pm[i % 2][:, 0:TT], ALU.subtract)
        P.tt(q[:], d[:], d[:], ALU.mult, e="pool")
        P.mm(pv[i % 2][:, 0:TT], hb[:], q[:])
        P.act(r[:], pv[i % 2][:, 0:TT], AF.Sqrt, bias=P.const(64e-5)[:, 0:1])
        P.recip(r[:], r[:])
        P.tt(d[:], d[:], r[:], ALU.mult)
        P.act(d[:], d[:], AF.Identity, bias=pp[:, 1:2], scale=pp[:, 0:1])
        P.tt(d[:], d[:], B[:], ALU.add, e="pool")
        P.tt(mo[i % 2][:], d[:], G[:], ALU.mult)
        P.dma("sp", mix_src[(i * TT) // 2048][0:128, (i * TT) % 2048:(i * TT) % 2048 + TT], mo[i % 2][:])
    P.phase_end()


def emit_outproj(P, x_ap, mix_all, Wd, x1_ap, halo_src, qoff, TT=256):
    P.phase_begin()
    onesm = P.sb([128, 128], F32)
    P.memset(onesm[:], 1.0 / 1024.0)
    lg = P.sb([128, 8]); lb = P.sb([128, 8])
    P.dma("sp", lg[:], Wd["ln_g"]); P.dma("sp", lb[:], Wd["ln_b"])
    wout = [P.sb([128, 1024], BF16) for _ in range(8)]
    stg = [P.sb([128, 1024], F32) for _ in range(2)]
    load_cast(P, [w[:, :] for w in wout], [Wd["w_out"][:, c, :] for c in range(8)], stg)
    xf = [P.sb([128, 8, TT], F32) for _ in range(2)]
    MIX = [P.sb([128, 8, TT], BF16) for _ in range(2)]
    sq = [P.sb([128, TT], F32) for _ in range(2)]
    stat = [P.sb([128, TT], F32) for _ in range(3)]
    po = [P.ps([128, 512]) for _ in range(2)]
    pl = [P.ps([128, 512]) for _ in range(2)]
    NTL = NTOK // TT

    selq = P.sb([128, 4])
    P.dma("sp", selq[:], qoff)
    CAND = [[P.sb([128, 8, TT], BF16) for _ in range(4)] for _ in range(2)]

    def load(i):
        P.dma("sp", xf[i % 2][:], x_ap[:, :, i * TT:(i + 1) * TT])
        for qq in range(4):
            for hf in range(2):
                g_ = qq * NTOK + i * TT
                P.dma("sp", CAND[i % 2][qq][:, hf * 4:(hf + 1) * 4, :],
                      mix_all[g_ // 2048][:, hf, :, g_ % 2048: g_ % 2048 + TT].rearrange("r p t -> p r t"))
        M_ = MIX[i % 2]
        P.ts(M_[:], CAND[i % 2][0][:], selq[:, 0:1], ALU.mult)
        for qq in range(1, 4):
            P.stt(M_[:], CAND[i % 2][qq][:], selq[:, qq:qq + 1], M_[:], ALU.mult, ALU.add)

    load(0)
    for i in range(NTL):
        X, M = xf[i % 2], MIX[i % 2]
        if i + 1 < NTL:
            load(i + 1)
        for oc in range(8):
            p_ = po[oc % 2]
            for c in range(8):
                P.mm(p_[:, 0:TT], wout[c][:, oc * 128:(oc + 1) * 128], M[:, c, :], start=(c == 0), stop=(c == 7))
            P.stt(X[:, oc, :], X[:, oc, :], ALPHA, p_[:, 0:TT], ALU.mult, ALU.add)
        ln_inplace(P, X, TT, onesm, lg, lb, pl, sq, stat)
        P.dma("sp", x1_ap[:, :, i * TT:(i + 1) * TT], X[:])
        if i == NTL - 1:
            P.dma("sp", halo_src.rearrange("p (c t) -> p c t", t=2), X[:, :, TT - 2:TT])
    P.phase_end()


def emit_gmlp(P, x_ap, Wd, x1_ap, halo_src):
    TT = 128
    P.phase_begin()
    onesm = P.sb([128, 128], F32)
    P.memset(onesm[:], 1.0 / 1024.0)
    bu = P.sb([128, 8]); g = P.sb([128, 8]); b = P.sb([128, 8])
    bv = P.sb([128, 1024]); lg = P.sb([128, 1024]); lb = P.sb([128, 1024]); bs = P.sb([128, 1024])
    for t_, n in ((bu, "b_u"), (g, "ln_g"), (b, "ln_b")):
        P.dma("sp", t_[:], Wd[n])
    for t_, n in ((bv, "b_v"), (lg, "cln_g"), (lb, "cln_b"), (bs, "b_s")):
        P.dma("sp", t_[:], Wd[n][0:1, :].to_broadcast([128, 1024]))
    mask = P.sb([128, 128], F32)
    P.dma("sp", mask[:], Wd["tri"])
    wsf = P.sb([128, 8, 128], F32)
    P.dma("sp", wsf[:], Wd["wsT"])
    wsb = P.sb([128, 8, 128], BF16)
    for gi in range(8):
        P.tt(wsb[:, gi, :], wsf[:, gi, :], mask[:], ALU.mult)
    win = [P.sb([128, 2048], BF16) for _ in range(8)]
    wout = [P.sb([128, 1024], BF16) for _ in range(8)]
    stg = [P.sb([128, 2048], F32) for _ in range(2)]
    load_cast(P, [w[:, :] for w in win] + [w[:, :] for w in wout],
              [Wd["w_in"][:, kc, :] for kc in range(8)] + [Wd["w_out"][:, kc, :] for kc in range(8)], stg)
    xf = [P.sb([128, 8, TT], F32) for _ in range(2)]
    xb = [P.sb([128, 8, TT], BF16) for _ in range(2)]
    U = P.sb([128, 8, TT], F32)
    V1 = P.sb([128, 1024], F32)
    junk = P.sb([128, 1024], F32)
    VN = P.sb([128, 1024], BF16)
    GU = P.sb([128, 8, TT], BF16)
    mx = [P.sb([128, TT], F32) for _ in range(2)]
    st = P.sb([128, 4], F32)
    sq = [P.sb([128, TT], F32) for _ in range(2)]
    stat = [P.sb([128, TT], F32) for _ in range(3)]
    pu = [P.ps([128, 512]) for _ in range(2)]
    pvv = [P.ps([128, 512]) for _ in range(2)]
    pm = [P.ps([128, 512]) for _ in range(2)]
    pl = [P.ps([128, 512]) for _ in range(2)]
    NTL = NTOK // TT

    def load_tile(i):
        P.dma("sp", xf[i % 2][:], x_ap[:, :, i * TT:(i + 1) * TT])
        P.copy(xb[i % 2][:], xf[i % 2][:], e="pool")

    load_tile(0)
    for i in range(NTL):
        X, XB = xf[i % 2], xb[i % 2]
        if i + 1 < NTL:
            load_tile(i + 1)
        for oc in range(8):
            pp_ = pu[oc % 2]
            for kc in range(8):
                P.mm(pp_[:, 0:TT], win[kc][:, oc * 128:(oc + 1) * 128], XB[:, kc, :], start=(kc == 0), stop=(kc == 7))
            P.act(U[:, oc, :], pp_[:, 0:TT], AF.Gelu, bias=bu[:, oc:oc + 1])
        for hf in range(2):
            pp_ = pvv[hf]
            for kc in range(8):
                P.mm(pp_[:, :], XB[:, kc, :], win[kc][:, 1024 + hf * 512: 1024 + (hf + 1) * 512], start=(kc == 0), stop=(kc == 7))
            P.tt(V1[:, hf * 512:(hf + 1) * 512], pp_[:, :], bv[:, hf * 512:(hf + 1) * 512], ALU.add)
        P.act(V1[:], V1[:], AF.Gelu)
        P.reduce(st[:, 0:1], V1[:], ALU.add)
        P.ts(st[:, 1:2], st[:, 0:1], -1.0 / 1024.0, ALU.mult)
        P.ts(V1[:], V1[:], st[:, 1:2], ALU.add)
        P.stt(junk[:], V1[:], 1.0, V1[:], ALU.mult, ALU.mult, accum_out=st[:, 2:3])
        P.act(st[:, 3:4], st[:, 2:3], AF.Sqrt, bias=P.const(1e-5)[:, 0:1], scale=1.0 / 1024.0)
        P.recip(st[:, 3:4], st[:, 3:4])
        P.stt(V1[:], V1[:], st[:, 3:4], lg[:], ALU.mult, ALU.mult)
        P.tt(VN[:], V1[:], lb[:], ALU.add, e="pool")
        for gi in range(8):
            pp_ = pm[gi % 2]
            P.mm(pp_[:, 0:TT], VN[:, gi * 128:(gi + 1) * 128], wsb[:, gi, :])
            m = mx[gi % 2]
            P.tt(m[:], pp_[:, 0:TT], bs[:, gi * 128:(gi + 1) * 128], ALU.add)
            P.tt(GU[:, gi, :], m[:], U[:, gi, :], ALU.mult, e="pool")
        for oc in range(8):
            pp_ = pu[oc % 2]
            for c in range(8):
                P.mm(pp_[:, 0:TT], wout[c][:, oc * 128:(oc + 1) * 128], GU[:, c, :], start=(c == 0), stop=(c == 7))
            P.stt(X[:, oc, :], X[:, oc, :], ALPHA, pp_[:, 0:TT], ALU.mult, ALU.add)
        ln_inplace(P, X, TT, onesm, g, b, pl, sq, stat)
        P.dma("sp", x1_ap[:, :, i * TT:(i + 1) * TT], X[:])
        if i == NTL - 1:
            P.dma("sp", halo_src.rearrange("p (c t) -> p c t", t=2), X[:, :, TT - 2:TT])
    P.phase_end()


def emit_ffn(P, x1_ap, halo_all, selp_ap, Wd, out_ap, xg_ap=None, TT=256):
    P.phase_begin()
    onesm = P.sb([128, 128], F32)
    P.memset(onesm[:], 1.0 / 1024.0)
    cw = P.sb([128, 3, NFC]); cb = P.sb([128, NFC]); g = P.sb([128, 8]); b = P.sb([128, 8]); selp = P.sb([128, 4])
    for t_, n in ((cw, "conv_w"), (cb, "conv_b"), (g, "ln_g"), (b, "ln_b")):
        P.dma("sp", t_[:], Wd[n])
    P.dma("sp", selp[:], selp_ap)
    wup = [P.sb([128, 2 * DFF], BF16) for _ in range(8)]
    wdn = [P.sb([128, D], BF16) for _ in range(NFC)]
    stg = [P.sb([128, 1408], F32) for _ in range(2)]
    dst, src = [], []
    for kc in range(8):
        for j in range(4):
            dst.append(wup[kc][:, j * 1408:(j + 1) * 1408]); src.append(Wd["w_up"][:, kc, j * 1408:(j + 1) * 1408])
    for c in range(NFC):
        dst.append(wdn[c][:, :]); src.append(Wd["w_down"][:, c, :])
    load_cast(P, dst, src, stg)
    xf = [P.sb([128, 8, TT], F32) for _ in range(2)]
    xb = [P.sb([128, 8, TT], BF16) for _ in range(2)]
    H = P.sb([128, NFC, TT], BF16)
    G = [P.sb([128, TT + 2], F32) for _ in range(2)]
    Tt = [P.sb([128, TT], F32) for _ in range(2)]
    sq = [P.sb([128, TT], F32) for _ in range(2)]
    stat = [P.sb([128, TT], F32) for _ in range(3)]
    carry = P.sb([128, NFC, 2], F32)
    HA = P.sb([128, 4, 16], F32); xh = P.sb([128, 16], F32); xhb = P.sb([128, 8, 2], BF16)
    pg = [P.ps([128, 512]) for _ in range(2)]
    pv = [P.ps([128, 512]) for _ in range(2)]
    pd = [P.ps([128, 512]) for _ in range(2)]
    pl = [P.ps([128, 512]) for _ in range(2)]
    NTL = NTOK // TT

    def load_tile(i):
        P.dma("sp", xf[i % 2][:], x1_ap[:, :, i * TT:(i + 1) * TT])
        P.copy(xb[i % 2][:], xf[i % 2][:], e="pool")

    load_tile(0)
    P.dma("sp", HA[:], halo_all.rearrange("r p f -> p r f"))
    P.ts(xh[:], HA[:, 0, :], selp[:, 0:1], ALU.mult)
    for r in range(1, 4):
        P.stt(xh[:], HA[:, r, :], selp[:, r:r + 1], xh[:], ALU.mult, ALU.add)
    P.copy(xhb[:].rearrange("p c t -> p (c t)"), xh[:])
    for c in range(NFC):
        pp_ = pg[c % 2]
        for kc in range(8):
            P.mm(pp_[:, 0:2], wup[kc][:, c * 128:(c + 1) * 128], xhb[:, kc, :], start=(kc == 0), stop=(kc == 7))
        P.copy(carry[:, c, :], pp_[:, 0:2])
    for i in range(NTL):
        X, XB = xf[i % 2], xb[i % 2]
        if i + 1 < NTL:
            load_tile(i + 1)
        for c in range(NFC):
            pgc, pvc = pg[c % 2], pv[c % 2]
            for kc in range(8):
                P.mm(pgc[:, 0:TT], wup[kc][:, c * 128:(c + 1) * 128], XB[:, kc, :], start=(kc == 0), stop=(kc == 7))
            for kc in range(8):
                P.mm(pvc[:, 0:TT], wup[kc][:, DFF + c * 128: DFF + (c + 1) * 128], XB[:, kc, :], start=(kc == 0), stop=(kc == 7))
            Gc, Tc = G[c % 2], Tt[c % 2]
            P.copy(Gc[:, 2:TT + 2], pgc[:, 0:TT], e="act")
            P.copy(Gc[:, 0:2], carry[:, c, :], e="pool")
            P.ts(Tc[:], Gc[:, 0:TT], cw[:, 0, c:c + 1], ALU.mult, cb[:, c:c + 1], ALU.add)
            P.stt(Tc[:], Gc[:, 1:TT + 1], cw[:, 1, c:c + 1], Tc[:], ALU.mult, ALU.add)
            P.stt(Tc[:], Gc[:, 2:TT + 2], cw[:, 2, c:c + 1], Tc[:], ALU.mult, ALU.add)
            P.copy(carry[:, c, :], Gc[:, TT:TT + 2], e="pool")
            P.act(Tc[:], Tc[:], AF.Silu)
            P.tt(H[:, c, :], Tc[:], pvc[:, 0:TT], ALU.mult)
        for oc in range(8):
            pp_ = pd[oc % 2]
            for c in range(NFC):
                P.mm(pp_[:, 0:TT], wdn[c][:, oc * 128:(oc + 1) * 128], H[:, c, :], start=(c == 0), stop=(c == NFC - 1))
            P.stt(X[:, oc, :], X[:, oc, :], ALPHA, pp_[:, 0:TT], ALU.mult, ALU.add)
        ln_inplace(P, X, TT, onesm, g, b, pl, sq, stat)
        P.dma("sp", out_ap[:, :, i * TT:(i + 1) * TT], X[:])
        if xg_ap is not None:
            P.copy(XB[:], X[:], e="pool")
            P.dma("sp", xg_ap[(i * TT) // 512][:, :, (i * TT) % 512:(i * TT) % 512 + TT], XB[:])
    P.phase_end()


def build_fused():
    P = Prog()
    nc = P.nc
    ext_shapes = {}

    def ext(name, shape):
        ext_shapes[name] = tuple(shape)
        return nc.dram_tensor(name, list(shape), F32, kind="ExternalInput")[:]

    x0 = ext("x0", [128, 8, NTOK])
    selp = ext("selp", [128, 4])
    G_ = {n: ext(n, s) for n, s in (("ident", [128, 128]), ("hblk", [128, 128]), ("hblk64", [128, 128]), ("rotm", [128, 128]),
                                    ("tri", [128, 128]), ("sel", [64, 65]), ("TRI2", [64, 128]), ("TGT", [64, 64]),
                                    ("MASK2", [64, 128]), ("MASKT", [64, 64]), ("IDENT", [64, 64]),
                                    ("cosq", [128, T_SEQ]), ("sinq", [128, T_SEQ]), ("cosk", [128, T_SEQ]), ("sink", [128, T_SEQ]))}
    Wa, Wc, Wf = [], [], []
    for k in range(2):
        d = dict(G_)
        for n, s in (("w_in", [128, 8, 1024]), ("pp", [128, 10]), ("w2p", [128, 128]), ("a2p", [128, 128]), ("g2", [128, 128]),
                     ("lamv", [1, 256]), ("lam_init", [1, 1]), ("subln_g", [1, 128]), ("ppx", [128, 2]),
                     ("w_out", [128, 8, 1024]), ("ln_g", [128, 8]), ("ln_b", [128, 8])):
            d[n] = ext(f"a{k}_{n}", s)
        Wa.append(d)
        d = dict(G_)
        for n, s in (("w_in", [128, 8, 2048]), ("w_out", [128, 8, 1024]), ("wsT", [128, 8, 128]), ("b_u", [128, 8]),
                     ("b_v", [1, 1024]), ("cln_g", [1, 1024]), ("cln_b", [1, 1024]), ("b_s", [1, 1024]),
                     ("ln_g", [128, 8]), ("ln_b", [128, 8])):
            d[n] = ext(f"c{k}_{n}", s)
        Wc.append(d)
    for i in range(4):
        d = {}
        for n, s in (("w_up", [128, 8, 2 * DFF]), ("w_down", [128, NFC, D]), ("conv_w", [128, 3, NFC]), ("conv_b", [128, NFC]),
                     ("ln_g", [128, 8]), ("ln_b", [128, 8])):
            d[n] = ext(f"f{i}_{n}", s)
        Wf.append(d)
    out_d = nc.dram_tensor("x_out", [128, 8, NTOK], F32, kind="ExternalOutput")[:]

    T = T_SEQ
    S = {"fmp": nc.dram_tensor("s_fmp", [2, T // 64, 64, 256], F32), "tmp": nc.dram_tensor("s_tmp", [2, T // 64, 64, 256], F32),
         "qT": nc.dram_tensor("s_qT", [64, 2, T], F32), "kT": nc.dram_tensor("s_kT", [64, 2, T], F32),
         "vtm": nc.dram_tensor("s_vtm", [128, T // 128, 128], F32), "g": nc.dram_tensor("s_g", [128, T], F32),
         "bonus": nc.dram_tensor("s_bonus", [128, T], F32), "y": nc.dram_tensor("s_y", [2, 64, T], F32)}
    S = {k: v[:] for k, v in S.items()}
    xs = [nc.dram_tensor(f"s_x{k}", [128, 8, NTOK], F32)[:] for k in range(2)]
    x1s = nc.dram_tensor("s_xone", [128, 8, NTOK], F32)[:]
    xgs = [[nc.dram_tensor(f"s_xgs{k}_{j}", [1024, 512], BF16) for j in range(8)] for k in range(2)]
    xga = [[nc.dram_tensor(f"s_xga{k}_{j}", [4096, 512], BF16) for j in range(8)] for k in range(2)]
    mxs = [[nc.dram_tensor(f"s_mxs{k}_{j}", [256, 2048], BF16) for j in range(8)] for k in range(2)]
    mxa = [[nc.dram_tensor(f"s_mxa{k}_{j}", [1024, 2048], BF16) for j in range(8)] for k in range(2)]
    hls = [nc.dram_tensor(f"s_hls{k}", [128, 16], F32) for k in range(4)]
    hla = [nc.dram_tensor(f"s_hla{k}", [512, 16], F32) for k in range(4)]
    qoff = ext("selq", [128, 4])

    def xg_view(hs):
        return [h[:].rearrange("(p c) t -> p c t", c=8) for h in hs]

    emit_cast_x(P, x0, xg_view(xgs[0]))
    x_cur = x0
    for i in range(4):
        k = i // 2
        if i % 2 == 0:
            for j in range(8):
                P.allgather(xgs[k][j], xga[k][j], GROUPS)
            emit_abin_h(P, [h[:].rearrange("(r p c) t -> r p c t", r=4, c=8) for h in xga[k]], Wa[k], S)
            emit_scan(P, S, G_)
            emit_attn(P, S, Wa[k], [h[:] for h in mxs[k]])
            emit_post(P, S, Wa[k], [h[:] for h in mxs[k]])
            for j in range(8):
                P.allgather(mxs[k][j], mxa[k][j], GROUPS)
            emit_outproj(P, x_cur, [h[:].rearrange("(r f p) t -> r f p t", r=4, f=2) for h in mxa[k]], Wa[k], x1s, hls[i][:], qoff)
        else:
            emit_gmlp(P, x_cur, Wc[k], x1s, hls[i][:])
        P.allgather(hls[i], hla[i], GROUPS)
        out = out_d if i == 3 else xs[i % 2]
        emit_ffn(P, x1s, hla[i][:].rearrange("(r p) f -> r p f", r=4), selp, Wf[i], out,
                 xg_ap=(xg_view(xgs[1]) if i == 1 else None))
        x_cur = out
    return P.finish([]), ext_shapes


def _fm(a):
    T, C = a.shape
    return np.ascontiguousarray(a.T.reshape(C // 128, 128, T).transpose(1, 0, 2))


def _col(v, n):
    return np.ascontiguousarray(np.asarray(v, np.float32).reshape(n, 128).T)


def _wl(w):
    K, N = w.shape
    return np.ascontiguousarray(w.reshape(K // 128, 128, N).transpose(1, 0, 2))


def _rope_tabs(pos):
    inv = (500000.0 ** (-np.arange(0, 16, 2, dtype=np.float32) / 16)).astype(np.float32)
    ang = pos.astype(np.float32)[:, None] * inv[None, :]
    c, s = np.cos(ang).astype(np.float32), np.sin(ang).astype(np.float32)
    C = np.ones((128, len(pos)), np.float32)
    S = np.zeros((128, len(pos)), np.float32)
    for comp in range(2):
        for half in range(2):
            lo = comp * 64 + half * 8
            C[lo:lo + 8] = c.T
            S[lo:lo + 8] = s.T
    return C * np.float32(0.125), S * np.float32(0.125), C, S


def _rotm():
    R = np.zeros((128, 128), np.float32)
    for comp in range(2):
        for p in range(8):
            R[comp * 64 + p + 8, comp * 64 + p] = -1.0
            R[comp * 64 + p, comp * 64 + p + 8] = 1.0
    return R


def _consts():
    f32 = np.float32
    L = 64
    i = np.arange(L)[:, None]; t = np.arange(L)[None, :]
    incl = (i <= t).astype(f32); strict = (i < t).astype(f32); gt = (i > t).astype(f32)
    hb = np.zeros((128, 128), f32); hb[:64, :64] = 1; hb[64:, 64:] = 1
    kk_ = np.arange(128)[:, None]; qq_ = np.arange(128)[None, :]
    sel = np.zeros((64, 65), f32); sel[:, 64] = 1
    cq, sq, ck, sk = _rope_tabs(np.arange(T_SEQ))
    return {"ident": np.eye(128, dtype=f32), "hblk": hb, "hblk64": hb / f32(64.0), "rotm": _rotm(),
            "tri": (kk_ <= qq_).astype(f32), "sel": sel,
            "TRI2": np.concatenate([incl, strict], 1), "TGT": gt, "MASK2": np.concatenate([strict, incl], 1),
            "MASKT": np.ascontiguousarray(strict.T), "IDENT": np.eye(L, dtype=f32),
            "cosq": cq, "sinq": sq, "ck_": None, "cosk": ck, "sink": sk}


_NC = []


def kernel(**inp):
    f32 = np.float32
    inp = {k: np.asarray(v, f32) for k, v in inp.items()}
    if not _NC:
        _NC.append(build_fused())
    nc, shapes = _NC[0]
    cst = _consts()
    cst.pop("ck_")
    shared = dict(cst)
    for k in range(2):
        i = 2 * k
        lam_init = 0.8 - 0.6 * math.exp(-0.3 * i)
        shared[f"a{k}_lamv"] = np.concatenate([inp["ab_lam_q1"][k], inp["ab_lam_k1"][k], inp["ab_lam_q2"][k], inp["ab_lam_k2"][k]]).reshape(1, 256).astype(f32)
        shared[f"a{k}_lam_init"] = np.full((1, 1), lam_init, f32)
        shared[f"a{k}_subln_g"] = np.ascontiguousarray(inp["ab_subln_g"][k].reshape(1, 128))
        shared[f"a{k}_w_out"] = _wl(inp["ab_w_out"][k])
        shared[f"a{k}_ln_g"] = _col(inp["ln1_g"][i], 8); shared[f"a{k}_ln_b"] = _col(inp["ln1_b"][i], 8)
        i = 2 * k + 1
        b_in = inp["c_b_in"][k]
        shared[f"c{k}_w_in"] = _wl(inp["c_w_in"][k]); shared[f"c{k}_w_out"] = _wl(inp["c_w_out"][k])
        shared[f"c{k}_wsT"] = np.ascontiguousarray(inp["c_w_s"][k].transpose(2, 0, 1))
        shared[f"c{k}_b_u"] = _col(b_in[:1024], 8); shared[f"c{k}_b_v"] = np.ascontiguousarray(b_in[1024:].reshape(1, 1024))
        shared[f"c{k}_cln_g"] = np.ascontiguousarray(inp["c_ln_g"][k].reshape(1, 1024))
        shared[f"c{k}_cln_b"] = np.ascontiguousarray(inp["c_ln_b"][k].reshape(1, 1024))
        shared[f"c{k}_b_s"] = np.ascontiguousarray(inp["c_b_s"][k].reshape(1, 1024))
        shared[f"c{k}_ln_g"] = _col(inp["ln1_g"][i], 8); shared[f"c{k}_ln_b"] = _col(inp["ln1_b"][i], 8)
    for i in range(4):
        shared[f"f{i}_w_up"] = _wl(inp["ffn_w_up"][i]); shared[f"f{i}_w_down"] = _wl(inp["ffn_w_down"][i])
        shared[f"f{i}_conv_w"] = np.ascontiguousarray(inp["ffn_conv_w"][i].reshape(3, 22, 128).transpose(2, 0, 1))
        shared[f"f{i}_conv_b"] = _col(inp["ffn_conv_b"][i], 22)
        shared[f"f{i}_ln_g"] = _col(inp["ln2_g"][i], 8); shared[f"f{i}_ln_b"] = _col(inp["ln2_b"][i], 8)
    in_maps = []
    x = inp["x"]
    for c in range(8):
        b, q = divmod(c, 4)
        hp = q
        m = dict(shared)
        m["x0"] = _fm(x[b, q * NTOK:(q + 1) * NTOK])
        sp = np.zeros((128, 4), f32)
        if q > 0:
            sp[:, q - 1] = 1.0
        m["selp"] = sp
        sq_ = np.zeros((128, 4), f32); sq_[:, q] = 1.0
        m["selq"] = sq_
        hs = slice(hp * 128, (hp + 1) * 128)
        cols = np.concatenate([np.arange(hp * 128, hp * 128 + 128), 512 + np.arange(hp * 128, hp * 128 + 128),
                               1024 + np.arange(hp * 128, hp * 128 + 128), np.arange(1536, 1792),
                               1792 + np.arange(hp * 128, hp * 128 + 128), 2304 + np.arange(hp * 128, hp * 128 + 128),
                               2816 + np.arange(hp * 128, hp * 128 + 128)])
        for k in range(2):
            m[f"a{k}_w_in"] = _wl(np.ascontiguousarray(inp["ab_w_in"][k][:, cols]))
            mu = inp["ab_shift_mu"][k][cols[:640]]
            m[f"a{k}_pp"] = np.ascontiguousarray(np.concatenate(
                [_col(mu, 5), _col(inp["ab_w0"][k][hs], 1), _col(inp["ab_a0"][k][hs], 1), _col(inp["ab_k_k"][k][hs], 1),
                 _col(inp["ab_k_a"][k][hs], 1), _col(inp["ab_r_k"][k].reshape(-1)[hs], 1)], axis=1))
            w2p = np.zeros((128, 128), f32); w2p[:64] = inp["ab_w2"][k][:, hs]
            a2p = np.zeros((128, 128), f32); a2p[64:] = inp["ab_a2"][k][:, hs]
            m[f"a{k}_w2p"] = w2p; m[f"a{k}_a2p"] = a2p
            m[f"a{k}_g2"] = np.ascontiguousarray(inp["ab_g2"][k][:, hs])
            m[f"a{k}_ppx"] = np.ascontiguousarray(np.concatenate([_col(inp["ab_lnx_g"][k][hs], 1), _col(inp["ab_lnx_b"][k][hs], 1)], axis=1))
        for n, s_ in shapes.items():
            assert m[n].shape == s_, (n, m[n].shape, s_)
        in_maps.append({n: m[n] for n in shapes})
    res = run_bass_kernel_spmd(nc, in_maps, core_ids=list(range(8))).results
    out = np.empty((2, T_SEQ, 1024), f32)
    for c in range(8):
        b, q = divmod(c, 4)
        out[b, q * NTOK:(q + 1) * NTOK] = res[c]["x_out"].transpose(1, 0, 2).reshape(1024, NTOK).T
    return out
```
